# Optimizing a Trainium2 kernel written in Bass

```python
import math
import jax, jax.numpy as jnp
from jax import lax
import numpy as np

D_MODEL = 1024
BATCH = 8
SEQ = 4096
DEPTH = 4

GRID_W = 64
CTX_LEN = 256
HEAD_DIM = 64
A_Q_HEADS = 8
A_KV_HEADS = 2
B_HEADS = 4
B_KEY_DIM = 128
B_VAL_DIM = 128
HGRN_CHUNK = 32
C_HEADS = 16
NA_ROWS = 8
NA_COLS = 16
D_FF = 2816
CONV_W = 3

ROPE_THETA = 10000.0
Q_BLOCK = 128
EPS = 1e-6
ATTN_SCALE = HEAD_DIM ** -0.5
HGRN_SCALE = B_KEY_DIM ** -0.5

N_EVEN = (DEPTH + 1) // 2
N_ODD = DEPTH // 2
A_Q_W = A_Q_HEADS * HEAD_DIM
A_KV_W = A_KV_HEADS * HEAD_DIM
B_K_W = B_HEADS * B_KEY_DIM
B_V_W = B_HEADS * B_VAL_DIM
EVEN_IN_W = A_Q_W + 2 * A_KV_W + 3 * B_K_W + 2 * B_V_W
EVEN_SPLITS = (A_Q_W, A_Q_W + A_KV_W, A_Q_W + 2 * A_KV_W, A_Q_W + 2 * A_KV_W + B_K_W,
               A_Q_W + 2 * A_KV_W + 2 * B_K_W, A_Q_W + 2 * A_KV_W + 3 * B_K_W,
               A_Q_W + 2 * A_KV_W + 3 * B_K_W + B_V_W)
EVEN_OUT_W = A_Q_W + B_V_W
C_W = C_HEADS * HEAD_DIM

kernel_name = "hybrid_gqa_hgrn2_natten_dit_prefix"


def rms_norm(x, w):
    xf = x.astype(jnp.float32)
    y = xf * lax.rsqrt(jnp.mean(xf * xf, axis=-1, keepdims=True) + EPS)
    return (y * w.astype(jnp.float32)).astype(x.dtype)


def axial_rope_tables(n_tokens):
    t = jnp.arange(n_tokens)
    row = (t // GRID_W).astype(jnp.float32)
    col = (t % GRID_W).astype(jnp.float32)
    n_freq = HEAD_DIM // 4
    inv_freq = ROPE_THETA ** (-jnp.arange(n_freq, dtype=jnp.float32) / n_freq)
    ang = jnp.concatenate([row[:, None] * inv_freq, col[:, None] * inv_freq], axis=-1)
    return jnp.cos(ang), jnp.sin(ang)


def apply_rope(x, cos, sin):
    x1, x2 = jnp.split(x, 2, axis=-1)
    c = cos[None, :, None, :].astype(x.dtype)
    s = sin[None, :, None, :].astype(x.dtype)
    return jnp.concatenate([x1 * c - x2 * s, x1 * s + x2 * c], axis=-1)


def gqa_axial_attention(q_lat, k_lat, v_lat, q_ctx, k_ctx, v_ctx, q_norm_w, k_norm_w, with_ctx_out):
    B_, L = q_lat.shape[:2]
    n_ctx = q_ctx.shape[1]
    group = A_Q_HEADS // A_KV_HEADS
    q_lat, q_ctx = rms_norm(q_lat, q_norm_w), rms_norm(q_ctx, q_norm_w)
    k_lat, k_ctx = rms_norm(k_lat, k_norm_w), rms_norm(k_ctx, k_norm_w)
    cos, sin = axial_rope_tables(L)
    q_lat = apply_rope(q_lat, cos, sin)
    k_lat = apply_rope(k_lat, cos, sin)
    k_all = jnp.concatenate([k_ctx, k_lat], axis=1)
    v_all = jnp.concatenate([v_ctx, v_lat], axis=1)

    def attend(q, k, v):
        s = jnp.einsum("bqhgd,bkhd->bhgqk", q, k).astype(jnp.float32) * ATTN_SCALE
        p = jax.nn.softmax(s, axis=-1).astype(v.dtype)
        return jnp.einsum("bhgqk,bkhd->bqhgd", p, v)

    n_blk = L // Q_BLOCK
    q_blocks = q_lat.reshape(B_, n_blk, Q_BLOCK, A_KV_HEADS, group, HEAD_DIM).transpose(1, 0, 2, 3, 4, 5)
    o = lax.map(lambda qb: attend(qb, k_all, v_all), q_blocks)
    o_lat = o.transpose(1, 0, 2, 3, 4, 5).reshape(B_, L, A_Q_W)
    o_ctx = None
    if with_ctx_out:
        o_ctx = attend(q_ctx.reshape(B_, n_ctx, A_KV_HEADS, group, HEAD_DIM), k_ctx, v_ctx).reshape(B_, n_ctx, A_Q_W)
    return o_lat, o_ctx


def forget_gate(f_raw, lb):
    z = f_raw.astype(jnp.float32)
    log_f = jnp.logaddexp(jnp.log(lb), jnp.log1p(-lb) + jax.nn.log_sigmoid(z))
    k = (1.0 - lb) * jax.nn.sigmoid(-z)
    return k, log_f


def hgrn2_chunk_scan(q, k, log_f, v, s0):
    B_, L, H, dk = q.shape
    dv = v.shape[-1]
    n = L // HGRN_CHUNK

    def to_chunks(a):
        return a.reshape(B_, n, HGRN_CHUNK, H, a.shape[-1]).transpose(1, 0, 3, 2, 4)

    tri = jnp.tril(jnp.ones((HGRN_CHUNK, HGRN_CHUNK), dtype=bool))[None, None, :, :, None]

    def step(S, inp):
        qc, kc, gc, vc = inp
        b = jnp.cumsum(gc, axis=2)
        diff = b[:, :, :, None, :] - b[:, :, None, :, :]
        decay = jnp.exp(jnp.where(tri, diff, -jnp.inf))
        scores = jnp.einsum("bhtd,bhtsd,bhsd->bhts", qc, decay, kc)
        o = jnp.einsum("bhts,bhsv->bhtv", scores, vc) + jnp.einsum("bhtd,bhdv->bhtv", qc * jnp.exp(b), S)
        b_last = b[:, :, -1:, :]
        S_new = jnp.exp(b_last[:, :, 0, :])[..., None] * S + jnp.einsum("bhsd,bhsv->bhdv", kc * jnp.exp(b_last - b), vc)
        return S_new, o

    S, o = lax.scan(step, s0, (to_chunks(q), to_chunks(k), to_chunks(log_f), to_chunks(v)))
    o = o.transpose(1, 0, 3, 2, 4).reshape(B_, L, H, dv).astype(v.dtype)
    return o, S


def hgrn2_bidirectional(q_lat, f_lat, i_lat, q_ctx, f_ctx, i_ctx, lb):
    B_ = q_lat.shape[0]
    s0 = jnp.zeros((B_, B_HEADS, B_KEY_DIM, B_VAL_DIM), jnp.float32)
    o_lat, o_ctx = None, None
    for d in range(2):
        flip = (lambda a: a) if d == 0 else (lambda a: a[:, ::-1])
        k_l, lf_l = forget_gate(f_lat[d], lb[d])
        k_c, lf_c = forget_gate(f_ctx[d], lb[d])
        oc, s_ctx = hgrn2_chunk_scan(flip(q_ctx), flip(k_c), flip(lf_c), flip(i_ctx), s0)
        ol, _ = hgrn2_chunk_scan(flip(q_lat), flip(k_l), flip(lf_l), flip(i_lat), s_ctx)
        o_lat = flip(ol) if o_lat is None else o_lat + flip(ol)
        o_ctx = flip(oc) if o_ctx is None else o_ctx + flip(oc)
    return o_lat, o_ctx


def even_mixer(h_lat, h_ctx, w_in, w_out, q_norm_w, k_norm_w, lb, out_norm_w, with_ctx_out):
    def project(h):
        B_, n = h.shape[:2]
        aq, ak, av, bq, bff, bfb, bi, bg = jnp.split(h @ w_in, EVEN_SPLITS, axis=-1)
        bshape = (B_, n, B_HEADS, B_KEY_DIM)
        return (aq.reshape(B_, n, A_Q_HEADS, HEAD_DIM), ak.reshape(B_, n, A_KV_HEADS, HEAD_DIM),
                av.reshape(B_, n, A_KV_HEADS, HEAD_DIM), (jax.nn.silu(bq) * HGRN_SCALE).reshape(bshape),
                (bff.reshape(bshape), bfb.reshape(bshape)), bi.reshape(B_, n, B_HEADS, B_VAL_DIM), bg)

    aq_l, ak_l, av_l, bq_l, bf_l, bi_l, bg_l = project(h_lat)
    aq_c, ak_c, av_c, bq_c, bf_c, bi_c, bg_c = project(h_ctx)
    oa_lat, oa_ctx = gqa_axial_attention(aq_l, ak_l, av_l, aq_c, ak_c, av_c, q_norm_w, k_norm_w, with_ctx_out)
    ob_lat, ob_ctx = hgrn2_bidirectional(bq_l, bf_l, bi_l, bq_c, bf_c, bi_c, lb)

    def readout(o, g):
        B_, n = o.shape[:2]
        return (rms_norm(o, out_norm_w).reshape(B_, n, B_V_W) * jax.nn.silu(g)).astype(g.dtype)

    y_lat = jnp.concatenate([oa_lat, readout(ob_lat, bg_l)], axis=-1) @ w_out
    y_ctx = None
    if with_ctx_out:
        y_ctx = jnp.concatenate([oa_ctx, readout(ob_ctx, bg_c)], axis=-1) @ w_out
    return y_lat, y_ctx


def odd_mixer(h_lat, h_ctx, w_qkv, w_out, rpb, with_ctx_out):
    B_, L, _ = h_lat.shape
    n_ctx = h_ctx.shape[1]
    rows = L // GRID_W
    wr = min(NA_ROWS, rows)
    qkv = (h_lat @ w_qkv).reshape(B_, rows, GRID_W, 3, C_HEADS, HEAD_DIM)
    q, k, v = qkv[:, :, :, 0], qkv[:, :, :, 1], qkv[:, :, :, 2]
    qkv_c = (h_ctx @ w_qkv).reshape(B_, n_ctx, 3, C_HEADS, HEAD_DIM)
    q_c, k_c, v_c = qkv_c[:, :, 0], qkv_c[:, :, 1], qkv_c[:, :, 2]

    r_start = jnp.clip(jnp.arange(rows) - wr // 2, 0, rows - wr)
    c_start = jnp.clip(jnp.arange(GRID_W) - NA_COLS // 2, 0, GRID_W - NA_COLS)
    c_idx = c_start[:, None] + jnp.arange(NA_COLS)
    dx = c_idx - jnp.arange(GRID_W)[:, None] + (NA_COLS - 1)

    def row_block(inp):
        r, q_r = inp
        r0 = r_start[r]
        k_band = lax.dynamic_slice_in_dim(k, r0, wr, axis=1)
        v_band = lax.dynamic_slice_in_dim(v, r0, wr, axis=1)
        k_win = k_band[:, :, c_idx]
        v_win = v_band[:, :, c_idx]
        dy = r0 + jnp.arange(wr) - r + (NA_ROWS - 1)
        bias = rpb[:, dy[:, None, None], dx[None, :, :]].transpose(0, 2, 1, 3)
        s_loc = jnp.einsum("bqhd,brqjhd->bhqrj", q_r, k_win).astype(jnp.float32) * ATTN_SCALE
        s_loc = (s_loc + bias[None].astype(jnp.float32)).reshape(B_, C_HEADS, GRID_W, wr * NA_COLS)
        s_ctx = jnp.einsum("bqhd,bchd->bhqc", q_r, k_c).astype(jnp.float32) * ATTN_SCALE
        p = jax.nn.softmax(jnp.concatenate([s_loc, s_ctx], axis=-1), axis=-1).astype(v.dtype)
        p_loc = p[..., :wr * NA_COLS].reshape(B_, C_HEADS, GRID_W, wr, NA_COLS)
        p_ctx = p[..., wr * NA_COLS:]
        return jnp.einsum("bhqrj,brqjhd->bqhd", p_loc, v_win) + jnp.einsum("bhqc,bchd->bqhd", p_ctx, v_c)

    o = lax.map(row_block, (jnp.arange(rows), q.transpose(1, 0, 2, 3, 4)))
    y_lat = o.transpose(1, 0, 2, 3, 4).reshape(B_, L, C_W) @ w_out
    y_ctx = None
    if with_ctx_out:
        s = jnp.einsum("bqhd,bkhd->bhqk", q_c, k_c).astype(jnp.float32) * ATTN_SCALE
        p = jax.nn.softmax(s, axis=-1).astype(v_c.dtype)
        y_ctx = jnp.einsum("bhqk,bkhd->bqhd", p, v_c).reshape(B_, n_ctx, C_W) @ w_out
    return y_lat, y_ctx


def conv_ffn(h, w_up, conv_w, conv_b, w_down):
    u = h @ w_up
    u = lax.conv_general_dilated(u, conv_w[:, None, :], window_strides=(1,),
                                 padding=[(CONV_W // 2, CONV_W // 2)],
                                 dimension_numbers=("NWC", "WIO", "NWC"),
                                 feature_group_count=u.shape[-1]) + conv_b
    a, g = jnp.split(u, 2, axis=-1)
    return (jax.nn.silu(g) * a) @ w_down


def setup_inputs(seed: int = 0) -> dict:
    key = jax.random.key(seed)
    ks = jax.random.split(key, 24)

    def nrm(k, shape, s):
        return jax.random.normal(k, shape, jnp.float32) * s

    return {
        "x": nrm(ks[0], (BATCH, SEQ, D_MODEL), 1.0),
        "c": nrm(ks[1], (BATCH, D_MODEL), 1.0),
        "ctx": nrm(ks[2], (BATCH, CTX_LEN, D_MODEL), 1.0),
        "c_ctx": nrm(ks[3], (D_MODEL,), 1.0),
        "w_mod": nrm(ks[4], (DEPTH, D_MODEL, 6 * D_MODEL), D_MODEL ** -0.5),
        "b_mod": nrm(ks[5], (DEPTH, 6 * D_MODEL), 0.02),
        "norm_pre_mix": 1.0 + nrm(ks[6], (DEPTH, D_MODEL), 0.1),
        "norm_post_mix": 1.0 + nrm(ks[7], (DEPTH, D_MODEL), 0.1),
        "norm_pre_ffn": 1.0 + nrm(ks[8], (DEPTH, D_MODEL), 0.1),
        "norm_post_ffn": 1.0 + nrm(ks[9], (DEPTH, D_MODEL), 0.1),
        "even_w_in": nrm(ks[10], (N_EVEN, D_MODEL, EVEN_IN_W), D_MODEL ** -0.5),
        "even_w_out": nrm(ks[11], (N_EVEN, EVEN_OUT_W, D_MODEL), EVEN_OUT_W ** -0.5),
        "even_q_norm": 1.0 + nrm(ks[12], (N_EVEN, HEAD_DIM), 0.1),
        "even_k_norm": 1.0 + nrm(ks[13], (N_EVEN, HEAD_DIM), 0.1),
        "hgrn_lb_logits": nrm(ks[14], (N_EVEN, 2, B_K_W), 1.0),
        "hgrn_out_norm": 1.0 + nrm(ks[15], (N_EVEN, B_VAL_DIM), 0.1),
        "odd_w_qkv": nrm(ks[16], (N_ODD, D_MODEL, 3 * C_W), D_MODEL ** -0.5),
        "odd_w_out": nrm(ks[17], (N_ODD, C_W, D_MODEL), C_W ** -0.5),
        "odd_rpb": nrm(ks[18], (N_ODD, C_HEADS, 2 * NA_ROWS - 1, 2 * NA_COLS - 1), 0.1),
        "ffn_w_up": nrm(ks[19], (DEPTH, D_MODEL, 2 * D_FF), D_MODEL ** -0.5),
        "ffn_conv_w": nrm(ks[20], (DEPTH, CONV_W, 2 * D_FF), CONV_W ** -0.5),
        "ffn_conv_b": nrm(ks[21], (DEPTH, 2 * D_FF), 0.02),
        "ffn_w_down": nrm(ks[22], (DEPTH, D_FF, D_MODEL), D_FF ** -0.5),
    }


def reference(x, c, ctx, c_ctx, w_mod, b_mod, norm_pre_mix, norm_post_mix, norm_pre_ffn, norm_post_ffn,
              even_w_in, even_w_out, even_q_norm, even_k_norm, hgrn_lb_logits, hgrn_out_norm,
              odd_w_qkv, odd_w_out, odd_rpb, ffn_w_up, ffn_conv_w, ffn_conv_b, ffn_w_down):
    p_lb = jax.nn.softmax(hgrn_lb_logits.astype(jnp.float32), axis=0)
    lb_all = jnp.concatenate([jnp.zeros_like(p_lb[:1]), jnp.cumsum(p_lb[1:], axis=0)], axis=0)
    lb_all = lb_all.reshape(N_EVEN, 2, B_HEADS, B_KEY_DIM)

    for l in range(DEPTH):
        with_ctx_out = l < DEPTH - 1
        mod_lat = (jax.nn.silu(c) @ w_mod[l] + b_mod[l])[:, None, :]
        mod_ctx = (jax.nn.silu(c_ctx) @ w_mod[l] + b_mod[l])[None, None, :]
        sh1, sc1, g1, sh2, sc2, g2 = jnp.split(mod_lat, 6, axis=-1)
        sh1c, sc1c, g1c, sh2c, sc2c, g2c = jnp.split(mod_ctx, 6, axis=-1)

        h_lat = rms_norm(x, norm_pre_mix[l]) * (1.0 + sc1) + sh1
        h_ctx = rms_norm(ctx, norm_pre_mix[l]) * (1.0 + sc1c) + sh1c
        if l % 2 == 0:
            e = l // 2
            y_lat, y_ctx = even_mixer(h_lat, h_ctx, even_w_in[e], even_w_out[e], even_q_norm[e], even_k_norm[e],
                                      lb_all[e], hgrn_out_norm[e], with_ctx_out)
        else:
            o = l // 2
            y_lat, y_ctx = odd_mixer(h_lat, h_ctx, odd_w_qkv[o], odd_w_out[o], odd_rpb[o], with_ctx_out)
        x = x + g1 * rms_norm(y_lat, norm_post_mix[l])

        h_lat = rms_norm(x, norm_pre_ffn[l]) * (1.0 + sc2) + sh2
        x = x + g2 * rms_norm(conv_ffn(h_lat, ffn_w_up[l], ffn_conv_w[l], ffn_conv_b[l], ffn_w_down[l]), norm_post_ffn[l])
        if with_ctx_out:
            ctx = ctx + g1c * rms_norm(y_ctx, norm_post_mix[l])
            h_ctx = rms_norm(ctx, norm_pre_ffn[l]) * (1.0 + sc2c) + sh2c
            ctx = ctx + g2c * rms_norm(conv_ffn(h_ctx, ffn_w_up[l], ffn_conv_w[l], ffn_conv_b[l], ffn_w_down[l]), norm_post_ffn[l])
    return x
```

```python
import contextlib
import os


def CUT(n):
    return int(os.environ.get('KCUT', '-1')) == n
import numpy as np
import ml_dtypes
import concourse.bass as bass
import concourse.mybir as mybir
from concourse.bass_utils import run_bass_kernel_spmd

F32 = mybir.dt.float32
BF16 = mybir.dt.bfloat16
AF = mybir.ActivationFunctionType
ALU = mybir.AluOpType
AX = mybir.AxisListType

DM = 1024
TC = 256
TL = 4096
TT = TC + TL
DEPTH = 4
DFF = 2816
EPS = 1e-6
NEG = -30000.0
ATTN_SCALE = 0.125
HG_SCALE = 128 ** -0.5
GRID_W = 64
SAME_ENG_SYNC = os.environ.get("KSES", "1") == "1"

ENGS = ["pe", "act", "dve", "pool", "sp"]


class Sem:
    def __init__(self, h):
        self.h = h
        self.cnt = 0


class Tile:
    def __init__(self, h, name):
        self.h = h
        self.name = name
        self.w = None
        self.r = {}
        self.sem = None

    def __getitem__(self, idx):
        return View(self, self.h[idx])

    @property
    def v(self):
        return View(self, self.h[:])


class View:
    def __init__(self, t, ap):
        self.t = t
        self.ap = ap

    def __getitem__(self, idx):
        return View(self.t, self.ap[idx])

    def re(self, s, **kw):
        return View(self.t, self.ap.rearrange(s, **kw))

    def bc(self, shape):
        return View(self.t, self.ap.to_broadcast(list(shape)))

    def un(self, axis):
        return View(self.t, self.ap.unsqueeze(axis))


def _ap(x):
    return x.ap if isinstance(x, View) else x


class Prog:
    def __init__(self, nc):
        self.nc = nc
        self.gstack = contextlib.ExitStack()
        self.esem = {e: self.gstack.enter_context(nc.semaphore("es_" + e)) for e in ENGS}
        self.bsem = self.gstack.enter_context(nc.semaphore("bar"))
        self.dsems = [Sem(self.gstack.enter_context(nc.semaphore("ds%d" % i))) for i in range(56)]
        self.nstage = 0
        self.uid = 0
        self._reset()

    def _reset(self):
        self.ops = {e: [] for e in ENGS}
        self.seen = {e: {} for e in ENGS}
        self.dnext = 0
        self.used_dsems = []
        self.stack = None

    def eng(self, e):
        nc = self.nc
        return {"pe": nc.tensor, "act": nc.scalar, "dve": nc.vector, "pool": nc.gpsimd, "sp": nc.sync}[e]

    def begin(self, name):
        self.stack = contextlib.ExitStack()
        self.sname = name

    def sb(self, name, shape, dt):
        self.uid += 1
        h = self.stack.enter_context(self.nc.sbuf_tensor("%s_%d" % (name, self.uid), list(shape), dt))
        return Tile(h, name)

    def ps(self, name, shape, dt=F32):
        self.uid += 1
        h = self.stack.enter_context(self.nc.psum_tensor("%s_%d" % (name, self.uid), list(shape), dt))
        return Tile(h, name)

    def sbn(self, name, n, shape, dt):
        return [self.sb("%s%d" % (name, i), shape, dt) for i in range(n)]

    def psn(self, name, n, shape, dt=F32):
        return [self.ps("%s%d" % (name, i), shape, dt) for i in range(n)]

    def _waits(self, e, reads, writes):
        need = {}

        def add(ev):
            if ev is None:
                return
            k, v = ev
            if k == e and (e in ("pe", "sp") or not SAME_ENG_SYNC):
                return
            if need.get(k, 0) < v:
                need[k] = v

        for t in reads:
            add(t.w)
        for t in writes:
            add(t.w)
            for ev in t.r.items():
                add(ev)
        out = []
        for k, v in need.items():
            if self.seen[e].get(k, 0) >= v:
                continue
            self.seen[e][k] = v
            out.append((k, v))
        return out

    def op(self, e, fn, reads=(), writes=()):
        reads = [t for t in reads if t is not None]
        writes = [t for t in writes if t is not None]
        waits = self._waits(e, reads, writes)
        idx = len(self.ops[e]) + 1
        for t in reads:
            if t.r.get(e, 0) < idx:
                t.r[e] = idx
        for t in writes:
            t.w = (e, idx)
            t.r = {}
        self.ops[e].append([waits, fn, None])

    def _dsem(self, tile):
        if tile.sem is None or tile.sem[0] != self.nstage:
            assert self.dnext < len(self.dsems), "out of DMA semaphores in stage " + self.sname
            s = self.dsems[self.dnext]
            self.dnext += 1
            tile.sem = (self.nstage, s)
            self.used_dsems.append(s)
        return tile.sem[1]

    def dma(self, q, out, in_, tile, is_load, **kw):
        o, i = _ap(out), _ap(in_)
        s = self._dsem(tile)
        if is_load:
            waits = self._waits(q, [], [tile])
        else:
            waits = self._waits(q, [tile], [])
        s.cnt += 16
        ev = (s, s.cnt)
        if is_load:
            tile.w = ev
            tile.r = {}
        else:
            tile.r[s] = s.cnt
        self.ops[q].append([waits, lambda eng: eng.dma_start(out=o, in_=i, **kw), s])

    def load(self, view, src, q="sp", **kw):
        self.dma(q, view, src, view.t, True, **kw)

    def store(self, dst, view, q="sp", **kw):
        self.dma(q, dst, view, view.t, False, **kw)

    def end(self):
        nc = self.nc
        sig = {e: set() for e in ENGS}
        for e in ENGS:
            for waits, fn, ds in self.ops[e]:
                for k, v in waits:
                    if isinstance(k, str):
                        sig[k].add(v)
        last = {e: len(self.ops[e]) for e in ENGS if e != "sp"}
        for e, n in last.items():
            if n > 0:
                sig[e].add(n)
        rank = {e: {v: i + 1 for i, v in enumerate(sorted(sig[e]))} for e in ENGS}
        for e in ENGS:
            assert len(rank[e]) < 60000, (self.sname, e, len(rank[e]))
        for s in self.used_dsems:
            assert s.cnt < 60000, (self.sname, s.cnt)
        self.nstage += 1
        k = self.nstage
        esem, bsem = self.esem, self.bsem
        ops = self.ops
        used = list(self.used_dsems)

        def emit(e, eng):
            if k > 1:
                eng.wait_ge(bsem, k - 1)
            for idx, (waits, fn, ds) in enumerate(ops[e]):
                for kk, v in waits:
                    if isinstance(kk, str):
                        eng.wait_ge(esem[kk], rank[kk][v])
                    else:
                        eng.wait_ge(kk.h, v)
                ins = fn(eng)
                if ds is not None:
                    ins.then_inc(ds.h, 16)
                elif (idx + 1) in rank[e]:
                    ins.then_inc(esem[e], 1)
            if e == "sp":
                for e2, n in last.items():
                    if n > 0:
                        eng.wait_ge(esem[e2], rank[e2][n])
                for s in used:
                    eng.wait_ge(s.h, s.cnt)
                for e2 in ENGS:
                    eng.sem_clear(esem[e2])
                for s in used:
                    eng.sem_clear(s.h)
                eng.sem_inc(bsem, 1)

        with nc.Block() as blk:
            blk.sync(lambda eng: emit("sp", eng))
            blk.tensor(lambda eng: emit("pe", eng))
            blk.scalar(lambda eng: emit("act", eng))
            blk.vector(lambda eng: emit("dve", eng))
            blk.gpsimd(lambda eng: emit("pool", eng))
        for s in used:
            s.cnt = 0
        self.stack.close()
        self._reset()

    def finish(self):
        nc = self.nc
        k = self.nstage
        bsem = self.bsem
        with nc.Block() as blk:
            blk.sync(lambda eng: eng.wait_ge(bsem, k))
            blk.scalar(lambda eng: eng.wait_ge(bsem, k))
        self.gstack.close()

    def mm(self, out, lhsT, rhs, start=True, stop=True):
        o, l, r = out.ap, lhsT.ap, rhs.ap
        self.op("pe", lambda e: e.matmul(o, l, r, start=start, stop=stop), [lhsT.t, rhs.t], [out.t])

    def tr(self, out, in_, ident):
        o, i, d = out.ap, in_.ap, ident.ap
        self.op("pe", lambda e: e.transpose(o, i, d), [in_.t, ident.t], [out.t])

    def act(self, out, in_, func, bias=None, scale=None, accum=None, eng="act"):
        o, i = out.ap, in_.ap
        kw = {}
        rd = [in_.t]
        wr = [out.t]
        if bias is not None:
            kw["bias"] = _ap(bias)
            if isinstance(bias, View):
                rd.append(bias.t)
        if scale is not None:
            kw["scale"] = _ap(scale)
            if isinstance(scale, View):
                rd.append(scale.t)
        if accum is not None:
            kw["accum_out"] = accum.ap
            wr.append(accum.t)
        self.op("act", lambda e: e.activation(out=o, in_=i, func=func, **kw), rd, wr)

    def tt(self, out, in0, in1, op, eng="dve"):
        o, a, b = out.ap, in0.ap, in1.ap
        self.op(eng, lambda e: e.tensor_tensor(out=o, in0=a, in1=b, op=op), [in0.t, in1.t], [out.t])

    def ts(self, out, in0, s1, s2, op0, op1=None, eng="dve"):
        o, a = out.ap, in0.ap
        rd = [in0.t]
        for s in (s1, s2):
            if isinstance(s, View):
                rd.append(s.t)
        a1, a2 = _ap(s1), _ap(s2)
        if op1 is None:
            self.op(eng, lambda e: e.tensor_scalar(out=o, in0=a, scalar1=a1, scalar2=None, op0=op0), rd, [out.t])
        else:
            self.op(eng, lambda e: e.tensor_scalar(out=o, in0=a, scalar1=a1, scalar2=a2, op0=op0, op1=op1), rd, [out.t])

    def stt(self, out, in0, scalar, in1, op0, op1):
        o, a, b = out.ap, in0.ap, in1.ap
        rd = [in0.t, in1.t]
        if isinstance(scalar, View):
            rd.append(scalar.t)
        sc = _ap(scalar)
        self.op("dve", lambda e: e.scalar_tensor_tensor(out=o, in0=a, scalar=sc, in1=b, op0=op0, op1=op1), rd, [out.t])

    def copy(self, out, in_, eng="dve"):
        o, i = out.ap, in_.ap
        if eng == "act":
            self.op("act", lambda e: e.copy(out=o, in_=i), [in_.t], [out.t])
        else:
            self.op(eng, lambda e: e.tensor_copy(out=o, in_=i), [in_.t], [out.t])

    def recip(self, out, in_):
        o, i = out.ap, in_.ap
        self.op("dve", lambda e: e.reciprocal(out=o, in_=i), [in_.t], [out.t])

    def reduce(self, out, in_, op=ALU.add, axis=AX.X):
        o, i = out.ap, in_.ap
        self.op("dve", lambda e: e.tensor_reduce(out=o, in_=i, axis=axis, op=op), [in_.t], [out.t])

    def scan(self, out, d0, d1, initial, op0, op1):
        o, a, b = out.ap, d0.ap, d1.ap
        self.op("dve", lambda e: e.tensor_tensor_scan(out=o, data0=a, data1=b, initial=initial, op0=op0, op1=op1),
                [d0.t, d1.t], [out.t])

    def memset(self, view, val, eng="dve"):
        o = view.ap
        self.op(eng, lambda e: e.memset(o, val), [], [view.t])


def token_tiles():
    return [(0, TC)] + [(TC + 512 * i, 512) for i in range(TL // 512)]


def build(debug=(), layers=tuple(range(DEPTH)), stop_after=None, only_inputs=None):
    nc = bass.Bass("TRN2", target_bir_lowering=False)

    def din(name, shape, dt=F32):
        kind = "ExternalInput" if (only_inputs is None or name in only_inputs) else "Internal"
        return nc.dram_tensor(name, list(shape), dt, kind=kind).ap()

    def dscr(name, shape, dt=F32):
        kind = "ExternalOutput" if name in debug else "Internal"
        return nc.dram_tensor(name, list(shape), dt, kind=kind).ap()

    x_in = din("x", [TL, DM])
    ctx_in = din("ctx", [TC, DM])
    cc_in = din("cc", [128, 8, 2])
    w_mod = [din("w_mod%d" % l, [DM, 6 * DM]) for l in range(DEPTH)]
    b_mod = din("b_mod", [DEPTH, 6 * DM])
    nrm = din("nrm", [DEPTH, 4, DM])
    ew_in = [din("even_w_in%d" % e, [DM, 3328]) for e in range(2)]
    ew_out = [din("even_w_out%d" % e, [DM, DM]) for e in range(2)]
    qkn = din("qkn", [2, 2, 64])
    lbl = din("lbl", [128, 16])
    wn_in = din("wn", [128, 2])
    ow_qkv = [din("odd_w_qkv%d" % e, [DM, 3072]) for e in range(2)]
    ow_out = [din("odd_w_out%d" % e, [DM, DM]) for e in range(2)]
    t2_in = [din("t2_%d" % e, [16, 128, 14, 64]) for e in range(2)]
    w_up = [din("ffn_w_up%d" % l, [DM, 2 * DFF]) for l in range(DEPTH)]
    convw = din("convw", [DEPTH, 128, 44, 4])
    w_down = [din("ffn_w_down%d" % l, [DFF, DM]) for l in range(DEPTH)]
    ident_in = din("ident", [128, 128], BF16)
    rope_in = din("rope", [TL, 64])
    hmask_in = din("hmask", [32, 64])
    segm_in = din("segm", [128, 512])
    hmask4_in = din("hmask4", [128, 512])
    cmask_in = din("cmask", [128, 4])
    y_out = nc.dram_tensor("y", [TL, DM], F32, kind="ExternalOutput").ap()

    R = dscr("R", [TT, DM])
    modv = dscr("modv", [DEPTH, 2, 6, DM])
    Ht = dscr("Ht", [DM, TT], BF16)
    Mt = dscr("Mt", [DFF, TT], BF16)
    AOt = dscr("AOt", [DM, TT], BF16)
    Qt = dscr("Qt", [16, 64, TT], BF16)
    Kt = dscr("Kt", [16, 64, TT], BF16)
    Va = dscr("Va", [16, 128, 34, 128], BF16)
    Vb = dscr("Vb", [16, 128, 32, 128], BF16)
    HQ = dscr("HQ", [512, TT])
    ZF = dscr("ZF", [2, 512, TT])
    HV = dscr("HV", [TT, 512], BF16)
    SG = dscr("SG", [512, TT], BF16)
    OFB = dscr("OFB", [2, 512, TT])

    P = Prog(nc)
    nstages = [0]

    def done():
        nstages[0] += 1
        return stop_after is not None and nstages[0] >= stop_after

    def stage_prologue():
        P.begin("prologue")
        dummy = P.sb("dummy", [1, 8], F32)
        P.dma("sp", R[0:TC, :], ctx_in[:, :], dummy, True)
        for i in range(4):
            P.dma("sp", R[TC + i * 1024:TC + (i + 1) * 1024, :], x_in[i * 1024:(i + 1) * 1024, :], dummy, True)
        cc = P.sb("cc", [128, 8, 2], F32)
        sc = P.sb("sc", [128, 8, 2], F32)
        P.load(cc.v, cc_in)
        P.act(sc.v, cc.v, AF.Silu)
        wm = P.sbn("wm", 3, [128, 8, 512], F32)
        pm = P.psn("pm", 2, [2, 512])
        mv = P.sbn("mv", 1, [2, 6, DM], F32)
        bm = P.sbn("bm", 1, [2, 6, DM], F32)
        nr = P.sbn("nr", 1, [2, 4, DM], F32)
        mo = P.sbn("mo", 1, [2, 6, DM], F32)
        it = 0
        for l in layers:
            mvl, bml, nrl, mol = mv[0], bm[0], nr[0], mo[0]
            P.load(bml.v, b_mod[l:l + 1, :].rearrange("o (s d) -> o s d", s=6).partition_broadcast(2))
            P.load(nrl.v, nrm[l:l + 1].partition_broadcast(2))
            for n in range(12):
                w = wm[it % 3]
                p = pm[it % 2]
                it += 1
                P.load(w.v, w_mod[l][:, n * 512:(n + 1) * 512].rearrange("(k p) n -> p k n", p=128))
                for k in range(8):
                    P.mm(p.v, sc[:, k, :], w[:, k, :], start=(k == 0), stop=(k == 7))
                s6, off = n // 2, (n % 2) * 512
                P.tt(mvl[:, s6, off:off + 512], p.v, bml[:, s6, off:off + 512], ALU.add)
            P.stt(mol[:, 0, :], mvl[:, 1, :], 1.0, nrl[:, 0, :], ALU.add, ALU.mult)
            P.copy(mol[:, 1, :], mvl[:, 0, :])
            P.tt(mol[:, 2, :], mvl[:, 2, :], nrl[:, 1, :], ALU.mult)
            P.stt(mol[:, 3, :], mvl[:, 4, :], 1.0, nrl[:, 2, :], ALU.add, ALU.mult)
            P.copy(mol[:, 4, :], mvl[:, 3, :])
            P.tt(mol[:, 5, :], mvl[:, 5, :], nrl[:, 3, :], ALU.mult)
            P.store(modv[l], mol.v)
        P.end()

    def load_cast(P, dst, src, stg, cnt):
        st = stg[cnt[0] % len(stg)]
        cnt[0] += 1
        shp = list(dst.ap.shape)
        if len(shp) == 2:
            sv = st[0:shp[0], 0:shp[1]]
        else:
            sv = st[0:shp[0], 0:shp[1] * shp[2]].re("p (a b) -> p a b", b=shp[2])
        P.load(sv, src)
        P.copy(dst, sv, eng="pool")

    def load_bc(P, tile_view, l, s, which):
        P.load(tile_view, modv[l, s, which:which + 1, :].partition_broadcast(128))

    def norm_rows(P, xv, ssq_scr, ss, rs, A, B, hb, tmp, eps_t):
        P.act(ssq_scr.v, xv, AF.Square, accum=ss.v)
        P.act(rs.v, ss.v, AF.Sqrt, bias=eps_t.v, scale=1.0 / DM)
        P.recip(rs.v, rs.v)
        P.stt(tmp.v, xv, rs.v, A.v, ALU.mult, ALU.mult)
        P.tt(hb.v, tmp.v, B.v, ALU.add)

    def stage_proj(l, even):
        e = l // 2
        P.begin("proj%d" % l)
        NW = 3328 if even else 3072
        wsrc = ew_in[e] if even else ow_qkv[e]
        W = P.sb("W", [128, 8, NW], BF16)
        SK = os.environ.get("KSKIP", "")
        wstg = P.sbn("wstg", 2, [128, 1664], F32)
        wcnt = [0]
        for k in range(8):
            for c0 in range(0, NW, 1664):
                cn = min(1664, NW - c0)
                load_cast(P, W[:, k, c0:c0 + cn], wsrc[k * 128:(k + 1) * 128, c0:c0 + cn], wstg, wcnt)
        ident = P.sb("ident", [128, 128], BF16)
        P.load(ident.v, ident_in)
        eps_t = P.sb("eps", [128, 1], F32)
        P.memset(eps_t.v, EPS)
        AB = {}
        for s in range(2):
            for wh in (0, 1):
                t = P.sb("AB%d%d" % (s, wh), [128, DM], F32)
                if "b" not in SK:
                    load_bc(P, t.v, l, s, wh)
                AB[(s, wh)] = t
        xr = P.sbn("xr", 2, [128, DM], F32)
        scr = P.sb("scr", [128, DM], BF16)
        tmp = P.sb("tmp", [128, DM], F32)
        hb = P.sbn("hb", 2, [128, DM], BF16)
        ss = P.sbn("ss", 2, [128, 1], F32)
        rs = P.sbn("rs", 2, [128, 1], F32)
        hT = P.sbn("hT", 2, [128, 8, 512], BF16)
        pT = P.psn("pT", 2, [128, 8, 128], BF16)
        pF = P.psn("pF", 2, [128, 512])
        pK = P.psn("pK", 2, [128, 512])
        if even:
            wqk = P.sb("wqk", [128, 2, 64], F32)
            P.load(wqk.v, qkn[e:e + 1].partition_broadcast(128))
            rope = P.sbn("rope", 2, [128, 64], F32)
            sq = P.sb("sq", [128, 512], F32)
            s8 = P.sb("s8", [128, 8], F32)
            qn = P.sb("qn", [128, 8, 64], F32)
            t1 = P.sb("t1", [128, 8, 32], F32)
            t2 = P.sb("t2", [128, 8, 32], F32)
            qr = P.sbn("qr", 2, [128, 10, 64], BF16)
            pQ = P.psn("pQ", 1, [64, 10, 128], BF16)
            qT = P.sbn("qT", 2, [64, 10, 512], BF16)
            va = P.sbn("va", 2, [128, 2, 128], BF16)
            for t in va:
                P.memset(t.v, 1.0)
            hv = P.sbn("hv", 2, [128, 512], BF16)
            fo = P.sbn("fo", 3, [128, 512], F32)
            fob = P.sbn("fob", 2, [128, 512], BF16)
        else:
            qT = P.sbn("qT", 2, [64, 16, 512], BF16)
            kT = P.sbn("kT", 2, [64, 16, 512], BF16)
            va = P.sbn("va", 2, [128, 16, 128], BF16)
            for t in va:
                P.memset(t.v, 1.0)
        ci = [0]

        def nxt(lst, key):
            return lst[key % len(lst)]
        if CUT(1):
            P.end()
            return

        sub_i = 0
        f_i = 0
        for ti, (t0, TS) in enumerate(token_tiles()):
            is_ctx = t0 < TC
            s = 1 if is_ctx else 0
            hTt = hT[ti % 2]
            qTt = qT[ti % 2]
            if not even:
                kTt = kT[ti % 2]
            nsub = TS // 128
            for sb_ in range(nsub):
                r0 = t0 + sb_ * 128
                x = xr[sub_i % 2]
                h = hb[sub_i % 2]
                P.load(x.v, R[r0:r0 + 128, :])
                norm_rows(P, x.v, scr, ss[sub_i % 2], rs[sub_i % 2], AB[(s, 0)], AB[(s, 1)], h, tmp, eps_t)
                if CUT(2):
                    P.end()
                    return
                pt = pT[sub_i % 2]
                for k in range(8):
                    P.tr(pt[:, k, :], h[:, k * 128:(k + 1) * 128], ident.v)
                P.copy(hTt[:, :, sb_ * 128:(sb_ + 1) * 128], pt.v, eng="act")
                if CUT(3):
                    P.end()
                    return
                lh = lambda k: hTt[:, k, sb_ * 128:(sb_ + 1) * 128]
                if even:
                    pq = pF[sub_i % 2]
                    pk = pK[sub_i % 2]
                    for k in range(8):
                        P.mm(pq.v, lh(k), W[:, k, 0:512], start=(k == 0), stop=(k == 7))
                    for k in range(8):
                        P.mm(pk[:, 0:256], lh(k), W[:, k, 512:768], start=(k == 0), stop=(k == 7))
                    if CUT(4):
                        P.end()
                        return
                    q_r = qr[sub_i % 2]
                    if not is_ctx:
                        rp = rope[sub_i % 2]
                        P.load(rp.v, rope_in[r0 - TC:r0 - TC + 128, :])
                    for (src, nh, wi, dst0) in ((pq.v, 8, 0, 0), (pk[:, 0:128], 2, 1, 8)):
                        n = nh * 64
                        P.act(sq[:, 0:n], src, AF.Square)
                        P.reduce(s8[:, 0:nh], sq[:, 0:n].re("p (h d) -> p h d", d=64))
                        P.act(s8[:, 0:nh], s8[:, 0:nh], AF.Sqrt, bias=eps_t.v, scale=1.0 / 64)
                        P.recip(s8[:, 0:nh], s8[:, 0:nh])
                        qv = qn[:, 0:nh, :]
                        P.tt(qv, src.re("p (h d) -> p h d", d=64), s8[:, 0:nh].un(2).bc([128, nh, 64]), ALU.mult)
                        dst = q_r[:, dst0:dst0 + nh, :]
                        wv = wqk[:, wi, :].un(1).bc([128, nh, 64])
                        if is_ctx:
                            P.tt(dst, qv, wv, ALU.mult)
                        else:
                            P.tt(qv, qv, wv, ALU.mult)
                            cs = rp[:, 0:32].un(1).bc([128, nh, 32])
                            sn = rp[:, 32:64].un(1).bc([128, nh, 32])
                            x1, x2 = qv[:, :, 0:32], qv[:, :, 32:64]
                            a1, a2 = t1[:, 0:nh, :], t2[:, 0:nh, :]
                            P.tt(a1, x1, cs, ALU.mult)
                            P.tt(a2, x2, sn, ALU.mult)
                            P.tt(dst[:, :, 0:32], a1, a2, ALU.subtract)
                            P.tt(a1, x1, sn, ALU.mult)
                            P.tt(a2, x2, cs, ALU.mult)
                            P.tt(dst[:, :, 32:64], a1, a2, ALU.add)
                    if CUT(5):
                        P.end()
                        return
                    pqt = pQ[0]
                    for hh in range(10):
                        P.tr(pqt[:, hh, :], q_r[:, hh, :], ident.v)
                    P.copy(qTt[:, :, sb_ * 128:(sb_ + 1) * 128], pqt.v, eng="act")
                    if CUT(6):
                        P.end()
                        return
                    vt = va[sub_i % 2]
                    P.copy(vt[:, :, 0:64], pk[:, 128:256].re("p (g d) -> p g d", d=64))
                    P.store(Va[0:2, :, r0 // 128, :].rearrange("g p n -> p g n"), vt.v)
                    pv = pK[(sub_i + 1) % 2] if False else None
                    pq2 = pF[(sub_i + 1) % 2]
                    for k in range(8):
                        P.mm(pq2.v, lh(k), W[:, k, 2304:2816], start=(k == 0), stop=(k == 7))
                    hvt = hv[sub_i % 2]
                    P.copy(hvt.v, pq2.v, eng="act")
                    P.store(HV[r0:r0 + 128, :], hvt.v)
                    if CUT(7):
                        P.end()
                        return
                else:
                    vt = va[sub_i % 2]
                    for half in range(2):
                        pq = (pF if half == 0 else pK)[sub_i % 2]
                        c0 = 2048 + half * 512
                        for k in range(8):
                            P.mm(pq.v, lh(k), W[:, k, c0:c0 + 512], start=(k == 0), stop=(k == 7))
                        P.copy(vt[:, half * 8:(half + 1) * 8, 0:64], pq.v.re("p (g d) -> p g d", d=64),
                               eng=("act" if half == 0 else "dve"))
                    for hf in range(2):
                        P.store(Va[hf * 8:(hf + 1) * 8, :, r0 // 128, :].rearrange("g p n -> p g n"), vt[:, hf * 8:(hf + 1) * 8, :])
                    if not is_ctx:
                        cb = (r0 - TC) // 128
                        for hf in range(2):
                            if cb <= 30:
                                P.store(Vb[hf * 8:(hf + 1) * 8, 0:64, cb, :].rearrange("g p n -> p g n"), vt[64:128, hf * 8:(hf + 1) * 8, :])
                            if cb >= 1:
                                P.store(Vb[hf * 8:(hf + 1) * 8, 64:128, cb - 1, :].rearrange("g p n -> p g n"), vt[0:64, hf * 8:(hf + 1) * 8, :])
                sub_i += 1
            if even:
                P.store(Qt[0:8, :, t0:t0 + TS].rearrange("h d t -> d h t"), qTt[:, 0:8, 0:TS])
                P.store(Kt[0:2, :, t0:t0 + TS].rearrange("h d t -> d h t"), qTt[:, 8:10, 0:TS])
                groups = [(768, "q"), (1280, "f"), (1792, "b"), (2816, "g")]
                for (c0, kind) in groups:
                    for j in range(4):
                        p = pK[f_i % 2]
                        for k in range(8):
                            P.mm(p[:, 0:TS], W[:, k, c0 + j * 128:c0 + (j + 1) * 128], hTt[:, k, 0:TS],
                                 start=(k == 0), stop=(k == 7))
                        if kind == "g":
                            o = fob[f_i % 2]
                            P.act(o[:, 0:TS], p[:, 0:TS], AF.Silu)
                            P.store(SG[j * 128:(j + 1) * 128, t0:t0 + TS], o[:, 0:TS])
                        else:
                            o = fo[f_i % 3]
                            if kind == "q":
                                P.act(o[:, 0:TS], p[:, 0:TS], AF.Silu)
                                P.store(HQ[j * 128:(j + 1) * 128, t0:t0 + TS], o[:, 0:TS])
                            else:
                                P.copy(o[:, 0:TS], p[:, 0:TS], eng="dve")
                                d = 0 if kind == "f" else 1
                                P.store(ZF[d, j * 128:(j + 1) * 128, t0:t0 + TS], o[:, 0:TS])
                        f_i += 1
                if CUT(8):
                    P.end()
                    return
            else:
                for which, dstT in ((0, qTt), (1, kTt)):
                    for hh in range(16):
                        p = pK[f_i % 2] if (f_i % 4) < 2 else pF[f_i % 2]
                        c0 = which * 1024 + hh * 64
                        for k in range(8):
                            P.mm(p[0:64, 0:TS], W[:, k, c0:c0 + 64], hTt[:, k, 0:TS], start=(k == 0), stop=(k == 7))
                        if which == 0:
                            P.act(dstT[:, hh, 0:TS], p[0:64, 0:TS], AF.Identity, scale=ATTN_SCALE)
                        else:
                            P.copy(dstT[:, hh, 0:TS], p[0:64, 0:TS], eng="dve")
                        f_i += 1
                P.store(Qt[:, :, t0:t0 + TS].rearrange("h d t -> d h t"), qTt[:, :, 0:TS])
                P.store(Kt[:, :, t0:t0 + TS].rearrange("h d t -> d h t"), kTt[:, :, 0:TS])
        P.end()

    def stage_gqa(l, with_ctx):
        P.begin("gqa%d" % l)
        KT = P.sb("KT", [128, 2, TT], BF16)
        P.memset(KT[64:128], 0.0, eng="pool")
        P.load(KT[0:64], Kt[0:2].rearrange("g d t -> d g t"))
        VA = P.sb("VA", [128, 2, 34, 128], BF16)
        for g in range(2):
            P.load(VA[:, g], Va[g])
        QT = P.sbn("QT", 2, [128, TT], BF16)
        for q in QT:
            P.memset(q[64:128], 0.0, eng="pool")
        NS = 4
        pS = P.psn("pS", NS, [128, 512])
        pO = P.psn("pO", 2, [128, 512])
        Pt = P.sbn("Pt", NS, [128, 512], BF16)
        dsh = P.sbn("dsh", 2, [64, 512], F32)
        ot = P.sbn("ot", 2, [64, 512], BF16)
        items = []
        oi = 0
        for h in range(8):
            tiles = [(TC + 512 * i, 512, 34) for i in range(8)]
            if with_ctx:
                tiles = [(0, TC, 2)] + tiles
            for (t0, n, nk) in tiles:
                for kc in range(nk):
                    items.append((h, t0, n, nk, kc, oi))
                oi += 1
        LA = 2
        loaded = set()
        for i in range(len(items) + LA):
            if i < len(items):
                h, t0, n, nk, kc, o_i = items[i]
                q = QT[h % 2]
                for hh in (h, h + 1):
                    if hh < 8 and hh not in loaded:
                        loaded.add(hh)
                        P.load(QT[hh % 2][0:64, :], Qt[hh])
                P.mm(pS[i % NS][:, 0:n], KT[:, h // 4, kc * 128:(kc + 1) * 128], q[:, t0:t0 + n])
            j = i - LA
            if j >= 0:
                h, t0, n, nk, kc, o_i = items[j]
                ps, pt, po = pS[j % NS], Pt[j % NS], pO[o_i % 2]
                P.act(pt[:, 0:n], ps[:, 0:n], AF.Exp, scale=ATTN_SCALE)
                P.mm(po[:, 0:n], VA[:, h // 4, kc, :], pt[:, 0:n], start=(kc == 0), stop=(kc == nk - 1))
                if kc == nk - 1:
                    d = dsh[o_i % 2]
                    o = ot[o_i % 2]
                    P.copy(d[:, 0:n], po[64:128, 0:n], eng="dve")
                    P.recip(d[:, 0:n], d[:, 0:n])
                    P.tt(o[:, 0:n], po[0:64, 0:n], d[:, 0:n], ALU.mult)
                    P.store(AOt[h * 64:(h + 1) * 64, t0:t0 + n], o[:, 0:n])
        P.end()

    def stage_hgrn(l, with_ctx):
        e = l // 2
        P.begin("hgrn%d" % l)
        ident = P.sb("ident", [128, 128], BF16)
        P.load(ident.v, ident_in)
        hm4 = P.sb("hm4", [128, 512], F32)
        P.load(hm4.v, hmask4_in)
        cm = P.sb("cm", [128, 4], F32)
        P.load(cm.v, cmask_in)
        segm = P.sb("segm", [128, 512], F32)
        P.load(segm.v, segm_in)
        lb_raw = P.sb("lbraw", [128, 16], F32)
        P.load(lb_raw.v, lbl)
        lb = P.sb("lb", [128, 8], F32)
        oml = P.sb("oml", [128, 8], F32)
        if e == 0:
            P.memset(lb.v, 0.0)
        else:
            P.tt(lb.v, lb_raw[:, 0:8], lb_raw[:, 8:16], ALU.subtract)
            P.act(lb.v, lb.v, AF.Exp)
            P.ts(lb.v, lb.v, 1.0, None, ALU.add)
            P.recip(lb.v, lb.v)
        P.ts(oml.v, lb.v, -1.0, 1.0, ALU.mult, ALU.add)
        chains = [(d, hh) for hh in range(4) for d in range(2)]
        NCH = len(chains)
        S = [P.sb("S%d" % c, [128, 128], F32) for c in range(NCH)]
        for s_ in S:
            P.memset(s_.v, 0.0)
        Smid = P.sbn("Smid", 6, [128, 128], BF16)

        def bufs(name, shape, dt, n=2):
            return [[P.sb("%s%d_%d" % (name, c, i), shape, dt) for i in range(n)] for c in range(NCH)]

        def shared(name):
            two = [P.sb("%s_%d" % (name, i), [128, 512], F32) for i in range(2)]
            return [[two[c % 2]] for c in range(NCH)]
        zt = shared("z")
        qf = shared("qf")
        ft = shared("f")
        lf = shared("lf")
        bt = shared("b")
        at = shared("a")
        qA_ = bufs("qA", [128, 512], BF16)
        kA_ = bufs("kA", [128, 512], BF16)
        qO_ = bufs("qO", [128, 512], BF16)
        kO_ = bufs("kO", [128, 512], BF16)
        qI_ = bufs("qI", [128, 512], BF16)
        kS_ = bufs("kS", [128, 512], BF16)
        vt_ = bufs("v", [128, 4, 128], BF16)
        el = bufs("el", [128, 16], F32)
        osb = bufs("o", [128, 512], F32)
        pAA = P.psn("pAA", 1, [128, 256])
        pKt = P.psn("pKt", 1, [128, 128], BF16)
        pO = P.psn("pO", 4, [128, 128])
        pD = P.psn("pD", 2, [128, 128])
        AtD = P.sbn("AtD", 3, [128, 128], BF16)
        AtO = P.sbn("AtO", 3, [128, 128], BF16)
        ktok = P.sbn("ktok", 16, [128, 128], BF16)
        sbs = [(0, TC)] + [(TC + 512 * i, 512) for i in range(8)]
        cnt = {"a": 0, "k": 0, "s": 0, "d": 0}
        for step in range(9):
            info = []
            for c, (d, hh) in enumerate(chains):
                if d == 0:
                    sbi = step
                else:
                    sbi = 0 if step == 0 else 9 - step
                t0, n = sbs[sbi]
                nch = n // 32
                n16 = n // 16
                bi = step % 2
                z, q, f, lff, b, a = zt[c][0], qf[c][0], ft[c][0], lf[c][0], bt[c][0], at[c][0]
                v = vt_[c][bi]
                elc = el[c][bi]
                info.append((t0, n, nch, bi))
                N = slice(0, n)
                P.load(z[:, N], ZF[d, hh * 128:(hh + 1) * 128, t0:t0 + n])
                P.load(q[:, N], HQ[hh * 128:(hh + 1) * 128, t0:t0 + n])
                P.load(v[:, 0:n // 128, :], HV[t0:t0 + n, hh * 128:(hh + 1) * 128].rearrange("(g p) f -> p g f", p=128))
                lbc = lb[:, d * 4 + hh:d * 4 + hh + 1]
                omc = oml[:, d * 4 + hh:d * 4 + hh + 1]
                P.act(f[:, N], z[:, N], AF.Exp, scale=-1.0)
                P.ts(f[:, N], f[:, N], 1.0, None, ALU.add)
                P.recip(f[:, N], f[:, N])
                P.ts(f[:, N], f[:, N], omc, lbc, ALU.mult, ALU.add)
                P.act(lff[:, N], f[:, N], AF.Ln)
                P.ts(z[:, N], f[:, N], -1.0, 1.0, ALU.mult, ALU.add)
                P.scan(b[:, N], segm[:, N], lff[:, N], 0.0, ALU.mult, ALU.add)
                b3 = b[:, N].re("p (c t) -> p c t", t=32)
                a3 = a[:, N].re("p (c t) -> p c t", t=32)
                b16 = b[:, N].re("p (c t) -> p c t", t=16)
                a16 = a[:, N].re("p (c t) -> p c t", t=16)
                if d == 0:
                    last, r16, mo_ = 31, 7, 15
                else:
                    l3 = lff[:, N].re("p (c t) -> p c t", t=32)
                    P.tt(a3, b3[:, :, 31:32].bc([128, nch, 32]), b3, ALU.subtract)
                    P.tt(b3, a3, l3, ALU.add)
                    last, r16, mo_ = 0, 8, 16
                P.act(elc[:, 0:nch], b3[:, :, last], AF.Exp)
                P.tt(a16, b16, b16[:, :, r16:r16 + 1].bc([128, n16, 16]), ALU.subtract)
                P.act(f[:, N], a[:, N], AF.Exp)
                P.act(lff[:, N], a[:, N], AF.Exp, scale=-1.0)
                P.stt(qA_[c][bi][:, N], q[:, N], HG_SCALE, f[:, N], ALU.mult, ALU.mult)
                P.tt(kA_[c][bi][:, N], z[:, N], lff[:, N], ALU.mult)
                P.tt(a3, b3, b3[:, :, mo_:mo_ + 1].bc([128, nch, 32]), ALU.subtract)
                P.ts(f[:, N], a[:, N], 0.0, None, ALU.min)
                P.act(f[:, N], f[:, N], AF.Exp)
                P.stt(qO_[c][bi][:, N], q[:, N], HG_SCALE, f[:, N], ALU.mult, ALU.mult)
                P.ts(lff[:, N], a[:, N], -1.0, 0.0, ALU.mult, ALU.min)
                P.act(lff[:, N], lff[:, N], AF.Exp)
                P.tt(kO_[c][bi][:, N], z[:, N], lff[:, N], ALU.mult)
                P.act(f[:, N], b[:, N], AF.Exp)
                P.stt(qI_[c][bi][:, N], q[:, N], HG_SCALE, f[:, N], ALU.mult, ALU.mult)
                P.tt(a3, b3[:, :, last:last + 1].bc([128, nch, 32]), b3, ALU.subtract)
                P.act(lff[:, N], a[:, N], AF.Exp)
                P.tt(kS_[c][bi][:, N], z[:, N], lff[:, N], ALU.mult)
            ng = 4 if step > 0 else 2
            for gj in range(ng):
                for wave in range(2):
                    wch = [c for c in range(NCH) if c // 4 == wave]
                    cur = {}
                    for wi, c in enumerate(wch):
                        d, hh = chains[c]
                        t0, n, nch, bi = info[c]
                        gi = gj if d == 0 else ng - 1 - gj
                        c4 = slice(gi * 128, gi * 128 + 128)
                        v = vt_[c][bi]
                        pa, pk, po = pAA[0], pKt[0], pO[wi]
                        atd = AtD[cnt["a"] % 3]
                        ato = AtO[cnt["a"] % 3]
                        cnt["a"] += 1
                        P.mm(pa[:, 0:128], kA_[c][bi][:, c4], qA_[c][bi][:, c4])
                        P.mm(pa[:, 128:256], kO_[c][bi][:, c4], qO_[c][bi][:, c4])
                        P.tt(atd.v, pa[:, 0:128], hm4[:, d * 128:(d + 1) * 128], ALU.mult)
                        P.tt(ato.v, pa[:, 128:256], hm4[:, 256 + d * 128:256 + (d + 1) * 128], ALU.mult)
                        P.tr(pk.v, kS_[c][bi][:, c4], ident.v)
                        kts = []
                        for cc in range(4):
                            kk = ktok[cnt["k"] % 16]
                            cnt["k"] += 1
                            P.act(kk.v, pk.v, AF.Identity, scale=cm[:, cc:cc + 1])
                            kts.append(kk)
                        P.mm(po.v, v[:, gi, :], atd.v, start=True, stop=False)
                        P.mm(po.v, v[:, gi, :], ato.v, start=False, stop=False)
                        cur[c] = (gi, kts, po)
                    for cj in range(4):
                        for wi, c in enumerate(wch):
                            d, hh = chains[c]
                            t0, n, nch, bi = info[c]
                            gi, kts, po = cur[c]
                            cc = cj if d == 0 else 3 - cj
                            ci_ = gi * 4 + cc
                            cols = slice(ci_ * 32, ci_ * 32 + 32)
                            v = vt_[c][bi]
                            elc = el[c][bi]
                            sm = Smid[cnt["s"] % 6]
                            cnt["s"] += 1
                            pd = pD[cnt["d"] % 2]
                            cnt["d"] += 1
                            P.copy(sm.v, S[c].v, eng="act")
                            P.mm(po[:, cc * 32:(cc + 1) * 32], sm.v, qI_[c][bi][:, cols], start=False, stop=(cj == 3))
                            P.mm(pd.v, kts[cc].v, v[:, gi, :])
                            P.stt(S[c].v, S[c].v, elc[:, ci_:ci_ + 1], pd.v, ALU.mult, ALU.add)
                    for wi, c in enumerate(wch):
                        t0, n, nch, bi = info[c]
                        gi, kts, po = cur[c]
                        P.copy(osb[c][bi][:, gi * 128:(gi + 1) * 128], po.v, eng="act")
            for c, (d, hh) in enumerate(chains):
                t0, n, nch, bi = info[c]
                if t0 < TC and not with_ctx:
                    continue
                P.store(OFB[d, hh * 128:(hh + 1) * 128, t0:t0 + n], osb[c][bi][:, 0:n])
        P.end()

    def stage_hgrn_out(l, with_ctx):
        e = l // 2
        P.begin("hgout%d" % l)
        onesm = P.sb("onesm", [128, 128], F32)
        P.memset(onesm.v, 1.0 / 128)
        eps_t = P.sb("eps", [128, 1], F32)
        P.memset(eps_t.v, EPS)
        wn = P.sb("wn", [128, 2], F32)
        P.load(wn.v, wn_in)
        of = P.sbn("of", 2, [128, 512], F32)
        ob = P.sbn("ob", 2, [128, 512], F32)
        sg = P.sbn("sg", 2, [128, 512], BF16)
        sq = P.sbn("sq", 2, [128, 512], F32)
        rr = P.sbn("rr", 2, [128, 512], F32)
        yo = P.sbn("yo", 2, [128, 512], BF16)
        pM = P.psn("pM", 2, [128, 512])
        i = 0
        for hh in range(4):
            for (t0, n) in token_tiles():
                if t0 < TC and not with_ctx:
                    continue
                a, b, g, s, r, y, p = of[i % 2], ob[i % 2], sg[i % 2], sq[i % 2], rr[i % 2], yo[i % 2], pM[i % 2]
                i += 1
                P.load(a[:, 0:n], OFB[0, hh * 128:(hh + 1) * 128, t0:t0 + n])
                P.load(b[:, 0:n], OFB[1, hh * 128:(hh + 1) * 128, t0:t0 + n])
                P.load(g[:, 0:n], SG[hh * 128:(hh + 1) * 128, t0:t0 + n])
                P.tt(a[:, 0:n], a[:, 0:n], b[:, 0:n], ALU.add)
                P.act(s[:, 0:n], a[:, 0:n], AF.Square)
                P.mm(p[:, 0:n], onesm.v, s[:, 0:n])
                P.act(r[:, 0:n], p[:, 0:n], AF.Sqrt, bias=eps_t.v, scale=1.0)
                P.recip(r[:, 0:n], r[:, 0:n])
                P.tt(r[:, 0:n], a[:, 0:n], r[:, 0:n], ALU.mult)
                P.stt(y[:, 0:n], r[:, 0:n], wn[:, e:e + 1], g[:, 0:n], ALU.mult, ALU.mult)
                P.store(AOt[512 + hh * 128:512 + (hh + 1) * 128, t0:t0 + n], y[:, 0:n])
        P.end()

    def stage_natten(l, with_ctx):
        o_ = l // 2
        P.begin("nat%d" % l)
        KT = P.sbn("KT", 2, [128, TT], BF16)
        QT = P.sbn("QT", 2, [128, TT], BF16)
        for t in KT + QT:
            P.memset(t[64:128], 0.0, eng="pool")
        VA = P.sbn("VA", 2, [128, 34, 128], BF16)
        VB = P.sbn("VB", 2, [128, 32, 128], BF16)
        T2 = P.sbn("T2", 2, [128, 14, 64], F32)
        NS = 4
        pS = P.psn("pS", NS, [128, 512])
        pO = P.psn("pO", 2, [128, 512])
        sl = P.sbn("sl", NS, [128, 4, 64], F32)
        pt = P.sbn("pt", NS, [128, 6, 64], BF16)
        dsh = P.sbn("dsh", 2, [64, 512], F32)
        ot = P.sbn("ot", 2, [64, 512], BF16)
        items = []
        oi = 0
        for h in range(int(os.environ.get("KHEADS", "16"))):
            if with_ctx:
                for qh in range(2):
                    items.append((h, "c", qh, oi))
                oi += 1
            for r in range(64):
                items.append((h, "r", r, oi))
                if r % 8 == 7:
                    oi += 1
        LA = 2
        loaded = set()

        def bufs_of(h):
            return KT[h % 2], QT[h % 2], VA[h % 2], VB[h % 2], T2[h % 2]

        def load_head(hh):
            if hh < 16 and hh not in loaded:
                loaded.add(hh)
                kt_, qt_, va_, vb_, t2_ = bufs_of(hh)
                P.load(kt_[0:64, :], Kt[hh])
                P.load(qt_[0:64, :], Qt[hh])
                P.load(va_.v, Va[hh])
                P.load(vb_[:, 0:31, :], Vb[hh][:, 0:31, :])
                P.load(t2_.v, t2_in[o_][hh])

        for i in range(len(items) + LA):
            if i < len(items):
                h, kind, idx, o_i = items[i]
                kt, qt, va, vb, t2 = bufs_of(h)
                load_head(h)
                ps = pS[i % NS]
                if kind == "c":
                    q0 = idx * 128
                    for kc in range(2):
                        P.mm(ps[:, kc * 128:(kc + 1) * 128], kt[:, kc * 128:(kc + 1) * 128], qt[:, q0:q0 + 128])
                else:
                    r = idx
                    r0 = min(max(r - 4, 0), 56)
                    qv = qt[:, TC + r * 64:TC + (r + 1) * 64]
                    for jj in range(4):
                        k0 = TC + (r0 + 2 * jj) * 64
                        P.mm(ps[:, jj * 64:(jj + 1) * 64], kt[:, k0:k0 + 128], qv)
                    for jj in range(2):
                        P.mm(ps[:, (4 + jj) * 64:(5 + jj) * 64], kt[:, jj * 128:(jj + 1) * 128], qv)
            j = i - LA
            if j >= 0:
                h, kind, idx, o_i = items[j]
                load_head(h + 1)
                kt, qt, va, vb, t2 = bufs_of(h)
                ps, p_, s_ = pS[j % NS], pt[j % NS], sl[j % NS]
                po, d, o = pO[o_i % 2], dsh[o_i % 2], ot[o_i % 2]
                ptv = p_.v.re("p a q -> p (a q)")
                if kind == "c":
                    q0 = idx * 128
                    P.act(ptv[:, 0:256], ps[:, 0:256], AF.Exp)
                    for kc in range(2):
                        P.mm(po[:, q0:q0 + 128], va[:, kc, :], ptv[:, kc * 128:(kc + 1) * 128],
                             start=(kc == 0), stop=(kc == 1))
                    if idx == 1:
                        P.copy(d[:, 0:256], po[64:128, 0:256], eng="dve")
                        P.recip(d[:, 0:256], d[:, 0:256])
                        P.tt(o[:, 0:256], po[0:64, 0:256], d[:, 0:256], ALU.mult)
                        P.store(AOt[h * 64:(h + 1) * 64, 0:TC], o[:, 0:256])
                else:
                    r = idx
                    r0 = min(max(r - 4, 0), 56)
                    dy0 = r0 - r + 7
                    rr_ = r % 8
                    t2v = t2.v.re("p (a b) q -> p a b q", b=2)
                    psv = ps[:, 0:384].re("p (a q) -> p a q", q=64)
                    P.tt(s_.v, psv[:, 0:4, :], t2v[:, dy0 // 2:dy0 // 2 + 4, dy0 % 2, :], ALU.add)
                    P.act(p_[:, 0:4, :], s_.v, AF.Exp)
                    P.act(p_[:, 4:6, :], psv[:, 4:6, :], AF.Exp)
                    ocol = po[:, rr_ * 64:(rr_ + 1) * 64]
                    for jj in range(6):
                        if jj < 4:
                            row = r0 + 2 * jj
                            vv = va[:, 2 + row // 2, :] if row % 2 == 0 else vb[:, (row - 1) // 2, :]
                        else:
                            vv = va[:, jj - 4, :]
                        P.mm(ocol, vv, p_[:, jj, :], start=(jj == 0), stop=(jj == 5))
                    if rr_ == 7:
                        r8 = r // 8
                        P.copy(d.v, po[64:128, :], eng="dve")
                        P.recip(d.v, d.v)
                        P.tt(o.v, po[0:64, :], d.v, ALU.mult)
                        P.store(AOt[h * 64:(h + 1) * 64, TC + r8 * 512:TC + (r8 + 1) * 512], o.v)
        P.end()

    def stage_outproj(l, even, with_ctx):
        e = l // 2
        P.begin("oproj%d" % l)
        Wo = P.sb("Wo", [128, 8, DM], BF16)
        wsrc = ew_out[e] if even else ow_out[e]
        wstg = P.sbn("wstg", 2, [128, DM], F32)
        wcnt = [0]
        for k in range(8):
            load_cast(P, Wo[:, k, :], wsrc[k * 128:(k + 1) * 128, :], wstg, wcnt)
        ident = P.sb("ident", [128, 128], BF16)
        P.load(ident.v, ident_in)
        eps_t = P.sb("eps", [128, 1], F32)
        P.memset(eps_t.v, EPS)
        MV = {}
        for s in range(2):
            if s == 1 and not with_ctx:
                continue
            for wh in (2, 3, 4):
                t = P.sb("MV%d%d" % (s, wh), [128, DM], F32)
                load_bc(P, t.v, l, s, wh)
                MV[(s, wh)] = t
        ao = P.sbn("ao", 2, [128, 8, 512], BF16)
        xr = P.sbn("xr", 2, [128, DM], F32)
        xn = P.sbn("xn", 2, [128, DM], F32)
        scr = P.sb("scr", [128, DM], BF16)
        tmp = P.sb("tmp", [128, DM], F32)
        hb = P.sbn("hb", 2, [128, DM], BF16)
        ss = P.sbn("ss", 4, [128, 1], F32)
        rs = P.sbn("rs", 4, [128, 1], F32)
        hT = P.sbn("hT", 2, [128, 8, 512], BF16)
        pY = P.psn("pY", 2, [128, DM])
        pT = P.psn("pT", 2, [128, 8, 128], BF16)
        sub_i = 0
        for ti, (t0, TS) in enumerate(token_tiles()):
            is_ctx = t0 < TC
            if is_ctx and not with_ctx:
                continue
            s = 1 if is_ctx else 0
            a = ao[ti % 2]
            hTt = hT[ti % 2]
            P.load(a[:, :, 0:TS], AOt[:, t0:t0 + TS].rearrange("(k p) t -> p k t", p=128))
            for sb_ in range(TS // 128):
                r0 = t0 + sb_ * 128
                x = xr[sub_i % 2]
                xo = xn[sub_i % 2]
                h = hb[sub_i % 2]
                py = pY[sub_i % 2]
                P.load(x.v, R[r0:r0 + 128, :])
                for half in range(2):
                    for k in range(8):
                        P.mm(py[:, half * 512:(half + 1) * 512], a[:, k, sb_ * 128:(sb_ + 1) * 128],
                             Wo[:, k, half * 512:(half + 1) * 512], start=(k == 0), stop=(k == 7))
                s1, r1 = ss[(2 * sub_i) % 4], rs[(2 * sub_i) % 4]
                P.act(scr.v, py.v, AF.Square, accum=s1.v)
                P.act(r1.v, s1.v, AF.Sqrt, bias=eps_t.v, scale=1.0 / DM)
                P.recip(r1.v, r1.v)
                P.stt(tmp.v, py.v, r1.v, MV[(s, 2)].v, ALU.mult, ALU.mult)
                P.tt(xo.v, tmp.v, x.v, ALU.add)
                P.store(R[r0:r0 + 128, :], xo.v)
                s2, r2 = ss[(2 * sub_i + 1) % 4], rs[(2 * sub_i + 1) % 4]
                norm_rows(P, xo.v, scr, s2, r2, MV[(s, 3)], MV[(s, 4)], h, tmp, eps_t)
                pt = pT[sub_i % 2]
                for k in range(8):
                    P.tr(pt[:, k, :], h[:, k * 128:(k + 1) * 128], ident.v)
                P.copy(hTt[:, :, sb_ * 128:(sb_ + 1) * 128], pt.v, eng="act")
                sub_i += 1
            P.store(Ht[:, t0:t0 + TS].rearrange("(k p) t -> p k t", p=128), hTt[:, :, 0:TS])
        P.end()

    def stage_ffn_up(l, with_ctx):
        P.begin("ffnup%d" % l)
        H = P.sb("H", [128, 8, TT], BF16)
        tiles = [t for t in token_tiles() if with_ctx or t[0] >= TC]
        for (t0, TS) in tiles:
            P.load(H[:, :, t0:t0 + TS], Ht[:, t0:t0 + TS].rearrange("(k p) t -> p k t", p=128))
        cw = P.sb("cw", [128, 44, 4], F32)
        P.load(cw.v, convw[l])
        NU = TT + 3
        U = [[P.sb("U%d_%d" % (b, part), [128, NU], F32) for part in range(2)] for b in range(2)]
        for ub in U:
            for u in ub:
                P.memset(u[:, 0:1], 0.0)
                P.memset(u[:, TC + 1:TC + 2], 0.0)
                P.memset(u[:, NU - 1:NU], 0.0)
        pieces = ([(0, TC)] if with_ctx else []) + [(TC + 1024 * i, 1024) for i in range(4)]
        acc = [[P.sb("acc%d_%d" % (part, pi), [128, qn_], F32) for pi, (q0, qn_) in enumerate(pieces)]
               for part in range(2)]
        mo = P.sbn("mo", 2, [128, TT], BF16)
        wa = P.sbn("wa", 4, [128, 8, 128], BF16)
        wstg = P.sbn("wstg", 2, [128, 1024], F32)
        wcnt = [0]
        pU = P.psn("pU", 6, [128, 512])

        def ucol(t):
            return t + 1 if t < TC else t + 2

        st = {"pi": 0, "wi": 0}

        def emit_mm(j):
            for part in range(2):
                c0 = part * DFF + j * 128
                w = wa[st["wi"] % 4]
                st["wi"] += 1
                load_cast(P, w.v, w_up[l][:, c0:c0 + 128].rearrange("(k p) n -> p k n", p=128), wstg, wcnt)
                u = U[j % 2][part]
                for (t0, TS) in tiles:
                    p = pU[st["pi"] % 6]
                    st["pi"] += 1
                    for k in range(8):
                        P.mm(p[:, 0:TS], w[:, k, :], H[:, k, t0:t0 + TS], start=(k == 0), stop=(k == 7))
                    P.copy(u[:, ucol(t0):ucol(t0) + TS], p[:, 0:TS], eng="act")

        def emit_conv(j):
            for part in range(2):
                cidx = part * 22 + j
                u = U[j % 2][part]
                for pi, (q0, qn_) in enumerate(pieces):
                    ac = acc[part][pi]
                    uc = ucol(q0)
                    P.ts(ac.v, u[:, uc:uc + qn_], cw[:, cidx, 1:2], cw[:, cidx, 3:4], ALU.mult, ALU.add, eng="pool")
                    P.stt(ac.v, u[:, uc - 1:uc - 1 + qn_], cw[:, cidx, 0:1], ac.v, ALU.mult, ALU.add)
                    P.stt(ac.v, u[:, uc + 1:uc + 1 + qn_], cw[:, cidx, 2:3], ac.v, ALU.mult, ALU.add)
            m = mo[j % 2]
            for pi, (q0, qn_) in enumerate(pieces):
                P.act(acc[1][pi].v, acc[1][pi].v, AF.Silu)
                P.tt(m[:, q0:q0 + qn_], acc[1][pi].v, acc[0][pi].v, ALU.mult)
            lo = 0 if with_ctx else TC
            P.store(Mt[j * 128:(j + 1) * 128, lo:TT], m[:, lo:TT])

        for j in range(23):
            if j < 22:
                emit_mm(j)
            if j >= 1:
                emit_conv(j - 1)
        P.end()

    def stage_ffn_down(l, with_ctx, last):
        P.begin("ffndn%d" % l)
        Wd = P.sb("Wd", [128, 22, DM], BF16)
        wstg = P.sbn("wstg", 2, [128, DM], F32)
        wcnt = [0]
        for k in range(22):
            load_cast(P, Wd[:, k, :], w_down[l][k * 128:(k + 1) * 128, :], wstg, wcnt)
        eps_t = P.sb("eps", [128, 1], F32)
        P.memset(eps_t.v, EPS)
        G2 = {}
        for s in range(2):
            if s == 1 and not with_ctx:
                continue
            t = P.sb("G2%d" % s, [128, DM], F32)
            load_bc(P, t.v, l, s, 5)
            G2[s] = t
        M = P.sbn("M", 2, [128, 22, 512], BF16)
        xr = P.sbn("xr", 2, [128, DM], F32)
        xn = P.sbn("xn", 2, [128, DM], F32)
        scr = P.sb("scr", [128, DM], BF16)
        tmp = P.sb("tmp", [128, DM], F32)
        ss = P.sbn("ss", 2, [128, 1], F32)
        rs = P.sbn("rs", 2, [128, 1], F32)
        pY = P.psn("pY", 2, [128, DM])
        sub_i = 0
        for ti, (t0, TS) in enumerate(token_tiles()):
            is_ctx = t0 < TC
            if is_ctx and not with_ctx:
                continue
            s = 1 if is_ctx else 0
            m = M[ti % 2]
            for (k, kn) in ((0, 6), (6, 6), (12, 5), (17, 5)):
                P.load(m[:, k:k + kn, 0:TS], Mt[k * 128:(k + kn) * 128, t0:t0 + TS].rearrange("(k p) t -> p k t", p=128))
            for sb_ in range(TS // 128):
                r0 = t0 + sb_ * 128
                x = xr[sub_i % 2]
                xo = xn[sub_i % 2]
                py = pY[sub_i % 2]
                P.load(x.v, R[r0:r0 + 128, :])
                for half in range(2):
                    for k in range(22):
                        P.mm(py[:, half * 512:(half + 1) * 512], m[:, k, sb_ * 128:(sb_ + 1) * 128],
                             Wd[:, k, half * 512:(half + 1) * 512], start=(k == 0), stop=(k == 21))
                s1, r1 = ss[sub_i % 2], rs[sub_i % 2]
                P.act(scr.v, py.v, AF.Square, accum=s1.v)
                P.act(r1.v, s1.v, AF.Sqrt, bias=eps_t.v, scale=1.0 / DM)
                P.recip(r1.v, r1.v)
                P.stt(tmp.v, py.v, r1.v, G2[s].v, ALU.mult, ALU.mult)
                P.tt(xo.v, tmp.v, x.v, ALU.add)
                if last:
                    P.store(y_out[r0 - TC:r0 - TC + 128, :], xo.v)
                else:
                    P.store(R[r0:r0 + 128, :], xo.v)
                sub_i += 1
        P.end()

    def assemble():
        stage_prologue()
        if done():
            return
        for l in layers:
            even = (l % 2 == 0)
            with_ctx = l < DEPTH - 1
            last = (l == DEPTH - 1)
            stage_proj(l, even)
            if done():
                return
            if even:
                stage_gqa(l, with_ctx)
                if done():
                    return
                stage_hgrn(l, with_ctx)
                if done():
                    return
                stage_hgrn_out(l, with_ctx)
                if done():
                    return
            else:
                stage_natten(l, with_ctx)
                if done():
                    return
            stage_outproj(l, even, with_ctx)
            if done():
                return
            stage_ffn_up(l, with_ctx)
            if done():
                return
            stage_ffn_down(l, with_ctx, last)
            if done():
                return

    assemble()
    P.finish()
    return nc


def _consts():
    ident = np.eye(128, dtype=np.float32).astype(ml_dtypes.bfloat16)
    t = np.arange(TL)
    row = (t // GRID_W).astype(np.float32)
    col = (t % GRID_W).astype(np.float32)
    inv = (np.float32(10000.0) ** (-np.arange(16, dtype=np.float32) / np.float32(16))).astype(np.float32)
    ang = np.concatenate([row[:, None] * inv, col[:, None] * inv], axis=-1).astype(np.float32)
    rope = np.concatenate([np.cos(ang), np.sin(ang)], axis=-1).astype(np.float32)
    s = np.arange(32)
    tri_f = (s[:, None] <= s[None, :]).astype(np.float32)
    tri_b = (s[:, None] >= s[None, :]).astype(np.float32)
    hmask = np.concatenate([tri_f, tri_b], axis=1)
    segm = np.ones((128, 512), np.float32)
    segm[:, ::32] = 0.0
    p = np.arange(128)
    same = (p[:, None] // 32) == (p[None, :] // 32)
    same16 = (p[:, None] // 16) == (p[None, :] // 16)
    dF = (same16 & (p[:, None] <= p[None, :])).astype(np.float32)
    dB = (same16 & (p[:, None] >= p[None, :])).astype(np.float32)
    sh0 = (p[:, None] % 32) < 16
    th1 = (p[None, :] % 32) >= 16
    oF = (same & sh0 & th1).astype(np.float32)
    oB = (same & (~sh0) & (~th1)).astype(np.float32)
    hmask4 = np.concatenate([dF, dB, oF, oB], axis=1)
    cmask = ((p[:, None] // 32) == np.arange(4)[None, :]).astype(np.float32)
    return ident, rope, hmask, segm, hmask4, cmask


def _t2_table(rpb):
    qc = np.arange(64)
    c0 = np.clip(qc - 8, 0, 48)
    kc = np.arange(64)
    inwin = (kc[:, None] >= c0[None, :]) & (kc[:, None] < c0[None, :] + 16)
    dx = np.clip(kc[:, None] - qc[None, :] + 15, 0, 30)
    m = np.arange(14)
    ee = np.arange(2)
    dy = m[None, :] + ee[:, None]
    g = rpb[:, :, dy[:, None, :, None], dx[None, :, None, :]]
    out = np.where(inwin[None, None, None, :, None, :], g, np.float32(NEG)).astype(np.float32)
    return np.ascontiguousarray(out.reshape(2, 16, 128, 14, 64))


def make_in_maps(inp, cores):
    ident, rope, hmask, segm, hmask4, cmask = _consts()
    f = lambda a: np.ascontiguousarray(np.asarray(a, dtype=np.float32))
    nrm = np.ascontiguousarray(np.stack([f(inp["norm_pre_mix"]), f(inp["norm_post_mix"]),
                                         f(inp["norm_pre_ffn"]), f(inp["norm_post_ffn"])], axis=1))
    qkn = np.ascontiguousarray(np.stack([f(inp["even_q_norm"]), f(inp["even_k_norm"])], axis=1))
    lg = f(inp["hgrn_lb_logits"]).reshape(2, 2, 4, 128)
    lbl = np.ascontiguousarray(lg.transpose(3, 0, 1, 2).reshape(128, 16))
    wn = np.ascontiguousarray(f(inp["hgrn_out_norm"]).T)
    t2 = _t2_table(f(inp["odd_rpb"]))
    cw = f(inp["ffn_conv_w"])
    cb = f(inp["ffn_conv_b"])
    cwb = np.concatenate([cw, cb[:, None, :]], axis=1)
    convw = np.ascontiguousarray(cwb.reshape(DEPTH, 4, 44, 128).transpose(0, 3, 2, 1))
    shared = {
        "b_mod": f(inp["b_mod"]), "nrm": nrm, "qkn": qkn, "lbl": lbl, "wn": wn, "convw": convw,
        "ident": ident, "rope": rope, "hmask": hmask, "segm": segm, "hmask4": hmask4, "cmask": cmask,
    }
    for l in range(DEPTH):
        shared["w_mod%d" % l] = f(inp["w_mod"][l])
        shared["ffn_w_up%d" % l] = f(inp["ffn_w_up"][l])
        shared["ffn_w_down%d" % l] = f(inp["ffn_w_down"][l])
    for e in range(2):
        shared["even_w_in%d" % e] = f(inp["even_w_in"][e])
        shared["even_w_out%d" % e] = f(inp["even_w_out"][e])
        shared["odd_w_qkv%d" % e] = f(inp["odd_w_qkv"][e])
        shared["odd_w_out%d" % e] = f(inp["odd_w_out"][e])
        shared["t2_%d" % e] = t2[e]
    x = f(inp["x"])
    ctx = f(inp["ctx"])
    c = f(inp["c"])
    cctx = f(inp["c_ctx"])
    maps = []
    for b in cores:
        cc = np.stack([c[b].reshape(8, 128).T, cctx.reshape(8, 128).T], axis=-1)
        m = dict(shared)
        m["x"] = x[b]
        m["ctx"] = ctx[b]
        m["cc"] = np.ascontiguousarray(cc)
        maps.append(m)
    return maps


def kernel(**inputs):
    nc = build()
    maps = make_in_maps(inputs, list(range(8)))
    res = run_bass_kernel_spmd(nc, maps, core_ids=list(range(8)))
    return np.stack([np.asarray(r["y"], dtype=np.float32) for r in res.results], axis=0)
```

```python
import contextlib
import os


def CUT(n):
    return int(os.environ.get('KCUT', '-1')) == n
import numpy as np
import ml_dtypes
import concourse.bass as bass
import concourse.mybir as mybir
from concourse.bass_utils import run_bass_kernel_spmd

F32 = mybir.dt.float32
BF16 = mybir.dt.bfloat16
AF = mybir.ActivationFunctionType
ALU = mybir.AluOpType
AX = mybir.AxisListType

DM = 1024
TC = 256
TL = 4096
TT = TC + TL
DEPTH = 4
DFF = 2816
EPS = 1e-6
NEG = -30000.0
ATTN_SCALE = 0.125
HG_SCALE = 128 ** -0.5
GRID_W = 64
SAME_ENG_SYNC = os.environ.get("KSES", "1") == "1"

ENGS = ["pe", "act", "dve", "pool", "sp"]


class Sem:
    def __init__(self, h):
        self.h = h
        self.cnt = 0


class Tile:
    def __init__(self, h, name):
        self.h = h
        self.name = name
        self.w = None
        self.r = {}
        self.sem = None

    def __getitem__(self, idx):
        return View(self, self.h[idx])

    @property
    def v(self):
        return View(self, self.h[:])


class View:
    def __init__(self, t, ap):
        self.t = t
        self.ap = ap

    def __getitem__(self, idx):
        return View(self.t, self.ap[idx])

    def re(self, s, **kw):
        return View(self.t, self.ap.rearrange(s, **kw))

    def bc(self, shape):
        return View(self.t, self.ap.to_broadcast(list(shape)))

    def un(self, axis):
        return View(self.t, self.ap.unsqueeze(axis))


def _ap(x):
    return x.ap if isinstance(x, View) else x


class Prog:
    def __init__(self, nc):
        self.nc = nc
        self.gstack = contextlib.ExitStack()
        self.esem = {e: self.gstack.enter_context(nc.semaphore("es_" + e)) for e in ENGS}
        self.bsem = self.gstack.enter_context(nc.semaphore("bar"))
        self.dsems = [Sem(self.gstack.enter_context(nc.semaphore("ds%d" % i))) for i in range(56)]
        self.nstage = 0
        self.uid = 0
        self._reset()

    def _reset(self):
        self.ops = {e: [] for e in ENGS}
        self.seen = {e: {} for e in ENGS}
        self.dnext = 0
        self.used_dsems = []
        self.stack = None

    def eng(self, e):
        nc = self.nc
        return {"pe": nc.tensor, "act": nc.scalar, "dve": nc.vector, "pool": nc.gpsimd, "sp": nc.sync}[e]

    def begin(self, name):
        self.stack = contextlib.ExitStack()
        self.sname = name

    def sb(self, name, shape, dt):
        self.uid += 1
        h = self.stack.enter_context(self.nc.sbuf_tensor("%s_%d" % (name, self.uid), list(shape), dt))
        return Tile(h, name)

    def ps(self, name, shape, dt=F32):
        self.uid += 1
        h = self.stack.enter_context(self.nc.psum_tensor("%s_%d" % (name, self.uid), list(shape), dt))
        return Tile(h, name)

    def sbn(self, name, n, shape, dt):
        return [self.sb("%s%d" % (name, i), shape, dt) for i in range(n)]

    def psn(self, name, n, shape, dt=F32):
        return [self.ps("%s%d" % (name, i), shape, dt) for i in range(n)]

    def _waits(self, e, reads, writes):
        need = {}

        def add(ev):
            if ev is None:
                return
            k, v = ev
            if k == e and (e in ("pe", "sp") or not SAME_ENG_SYNC):
                return
            if need.get(k, 0) < v:
                need[k] = v

        for t in reads:
            add(t.w)
        for t in writes:
            add(t.w)
            for ev in t.r.items():
                add(ev)
        out = []
        for k, v in need.items():
            if self.seen[e].get(k, 0) >= v:
                continue
            self.seen[e][k] = v
            out.append((k, v))
        return out

    def op(self, e, fn, reads=(), writes=()):
        reads = [t for t in reads if t is not None]
        writes = [t for t in writes if t is not None]
        waits = self._waits(e, reads, writes)
        idx = len(self.ops[e]) + 1
        for t in reads:
            if t.r.get(e, 0) < idx:
                t.r[e] = idx
        for t in writes:
            t.w = (e, idx)
            t.r = {}
        self.ops[e].append([waits, fn, None])

    def _dsem(self, tile):
        if tile.sem is None or tile.sem[0] != self.nstage:
            assert self.dnext < len(self.dsems), "out of DMA semaphores in stage " + self.sname
            s = self.dsems[self.dnext]
            self.dnext += 1
            tile.sem = (self.nstage, s)
            self.used_dsems.append(s)
        return tile.sem[1]

    def dma(self, q, out, in_, tile, is_load, **kw):
        o, i = _ap(out), _ap(in_)
        s = self._dsem(tile)
        if is_load:
            waits = self._waits(q, [], [tile])
        else:
            waits = self._waits(q, [tile], [])
        s.cnt += 16
        ev = (s, s.cnt)
        if is_load:
            tile.w = ev
            tile.r = {}
        else:
            tile.r[s] = s.cnt
        self.ops[q].append([waits, lambda eng: eng.dma_start(out=o, in_=i, **kw), s])

    def load(self, view, src, q="sp", **kw):
        self.dma(q, view, src, view.t, True, **kw)

    def store(self, dst, view, q="sp", **kw):
        self.dma(q, dst, view, view.t, False, **kw)

    def end(self):
        nc = self.nc
        sig = {e: set() for e in ENGS}
        for e in ENGS:
            for waits, fn, ds in self.ops[e]:
                for k, v in waits:
                    if isinstance(k, str):
                        sig[k].add(v)
        last = {e: len(self.ops[e]) for e in ENGS if e != "sp"}
        for e, n in last.items():
            if n > 0:
                sig[e].add(n)
        rank = {e: {v: i + 1 for i, v in enumerate(sorted(sig[e]))} for e in ENGS}
        for e in ENGS:
            assert len(rank[e]) < 60000, (self.sname, e, len(rank[e]))
        for s in self.used_dsems:
            assert s.cnt < 60000, (self.sname, s.cnt)
        self.nstage += 1
        k = self.nstage
        esem, bsem = self.esem, self.bsem
        ops = self.ops
        used = list(self.used_dsems)

        def emit(e, eng):
            if k > 1:
                eng.wait_ge(bsem, k - 1)
            for idx, (waits, fn, ds) in enumerate(ops[e]):
                for kk, v in waits:
                    if isinstance(kk, str):
                        eng.wait_ge(esem[kk], rank[kk][v])
                    else:
                        eng.wait_ge(kk.h, v)
                ins = fn(eng)
                if ds is not None:
                    ins.then_inc(ds.h, 16)
                elif (idx + 1) in rank[e]:
                    ins.then_inc(esem[e], 1)
            if e == "sp":
                for e2, n in last.items():
                    if n > 0:
                        eng.wait_ge(esem[e2], rank[e2][n])
                for s in used:
                    eng.wait_ge(s.h, s.cnt)
                for e2 in ENGS:
                    eng.sem_clear(esem[e2])
                for s in used:
                    eng.sem_clear(s.h)
                eng.sem_inc(bsem, 1)

        with nc.Block() as blk:
            blk.sync(lambda eng: emit("sp", eng))
            blk.tensor(lambda eng: emit("pe", eng))
            blk.scalar(lambda eng: emit("act", eng))
            blk.vector(lambda eng: emit("dve", eng))
            blk.gpsimd(lambda eng: emit("pool", eng))
        for s in used:
            s.cnt = 0
        self.stack.close()
        self._reset()

    def finish(self):
        nc = self.nc
        k = self.nstage
        bsem = self.bsem
        with nc.Block() as blk:
            blk.sync(lambda eng: eng.wait_ge(bsem, k))
            blk.scalar(lambda eng: eng.wait_ge(bsem, k))
        self.gstack.close()

    def mm(self, out, lhsT, rhs, start=True, stop=True):
        o, l, r = out.ap, lhsT.ap, rhs.ap
        self.op("pe", lambda e: e.matmul(o, l, r, start=start, stop=stop), [lhsT.t, rhs.t], [out.t])

    def tr(self, out, in_, ident):
        o, i, d = out.ap, in_.ap, ident.ap
        self.op("pe", lambda e: e.transpose(o, i, d), [in_.t, ident.t], [out.t])

    def act(self, out, in_, func, bias=None, scale=None, accum=None, eng="act"):
        o, i = out.ap, in_.ap
        kw = {}
        rd = [in_.t]
        wr = [out.t]
        if bias is not None:
            kw["bias"] = _ap(bias)
            if isinstance(bias, View):
                rd.append(bias.t)
        if scale is not None:
            kw["scale"] = _ap(scale)
            if isinstance(scale, View):
                rd.append(scale.t)
        if accum is not None:
            kw["accum_out"] = accum.ap
            wr.append(accum.t)
        self.op("act", lambda e: e.activation(out=o, in_=i, func=func, **kw), rd, wr)

    def tt(self, out, in0, in1, op, eng="dve"):
        o, a, b = out.ap, in0.ap, in1.ap
        self.op(eng, lambda e: e.tensor_tensor(out=o, in0=a, in1=b, op=op), [in0.t, in1.t], [out.t])

    def ts(self, out, in0, s1, s2, op0, op1=None, eng="dve"):
        o, a = out.ap, in0.ap
        rd = [in0.t]
        for s in (s1, s2):
            if isinstance(s, View):
                rd.append(s.t)
        a1, a2 = _ap(s1), _ap(s2)
        if op1 is None:
            self.op(eng, lambda e: e.tensor_scalar(out=o, in0=a, scalar1=a1, scalar2=None, op0=op0), rd, [out.t])
        else:
            self.op(eng, lambda e: e.tensor_scalar(out=o, in0=a, scalar1=a1, scalar2=a2, op0=op0, op1=op1), rd, [out.t])

    def stt(self, out, in0, scalar, in1, op0, op1):
        o, a, b = out.ap, in0.ap, in1.ap
        rd = [in0.t, in1.t]
        if isinstance(scalar, View):
            rd.append(scalar.t)
        sc = _ap(scalar)
        self.op("dve", lambda e: e.scalar_tensor_tensor(out=o, in0=a, scalar=sc, in1=b, op0=op0, op1=op1), rd, [out.t])

    def copy(self, out, in_, eng="dve"):
        o, i = out.ap, in_.ap
        if eng == "act":
            self.op("act", lambda e: e.copy(out=o, in_=i), [in_.t], [out.t])
        else:
            self.op(eng, lambda e: e.tensor_copy(out=o, in_=i), [in_.t], [out.t])

    def recip(self, out, in_):
        o, i = out.ap, in_.ap
        self.op("dve", lambda e: e.reciprocal(out=o, in_=i), [in_.t], [out.t])

    def reduce(self, out, in_, op=ALU.add, axis=AX.X):
        o, i = out.ap, in_.ap
        self.op("dve", lambda e: e.tensor_reduce(out=o, in_=i, axis=axis, op=op), [in_.t], [out.t])

    def scan(self, out, d0, d1, initial, op0, op1):
        o, a, b = out.ap, d0.ap, d1.ap
        self.op("dve", lambda e: e.tensor_tensor_scan(out=o, data0=a, data1=b, initial=initial, op0=op0, op1=op1),
                [d0.t, d1.t], [out.t])

    def memset(self, view, val, eng="dve"):
        o = view.ap
        self.op(eng, lambda e: e.memset(o, val), [], [view.t])


def token_tiles():
    return [(0, TC)] + [(TC + 512 * i, 512) for i in range(TL // 512)]


def build(debug=(), layers=tuple(range(DEPTH)), stop_after=None, only_inputs=None):
    nc = bass.Bass("TRN2", target_bir_lowering=False)

    def din(name, shape, dt=F32):
        kind = "ExternalInput" if (only_inputs is None or name in only_inputs) else "Internal"
        return nc.dram_tensor(name, list(shape), dt, kind=kind).ap()

    def dscr(name, shape, dt=F32):
        kind = "ExternalOutput" if name in debug else "Internal"
        return nc.dram_tensor(name, list(shape), dt, kind=kind).ap()

    x_in = din("x", [TL, DM])
    ctx_in = din("ctx", [TC, DM])
    cc_in = din("cc", [128, 8, 2])
    w_mod = [din("w_mod%d" % l, [DM, 6 * DM]) for l in range(DEPTH)]
    b_mod = din("b_mod", [DEPTH, 6 * DM])
    nrm = din("nrm", [DEPTH, 4, DM])
    ew_in = [din("even_w_in%d" % e, [DM, 3328]) for e in range(2)]
    ew_out = [din("even_w_out%d" % e, [DM, DM]) for e in range(2)]
    qkn = din("qkn", [2, 2, 64])
    lbl = din("lbl", [128, 16])
    wn_in = din("wn", [128, 2])
    ow_qkv = [din("odd_w_qkv%d" % e, [DM, 3072]) for e in range(2)]
    ow_out = [din("odd_w_out%d" % e, [DM, DM]) for e in range(2)]
    t2_in = [din("t2_%d" % e, [16, 128, 14, 64]) for e in range(2)]
    w_up = [din("ffn_w_up%d" % l, [DM, 2 * DFF]) for l in range(DEPTH)]
    convw = din("convw", [DEPTH, 128, 44, 4])
    w_down = [din("ffn_w_down%d" % l, [DFF, DM]) for l in range(DEPTH)]
    ident_in = din("ident", [128, 128], BF16)
    rope_in = din("rope", [TL, 64])
    hmask_in = din("hmask", [32, 64])
    segm_in = din("segm", [128, 512])
    hmask4_in = din("hmask4", [128, 512])
    cmask_in = din("cmask", [128, 4])
    y_out = nc.dram_tensor("y", [TL, DM], F32, kind="ExternalOutput").ap()

    R = dscr("R", [TT, DM])
    modv = dscr("modv", [DEPTH, 2, 6, DM])
    Ht = dscr("Ht", [DM, TT], BF16)
    Mt = dscr("Mt", [DFF, TT], BF16)
    AOt = dscr("AOt", [DM, TT], BF16)
    Qt = dscr("Qt", [16, 64, TT], BF16)
    Kt = dscr("Kt", [16, 64, TT], BF16)
    Va = dscr("Va", [16, 128, 34, 128], BF16)
    Vb = dscr("Vb", [16, 128, 32, 128], BF16)
    HQ = dscr("HQ", [512, TT])
    ZF = dscr("ZF", [2, 512, TT])
    HV = dscr("HV", [TT, 512], BF16)
    SG = dscr("SG", [512, TT], BF16)
    OFB = dscr("OFB", [2, 512, TT])

    P = Prog(nc)
    nstages = [0]

    def done():
        nstages[0] += 1
        return stop_after is not None and nstages[0] >= stop_after

    def stage_prologue():
        P.begin("prologue")
        dummy = P.sb("dummy", [1, 8], F32)
        P.dma("sp", R[0:TC, :], ctx_in[:, :], dummy, True)
        for i in range(4):
            P.dma("sp", R[TC + i * 1024:TC + (i + 1) * 1024, :], x_in[i * 1024:(i + 1) * 1024, :], dummy, True)
        cc = P.sb("cc", [128, 8, 2], F32)
        sc = P.sb("sc", [128, 8, 2], F32)
        P.load(cc.v, cc_in)
        P.act(sc.v, cc.v, AF.Silu)
        wm = P.sbn("wm", 3, [128, 8, 512], F32)
        pm = P.psn("pm", 2, [2, 512])
        mv = P.sbn("mv", 1, [2, 6, DM], F32)
        bm = P.sbn("bm", 1, [2, 6, DM], F32)
        nr = P.sbn("nr", 1, [2, 4, DM], F32)
        mo = P.sbn("mo", 1, [2, 6, DM], F32)
        it = 0
        for l in layers:
            mvl, bml, nrl, mol = mv[0], bm[0], nr[0], mo[0]
            P.load(bml.v, b_mod[l:l + 1, :].rearrange("o (s d) -> o s d", s=6).partition_broadcast(2))
            P.load(nrl.v, nrm[l:l + 1].partition_broadcast(2))
            for n in range(12):
                w = wm[it % 3]
                p = pm[it % 2]
                it += 1
                P.load(w.v, w_mod[l][:, n * 512:(n + 1) * 512].rearrange("(k p) n -> p k n", p=128))
                for k in range(8):
                    P.mm(p.v, sc[:, k, :], w[:, k, :], start=(k == 0), stop=(k == 7))
                s6, off = n // 2, (n % 2) * 512
                P.tt(mvl[:, s6, off:off + 512], p.v, bml[:, s6, off:off + 512], ALU.add)
            P.stt(mol[:, 0, :], mvl[:, 1, :], 1.0, nrl[:, 0, :], ALU.add, ALU.mult)
            P.copy(mol[:, 1, :], mvl[:, 0, :])
            P.tt(mol[:, 2, :], mvl[:, 2, :], nrl[:, 1, :], ALU.mult)
            P.stt(mol[:, 3, :], mvl[:, 4, :], 1.0, nrl[:, 2, :], ALU.add, ALU.mult)
            P.copy(mol[:, 4, :], mvl[:, 3, :])
            P.tt(mol[:, 5, :], mvl[:, 5, :], nrl[:, 3, :], ALU.mult)
            P.store(modv[l], mol.v)
        P.end()

    def load_cast(P, dst, src, stg, cnt):
        st = stg[cnt[0] % len(stg)]
        cnt[0] += 1
        shp = list(dst.ap.shape)
        if len(shp) == 2:
            sv = st[0:shp[0], 0:shp[1]]
        else:
            sv = st[0:shp[0], 0:shp[1] * shp[2]].re("p (a b) -> p a b", b=shp[2])
        P.load(sv, src)
        P.copy(dst, sv, eng="pool")

    def load_bc(P, tile_view, l, s, which):
        P.load(tile_view, modv[l, s, which:which + 1, :].partition_broadcast(128))

    def norm_rows(P, xv, ssq_scr, ss, rs, A, B, hb, tmp, eps_t):
        P.act(ssq_scr.v, xv, AF.Square, accum=ss.v)
        P.act(rs.v, ss.v, AF.Sqrt, bias=eps_t.v, scale=1.0 / DM)
        P.recip(rs.v, rs.v)
        P.stt(tmp.v, xv, rs.v, A.v, ALU.mult, ALU.mult)
        P.tt(hb.v, tmp.v, B.v, ALU.add)

    def stage_proj(l, even):
        e = l // 2
        P.begin("proj%d" % l)
        NW = 3328 if even else 3072
        wsrc = ew_in[e] if even else ow_qkv[e]
        W = P.sb("W", [128, 8, NW], BF16)
        SK = os.environ.get("KSKIP", "")
        wstg = P.sbn("wstg", 2, [128, 1664], F32)
        wcnt = [0]
        for k in range(8):
            for c0 in range(0, NW, 1664):
                cn = min(1664, NW - c0)
                load_cast(P, W[:, k, c0:c0 + cn], wsrc[k * 128:(k + 1) * 128, c0:c0 + cn], wstg, wcnt)
        ident = P.sb("ident", [128, 128], BF16)
        P.load(ident.v, ident_in)
        eps_t = P.sb("eps", [128, 1], F32)
        P.memset(eps_t.v, EPS)
        AB = {}
        for s in range(2):
            for wh in (0, 1):
                t = P.sb("AB%d%d" % (s, wh), [128, DM], F32)
                if "b" not in SK:
                    load_bc(P, t.v, l, s, wh)
                AB[(s, wh)] = t
        xr = P.sbn("xr", 2, [128, DM], F32)
        scr = P.sb("scr", [128, DM], BF16)
        tmp = P.sb("tmp", [128, DM], F32)
        hb = P.sbn("hb", 2, [128, DM], BF16)
        ss = P.sbn("ss", 2, [128, 1], F32)
        rs = P.sbn("rs", 2, [128, 1], F32)
        hT = P.sbn("hT", 2, [128, 8, 512], BF16)
        pT = P.psn("pT", 1 if even else 2, [128, 8, 128], BF16)
        pF = P.psn("pF", 2, [128, 512])
        pK = P.psn("pK", 2, [128, 512])
        pendB = [None]
        if even:
            pHV = P.psn("pHV", 1, [128, 512])
            wqk = P.sb("wqk", [128, 2, 64], F32)
            P.load(wqk.v, qkn[e:e + 1].partition_broadcast(128))
            rope = P.sbn("rope", 2, [128, 64], F32)
            sq = P.sb("sq", [128, 512], F32)
            s8 = P.sb("s8", [128, 8], F32)
            qn = P.sb("qn", [128, 8, 64], F32)
            t1 = P.sb("t1", [128, 8, 32], F32)
            t2 = P.sb("t2", [128, 8, 32], F32)
            qr = P.sbn("qr", 2, [128, 10, 64], BF16)
            pQ = P.psn("pQ", 1, [64, 10, 128], BF16)
            qT = P.sbn("qT", 2, [64, 10, 512], BF16)
            va = P.sbn("va", 2, [128, 2, 128], BF16)
            for t in va:
                P.memset(t.v, 1.0)
            hv = P.sbn("hv", 2, [128, 512], BF16)
            fo = P.sbn("fo", 3, [128, 512], F32)
            fob = P.sbn("fob", 2, [128, 512], BF16)
        else:
            qT = P.sbn("qT", 2, [64, 16, 512], BF16)
            kT = P.sbn("kT", 2, [64, 16, 512], BF16)
            va = P.sbn("va", 2, [128, 16, 128], BF16)
            for t in va:
                P.memset(t.v, 1.0)
        ci = [0]

        def nxt(lst, key):
            return lst[key % len(lst)]
        if CUT(1):
            P.end()
            return

        sub_i = 0
        f_i = 0
        for ti, (t0, TS) in enumerate(token_tiles()):
            is_ctx = t0 < TC
            s = 1 if is_ctx else 0
            hTt = hT[ti % 2]
            qTt = qT[ti % 2]
            if not even:
                kTt = kT[ti % 2]
            nsub = TS // 128
            for sb_ in range(nsub):
                r0 = t0 + sb_ * 128
                x = xr[sub_i % 2]
                h = hb[sub_i % 2]
                P.load(x.v, R[r0:r0 + 128, :])
                norm_rows(P, x.v, scr, ss[sub_i % 2], rs[sub_i % 2], AB[(s, 0)], AB[(s, 1)], h, tmp, eps_t)
                if CUT(2):
                    P.end()
                    return
                pt = pT[sub_i % len(pT)]
                for k in range(8):
                    P.tr(pt[:, k, :], h[:, k * 128:(k + 1) * 128], ident.v)
                P.copy(hTt[:, :, sb_ * 128:(sb_ + 1) * 128], pt.v, eng="act")
                if CUT(3):
                    P.end()
                    return
                lh = lambda k: hTt[:, k, sb_ * 128:(sb_ + 1) * 128]
                if even:
                    pq = pF[sub_i % 2]
                    pk = pK[sub_i % 2]
                    for k in range(8):
                        P.mm(pq.v, lh(k), W[:, k, 0:512], start=(k == 0), stop=(k == 7))
                    for k in range(8):
                        P.mm(pk[:, 0:256], lh(k), W[:, k, 512:768], start=(k == 0), stop=(k == 7))
                    rp = rope[sub_i % 2]
                    if not is_ctx:
                        P.load(rp.v, rope_in[r0 - TC:r0 - TC + 128, :])
                    vt = va[sub_i % 2]
                    P.copy(vt[:, :, 0:64], pk[:, 128:256].re("p (g d) -> p g d", d=64))
                    P.store(Va[0:2, :, r0 // 128, :].rearrange("g p n -> p g n"), vt.v)
                    pq2 = pHV[0]
                    for k in range(8):
                        P.mm(pq2.v, lh(k), W[:, k, 2304:2816], start=(k == 0), stop=(k == 7))
                    hvt = hv[sub_i % 2]
                    P.copy(hvt.v, pq2.v, eng="act")
                    P.store(HV[r0:r0 + 128, :], hvt.v)
                    if pendB[0] is not None:
                        pendB[0]()
                        pendB[0] = None

                    def stageB(pq=pq, pk=pk, q_r=qr[sub_i % 2], rp=rp, is_ctx=is_ctx, qTt=qTt, sb_=sb_):
                        for (src, nh, wi, dst0) in ((pq.v, 8, 0, 0), (pk[:, 0:128], 2, 1, 8)):
                            n = nh * 64
                            P.act(sq[:, 0:n], src, AF.Square)
                            P.reduce(s8[:, 0:nh], sq[:, 0:n].re("p (h d) -> p h d", d=64))
                            P.act(s8[:, 0:nh], s8[:, 0:nh], AF.Sqrt, bias=eps_t.v, scale=1.0 / 64)
                            P.recip(s8[:, 0:nh], s8[:, 0:nh])
                            qv = qn[:, 0:nh, :]
                            P.tt(qv, src.re("p (h d) -> p h d", d=64), s8[:, 0:nh].un(2).bc([128, nh, 64]), ALU.mult)
                            dst = q_r[:, dst0:dst0 + nh, :]
                            wv = wqk[:, wi, :].un(1).bc([128, nh, 64])
                            if is_ctx:
                                P.tt(dst, qv, wv, ALU.mult)
                            else:
                                P.tt(qv, qv, wv, ALU.mult)
                                cs = rp[:, 0:32].un(1).bc([128, nh, 32])
                                sn = rp[:, 32:64].un(1).bc([128, nh, 32])
                                x1, x2 = qv[:, :, 0:32], qv[:, :, 32:64]
                                a1, a2 = t1[:, 0:nh, :], t2[:, 0:nh, :]
                                P.tt(a1, x1, cs, ALU.mult)
                                P.tt(a2, x2, sn, ALU.mult)
                                P.tt(dst[:, :, 0:32], a1, a2, ALU.subtract)
                                P.tt(a1, x1, sn, ALU.mult)
                                P.tt(a2, x2, cs, ALU.mult)
                                P.tt(dst[:, :, 32:64], a1, a2, ALU.add)
                        pqt = pQ[0]
                        for hh in range(10):
                            P.tr(pqt[:, hh, :], q_r[:, hh, :], ident.v)
                        P.copy(qTt[:, :, sb_ * 128:(sb_ + 1) * 128], pqt.v, eng="act")
                    pendB[0] = stageB
                else:
                    vt = va[sub_i % 2]
                    for half in range(2):
                        pq = (pF if half == 0 else pK)[sub_i % 2]
                        c0 = 2048 + half * 512
                        for k in range(8):
                            P.mm(pq.v, lh(k), W[:, k, c0:c0 + 512], start=(k == 0), stop=(k == 7))
                        P.copy(vt[:, half * 8:(half + 1) * 8, 0:64], pq.v.re("p (g d) -> p g d", d=64),
                               eng=("act" if half == 0 else "dve"))
                    for hf in range(2):
                        P.store(Va[hf * 8:(hf + 1) * 8, :, r0 // 128, :].rearrange("g p n -> p g n"), vt[:, hf * 8:(hf + 1) * 8, :])
                    if not is_ctx:
                        cb = (r0 - TC) // 128
                        for hf in range(2):
                            if cb <= 30:
                                P.store(Vb[hf * 8:(hf + 1) * 8, 0:64, cb, :].rearrange("g p n -> p g n"), vt[64:128, hf * 8:(hf + 1) * 8, :])
                            if cb >= 1:
                                P.store(Vb[hf * 8:(hf + 1) * 8, 64:128, cb - 1, :].rearrange("g p n -> p g n"), vt[0:64, hf * 8:(hf + 1) * 8, :])
                sub_i += 1
            if even:
                if pendB[0] is not None:
                    pendB[0]()
                    pendB[0] = None
                P.store(Qt[0:8, :, t0:t0 + TS].rearrange("h d t -> d h t"), qTt[:, 0:8, 0:TS])
                P.store(Kt[0:2, :, t0:t0 + TS].rearrange("h d t -> d h t"), qTt[:, 8:10, 0:TS])
                groups = [(768, "q"), (1280, "f"), (1792, "b"), (2816, "g")]
                for (c0, kind) in groups:
                    for j in range(4):
                        p = pK[f_i % 2]
                        for k in range(8):
                            P.mm(p[:, 0:TS], W[:, k, c0 + j * 128:c0 + (j + 1) * 128], hTt[:, k, 0:TS],
                                 start=(k == 0), stop=(k == 7))
                        if kind == "g":
                            o = fob[f_i % 2]
                            P.act(o[:, 0:TS], p[:, 0:TS], AF.Silu)
                            P.store(SG[j * 128:(j + 1) * 128, t0:t0 + TS], o[:, 0:TS])
                        else:
                            o = fo[f_i % 3]
                            if kind == "q":
                                P.act(o[:, 0:TS], p[:, 0:TS], AF.Silu)
                                P.store(HQ[j * 128:(j + 1) * 128, t0:t0 + TS], o[:, 0:TS])
                            else:
                                P.copy(o[:, 0:TS], p[:, 0:TS], eng="dve")
                                d = 0 if kind == "f" else 1
                                P.store(ZF[d, j * 128:(j + 1) * 128, t0:t0 + TS], o[:, 0:TS])
                        f_i += 1
                if CUT(8):
                    P.end()
                    return
            else:
                for which, dstT in ((0, qTt), (1, kTt)):
                    for hh in range(16):
                        p = pK[f_i % 2] if (f_i % 4) < 2 else pF[f_i % 2]
                        c0 = which * 1024 + hh * 64
                        for k in range(8):
                            P.mm(p[0:64, 0:TS], W[:, k, c0:c0 + 64], hTt[:, k, 0:TS], start=(k == 0), stop=(k == 7))
                        if which == 0:
                            P.act(dstT[:, hh, 0:TS], p[0:64, 0:TS], AF.Identity, scale=ATTN_SCALE)
                        else:
                            P.copy(dstT[:, hh, 0:TS], p[0:64, 0:TS], eng="dve")
                        f_i += 1
                P.store(Qt[:, :, t0:t0 + TS].rearrange("h d t -> d h t"), qTt[:, :, 0:TS])
                P.store(Kt[:, :, t0:t0 + TS].rearrange("h d t -> d h t"), kTt[:, :, 0:TS])
        P.end()

    def stage_gqa(l, with_ctx):
        P.begin("gqa%d" % l)
        KT = P.sb("KT", [128, 2, TT], BF16)
        P.memset(KT[64:128], 0.0, eng="pool")
        P.load(KT[0:64], Kt[0:2].rearrange("g d t -> d g t"))
        VA = P.sb("VA", [128, 2, 34, 128], BF16)
        for g in range(2):
            P.load(VA[:, g], Va[g])
        QT = P.sbn("QT", 2, [128, TT], BF16)
        for q in QT:
            P.memset(q[64:128], 0.0, eng="pool")
        NS = 4
        pS = P.psn("pS", NS, [128, 512])
        pO = P.psn("pO", 2, [128, 512])
        Pt = P.sbn("Pt", NS, [128, 512], BF16)
        dsh = P.sbn("dsh", 2, [64, 512], F32)
        ot = P.sbn("ot", 2, [64, 512], BF16)
        items = []
        oi = 0
        for h in range(8):
            tiles = [(TC + 512 * i, 512, 34) for i in range(8)]
            if with_ctx:
                tiles = [(0, TC, 2)] + tiles
            for (t0, n, nk) in tiles:
                for kc in range(nk):
                    items.append((h, t0, n, nk, kc, oi))
                oi += 1
        LA = 2
        loaded = set()
        for i in range(len(items) + LA):
            if i < len(items):
                h, t0, n, nk, kc, o_i = items[i]
                q = QT[h % 2]
                for hh in (h, h + 1):
                    if hh < 8 and hh not in loaded:
                        loaded.add(hh)
                        P.load(QT[hh % 2][0:64, :], Qt[hh])
                P.mm(pS[i % NS][:, 0:n], KT[:, h // 4, kc * 128:(kc + 1) * 128], q[:, t0:t0 + n])
            j = i - LA
            if j >= 0:
                h, t0, n, nk, kc, o_i = items[j]
                ps, pt, po = pS[j % NS], Pt[j % NS], pO[o_i % 2]
                P.act(pt[:, 0:n], ps[:, 0:n], AF.Exp, scale=ATTN_SCALE)
                P.mm(po[:, 0:n], VA[:, h // 4, kc, :], pt[:, 0:n], start=(kc == 0), stop=(kc == nk - 1))
                if kc == nk - 1:
                    d = dsh[o_i % 2]
                    o = ot[o_i % 2]
                    P.copy(d[:, 0:n], po[64:128, 0:n], eng="dve")
                    P.recip(d[:, 0:n], d[:, 0:n])
                    P.tt(o[:, 0:n], po[0:64, 0:n], d[:, 0:n], ALU.mult)
                    P.store(AOt[h * 64:(h + 1) * 64, t0:t0 + n], o[:, 0:n])
        P.end()

    def stage_hgrn(l, with_ctx):
        e = l // 2
        P.begin("hgrn%d" % l)
        ident = P.sb("ident", [128, 128], BF16)
        P.load(ident.v, ident_in)
        hm4 = P.sb("hm4", [128, 512], F32)
        P.load(hm4.v, hmask4_in)
        cm = P.sb("cm", [128, 4], F32)
        P.load(cm.v, cmask_in)
        segm = P.sb("segm", [128, 512], F32)
        P.load(segm.v, segm_in)
        lb_raw = P.sb("lbraw", [128, 16], F32)
        P.load(lb_raw.v, lbl)
        lb = P.sb("lb", [128, 8], F32)
        oml = P.sb("oml", [128, 8], F32)
        if e == 0:
            P.memset(lb.v, 0.0)
        else:
            P.tt(lb.v, lb_raw[:, 0:8], lb_raw[:, 8:16], ALU.subtract)
            P.act(lb.v, lb.v, AF.Exp)
            P.ts(lb.v, lb.v, 1.0, None, ALU.add)
            P.recip(lb.v, lb.v)
        P.ts(oml.v, lb.v, -1.0, 1.0, ALU.mult, ALU.add)
        chains = [(d, hh) for hh in range(4) for d in range(2)]
        NCH = len(chains)
        S = [P.sb("S%d" % c, [128, 128], F32) for c in range(NCH)]
        for s_ in S:
            P.memset(s_.v, 0.0)
        Smid = P.sbn("Smid", 6, [128, 128], BF16)

        def bufs(name, shape, dt, n=2):
            return [[P.sb("%s%d_%d" % (name, c, i), shape, dt) for i in range(n)] for c in range(NCH)]

        def shared(name):
            two = [P.sb("%s_%d" % (name, i), [128, 512], F32) for i in range(2)]
            return [[two[c % 2]] for c in range(NCH)]
        zt = shared("z")
        qf = shared("qf")
        ft = shared("f")
        lf = shared("lf")
        bt = shared("b")
        at = shared("a")
        qA_ = bufs("qA", [128, 512], BF16)
        kA_ = bufs("kA", [128, 512], BF16)
        qO_ = bufs("qO", [128, 512], BF16)
        kO_ = bufs("kO", [128, 512], BF16)
        qI_ = bufs("qI", [128, 512], BF16)
        kS_ = bufs("kS", [128, 512], BF16)
        vt_ = bufs("v", [128, 4, 128], BF16)
        el = bufs("el", [128, 16], F32)
        osb = bufs("o", [128, 512], F32)
        pAA = P.psn("pAA", 1, [128, 256])
        pKt = P.psn("pKt", 1, [128, 128], BF16)
        pO = P.psn("pO", 4, [128, 128])
        pD = P.psn("pD", 2, [128, 128])
        AtD = P.sbn("AtD", 3, [128, 128], BF16)
        AtO = P.sbn("AtO", 3, [128, 128], BF16)
        ktok = P.sbn("ktok", 16, [128, 128], BF16)
        sbs = [(0, TC)] + [(TC + 512 * i, 512) for i in range(8)]
        cnt = {"a": 0, "k": 0, "s": 0, "d": 0}
        for step in range(9):
            info = []
            for c, (d, hh) in enumerate(chains):
                if d == 0:
                    sbi = step
                else:
                    sbi = 0 if step == 0 else 9 - step
                t0, n = sbs[sbi]
                nch = n // 32
                n16 = n // 16
                bi = step % 2
                z, q, f, lff, b, a = zt[c][0], qf[c][0], ft[c][0], lf[c][0], bt[c][0], at[c][0]
                v = vt_[c][bi]
                elc = el[c][bi]
                info.append((t0, n, nch, bi))
                N = slice(0, n)
                P.load(z[:, N], ZF[d, hh * 128:(hh + 1) * 128, t0:t0 + n])
                P.load(q[:, N], HQ[hh * 128:(hh + 1) * 128, t0:t0 + n])
                P.load(v[:, 0:n // 128, :], HV[t0:t0 + n, hh * 128:(hh + 1) * 128].rearrange("(g p) f -> p g f", p=128))
                lbc = lb[:, d * 4 + hh:d * 4 + hh + 1]
                omc = oml[:, d * 4 + hh:d * 4 + hh + 1]
                P.act(f[:, N], z[:, N], AF.Exp, scale=-1.0)
                P.ts(f[:, N], f[:, N], 1.0, None, ALU.add)
                P.recip(f[:, N], f[:, N])
                P.ts(f[:, N], f[:, N], omc, lbc, ALU.mult, ALU.add)
                P.act(lff[:, N], f[:, N], AF.Ln)
                P.ts(z[:, N], f[:, N], -1.0, 1.0, ALU.mult, ALU.add)
                P.scan(b[:, N], segm[:, N], lff[:, N], 0.0, ALU.mult, ALU.add)
                b3 = b[:, N].re("p (c t) -> p c t", t=32)
                a3 = a[:, N].re("p (c t) -> p c t", t=32)
                b16 = b[:, N].re("p (c t) -> p c t", t=16)
                a16 = a[:, N].re("p (c t) -> p c t", t=16)
                if d == 0:
                    last, r16, mo_ = 31, 7, 15
                else:
                    l3 = lff[:, N].re("p (c t) -> p c t", t=32)
                    P.tt(a3, b3[:, :, 31:32].bc([128, nch, 32]), b3, ALU.subtract)
                    P.tt(b3, a3, l3, ALU.add)
                    last, r16, mo_ = 0, 8, 16
                P.act(elc[:, 0:nch], b3[:, :, last], AF.Exp)
                P.tt(a16, b16, b16[:, :, r16:r16 + 1].bc([128, n16, 16]), ALU.subtract)
                P.act(f[:, N], a[:, N], AF.Exp)
                P.act(lff[:, N], a[:, N], AF.Exp, scale=-1.0)
                P.stt(qA_[c][bi][:, N], q[:, N], HG_SCALE, f[:, N], ALU.mult, ALU.mult)
                P.tt(kA_[c][bi][:, N], z[:, N], lff[:, N], ALU.mult)
                P.tt(a3, b3, b3[:, :, mo_:mo_ + 1].bc([128, nch, 32]), ALU.subtract)
                P.ts(f[:, N], a[:, N], 0.0, None, ALU.min)
                P.act(f[:, N], f[:, N], AF.Exp)
                P.stt(qO_[c][bi][:, N], q[:, N], HG_SCALE, f[:, N], ALU.mult, ALU.mult)
                P.ts(lff[:, N], a[:, N], -1.0, 0.0, ALU.mult, ALU.min)
                P.act(lff[:, N], lff[:, N], AF.Exp)
                P.tt(kO_[c][bi][:, N], z[:, N], lff[:, N], ALU.mult)
                P.act(f[:, N], b[:, N], AF.Exp)
                P.stt(qI_[c][bi][:, N], q[:, N], HG_SCALE, f[:, N], ALU.mult, ALU.mult)
                P.tt(a3, b3[:, :, last:last + 1].bc([128, nch, 32]), b3, ALU.subtract)
                P.act(lff[:, N], a[:, N], AF.Exp)
                P.tt(kS_[c][bi][:, N], z[:, N], lff[:, N], ALU.mult)
            ng = 4 if step > 0 else 2
            for gj in range(ng):
                for wave in range(2):
                    wch = [c for c in range(NCH) if c // 4 == wave]
                    cur = {}
                    for wi, c in enumerate(wch):
                        d, hh = chains[c]
                        t0, n, nch, bi = info[c]
                        gi = gj if d == 0 else ng - 1 - gj
                        c4 = slice(gi * 128, gi * 128 + 128)
                        v = vt_[c][bi]
                        pa, pk, po = pAA[0], pKt[0], pO[wi]
                        atd = AtD[cnt["a"] % 3]
                        ato = AtO[cnt["a"] % 3]
                        cnt["a"] += 1
                        P.mm(pa[:, 0:128], kA_[c][bi][:, c4], qA_[c][bi][:, c4])
                        P.mm(pa[:, 128:256], kO_[c][bi][:, c4], qO_[c][bi][:, c4])
                        P.tt(atd.v, pa[:, 0:128], hm4[:, d * 128:(d + 1) * 128], ALU.mult)
                        P.tt(ato.v, pa[:, 128:256], hm4[:, 256 + d * 128:256 + (d + 1) * 128], ALU.mult)
                        P.tr(pk.v, kS_[c][bi][:, c4], ident.v)
                        kts = []
                        for cc in range(4):
                            kk = ktok[cnt["k"] % 16]
                            cnt["k"] += 1
                            P.act(kk.v, pk.v, AF.Identity, scale=cm[:, cc:cc + 1])
                            kts.append(kk)
                        P.mm(po.v, v[:, gi, :], atd.v, start=True, stop=False)
                        P.mm(po.v, v[:, gi, :], ato.v, start=False, stop=False)
                        cur[c] = (gi, kts, po)
                    for cj in range(4):
                        for wi, c in enumerate(wch):
                            d, hh = chains[c]
                            t0, n, nch, bi = info[c]
                            gi, kts, po = cur[c]
                            cc = cj if d == 0 else 3 - cj
                            ci_ = gi * 4 + cc
                            cols = slice(ci_ * 32, ci_ * 32 + 32)
                            v = vt_[c][bi]
                            elc = el[c][bi]
                            sm = Smid[cnt["s"] % 6]
                            cnt["s"] += 1
                            pd = pD[cnt["d"] % 2]
                            cnt["d"] += 1
                            P.copy(sm.v, S[c].v, eng="act")
                            P.mm(po[:, cc * 32:(cc + 1) * 32], sm.v, qI_[c][bi][:, cols], start=False, stop=(cj == 3))
                            P.mm(pd.v, kts[cc].v, v[:, gi, :])
                            P.stt(S[c].v, S[c].v, elc[:, ci_:ci_ + 1], pd.v, ALU.mult, ALU.add)
                    for wi, c in enumerate(wch):
                        t0, n, nch, bi = info[c]
                        gi, kts, po = cur[c]
                        P.copy(osb[c][bi][:, gi * 128:(gi + 1) * 128], po.v, eng="act")
            for c, (d, hh) in enumerate(chains):
                t0, n, nch, bi = info[c]
                if t0 < TC and not with_ctx:
                    continue
                P.store(OFB[d, hh * 128:(hh + 1) * 128, t0:t0 + n], osb[c][bi][:, 0:n])
        P.end()

    def stage_hgrn_out(l, with_ctx):
        e = l // 2
        P.begin("hgout%d" % l)
        onesm = P.sb("onesm", [128, 128], F32)
        P.memset(onesm.v, 1.0 / 128)
        eps_t = P.sb("eps", [128, 1], F32)
        P.memset(eps_t.v, EPS)
        wn = P.sb("wn", [128, 2], F32)
        P.load(wn.v, wn_in)
        of = P.sbn("of", 2, [128, 512], F32)
        ob = P.sbn("ob", 2, [128, 512], F32)
        sg = P.sbn("sg", 2, [128, 512], BF16)
        sq = P.sbn("sq", 2, [128, 512], F32)
        rr = P.sbn("rr", 2, [128, 512], F32)
        yo = P.sbn("yo", 2, [128, 512], BF16)
        pM = P.psn("pM", 2, [128, 512])
        i = 0
        for hh in range(4):
            for (t0, n) in token_tiles():
                if t0 < TC and not with_ctx:
                    continue
                a, b, g, s, r, y, p = of[i % 2], ob[i % 2], sg[i % 2], sq[i % 2], rr[i % 2], yo[i % 2], pM[i % 2]
                i += 1
                P.load(a[:, 0:n], OFB[0, hh * 128:(hh + 1) * 128, t0:t0 + n])
                P.load(b[:, 0:n], OFB[1, hh * 128:(hh + 1) * 128, t0:t0 + n])
                P.load(g[:, 0:n], SG[hh * 128:(hh + 1) * 128, t0:t0 + n])
                P.tt(a[:, 0:n], a[:, 0:n], b[:, 0:n], ALU.add)
                P.act(s[:, 0:n], a[:, 0:n], AF.Square)
                P.mm(p[:, 0:n], onesm.v, s[:, 0:n])
                P.act(r[:, 0:n], p[:, 0:n], AF.Sqrt, bias=eps_t.v, scale=1.0)
                P.recip(r[:, 0:n], r[:, 0:n])
                P.tt(r[:, 0:n], a[:, 0:n], r[:, 0:n], ALU.mult)
                P.stt(y[:, 0:n], r[:, 0:n], wn[:, e:e + 1], g[:, 0:n], ALU.mult, ALU.mult)
                P.store(AOt[512 + hh * 128:512 + (hh + 1) * 128, t0:t0 + n], y[:, 0:n])
        P.end()

    def stage_natten(l, with_ctx):
        o_ = l // 2
        P.begin("nat%d" % l)
        KT = P.sbn("KT", 2, [128, TT], BF16)
        QT = P.sbn("QT", 2, [128, TT], BF16)
        for t in KT + QT:
            P.memset(t[64:128], 0.0, eng="pool")
        VA = P.sbn("VA", 2, [128, 34, 128], BF16)
        VB = P.sbn("VB", 2, [128, 32, 128], BF16)
        T2 = P.sbn("T2", 2, [128, 14, 64], F32)
        NS = 4
        pS = P.psn("pS", NS, [128, 512])
        pO = P.psn("pO", 2, [128, 512])
        sl = P.sbn("sl", NS, [128, 4, 64], F32)
        pt = P.sbn("pt", NS, [128, 6, 64], BF16)
        dsh = P.sbn("dsh", 2, [64, 512], F32)
        ot = P.sbn("ot", 2, [64, 512], BF16)
        items = []
        oi = 0
        for h in range(int(os.environ.get("KHEADS", "16"))):
            if with_ctx:
                for qh in range(2):
                    items.append((h, "c", qh, oi))
                oi += 1
            for r in range(64):
                items.append((h, "r", r, oi))
                if r % 8 == 7:
                    oi += 1
        LA = 2
        loaded = set()

        def bufs_of(h):
            return KT[h % 2], QT[h % 2], VA[h % 2], VB[h % 2], T2[h % 2]

        def load_head(hh):
            if hh < 16 and hh not in loaded:
                loaded.add(hh)
                kt_, qt_, va_, vb_, t2_ = bufs_of(hh)
                P.load(kt_[0:64, :], Kt[hh])
                P.load(qt_[0:64, :], Qt[hh])
                P.load(va_.v, Va[hh])
                P.load(vb_[:, 0:31, :], Vb[hh][:, 0:31, :])
                P.load(t2_.v, t2_in[o_][hh])

        for i in range(len(items) + LA):
            if i < len(items):
                h, kind, idx, o_i = items[i]
                kt, qt, va, vb, t2 = bufs_of(h)
                load_head(h)
                ps = pS[i % NS]
                if kind == "c":
                    q0 = idx * 128
                    for kc in range(2):
                        P.mm(ps[:, kc * 128:(kc + 1) * 128], kt[:, kc * 128:(kc + 1) * 128], qt[:, q0:q0 + 128])
                else:
                    r = idx
                    r0 = min(max(r - 4, 0), 56)
                    qv = qt[:, TC + r * 64:TC + (r + 1) * 64]
                    for jj in range(4):
                        k0 = TC + (r0 + 2 * jj) * 64
                        P.mm(ps[:, jj * 64:(jj + 1) * 64], kt[:, k0:k0 + 128], qv)
                    for jj in range(2):
                        P.mm(ps[:, (4 + jj) * 64:(5 + jj) * 64], kt[:, jj * 128:(jj + 1) * 128], qv)
            j = i - LA
            if j >= 0:
                h, kind, idx, o_i = items[j]
                load_head(h + 1)
                kt, qt, va, vb, t2 = bufs_of(h)
                ps, p_, s_ = pS[j % NS], pt[j % NS], sl[j % NS]
                po, d, o = pO[o_i % 2], dsh[o_i % 2], ot[o_i % 2]
                ptv = p_.v.re("p a q -> p (a q)")
                if kind == "c":
                    q0 = idx * 128
                    P.act(ptv[:, 0:256], ps[:, 0:256], AF.Exp)
                    for kc in range(2):
                        P.mm(po[:, q0:q0 + 128], va[:, kc, :], ptv[:, kc * 128:(kc + 1) * 128],
                             start=(kc == 0), stop=(kc == 1))
                    if idx == 1:
                        P.copy(d[:, 0:256], po[64:128, 0:256], eng="dve")
                        P.recip(d[:, 0:256], d[:, 0:256])
                        P.tt(o[:, 0:256], po[0:64, 0:256], d[:, 0:256], ALU.mult)
                        P.store(AOt[h * 64:(h + 1) * 64, 0:TC], o[:, 0:256])
                else:
                    r = idx
                    r0 = min(max(r - 4, 0), 56)
                    dy0 = r0 - r + 7
                    rr_ = r % 8
                    t2v = t2.v.re("p (a b) q -> p a b q", b=2)
                    psv = ps[:, 0:384].re("p (a q) -> p a q", q=64)
                    P.tt(s_.v, psv[:, 0:4, :], t2v[:, dy0 // 2:dy0 // 2 + 4, dy0 % 2, :], ALU.add)
                    P.act(p_[:, 0:4, :], s_.v, AF.Exp)
                    P.act(p_[:, 4:6, :], psv[:, 4:6, :], AF.Exp)
                    ocol = po[:, rr_ * 64:(rr_ + 1) * 64]
                    for jj in range(6):
                        if jj < 4:
                            row = r0 + 2 * jj
                            vv = va[:, 2 + row // 2, :] if row % 2 == 0 else vb[:, (row - 1) // 2, :]
                        else:
                            vv = va[:, jj - 4, :]
                        P.mm(ocol, vv, p_[:, jj, :], start=(jj == 0), stop=(jj == 5))
                    if rr_ == 7:
                        r8 = r // 8
                        P.copy(d.v, po[64:128, :], eng="dve")
                        P.recip(d.v, d.v)
                        P.tt(o.v, po[0:64, :], d.v, ALU.mult)
                        P.store(AOt[h * 64:(h + 1) * 64, TC + r8 * 512:TC + (r8 + 1) * 512], o.v)
        P.end()

    def stage_outproj(l, even, with_ctx):
        e = l // 2
        P.begin("oproj%d" % l)
        Wo = P.sb("Wo", [128, 8, DM], BF16)
        wsrc = ew_out[e] if even else ow_out[e]
        wstg = P.sbn("wstg", 2, [128, DM], F32)
        wcnt = [0]
        for k in range(8):
            load_cast(P, Wo[:, k, :], wsrc[k * 128:(k + 1) * 128, :], wstg, wcnt)
        ident = P.sb("ident", [128, 128], BF16)
        P.load(ident.v, ident_in)
        eps_t = P.sb("eps", [128, 1], F32)
        P.memset(eps_t.v, EPS)
        MV = {}
        for s in range(2):
            if s == 1 and not with_ctx:
                continue
            for wh in (2, 3, 4):
                t = P.sb("MV%d%d" % (s, wh), [128, DM], F32)
                load_bc(P, t.v, l, s, wh)
                MV[(s, wh)] = t
        ao = P.sbn("ao", 2, [128, 8, 512], BF16)
        xr = P.sbn("xr", 2, [128, DM], F32)
        xn = P.sbn("xn", 2, [128, DM], F32)
        scr = P.sb("scr", [128, DM], BF16)
        tmp = P.sb("tmp", [128, DM], F32)
        hb = P.sbn("hb", 2, [128, DM], BF16)
        ss = P.sbn("ss", 4, [128, 1], F32)
        rs = P.sbn("rs", 4, [128, 1], F32)
        hT = P.sbn("hT", 2, [128, 8, 512], BF16)
        pY = P.psn("pY", 2, [128, DM])
        pT = P.psn("pT", 2, [128, 8, 128], BF16)
        sub_i = 0
        pend = [None]
        for ti, (t0, TS) in enumerate(token_tiles()):
            is_ctx = t0 < TC
            if is_ctx and not with_ctx:
                continue
            s = 1 if is_ctx else 0
            a = ao[ti % 2]
            hTt = hT[ti % 2]
            P.load(a[:, :, 0:TS], AOt[:, t0:t0 + TS].rearrange("(k p) t -> p k t", p=128))
            for sb_ in range(TS // 128):
                r0 = t0 + sb_ * 128
                x = xr[sub_i % 2]
                xo = xn[sub_i % 2]
                h = hb[sub_i % 2]
                py = pY[sub_i % 2]
                P.load(x.v, R[r0:r0 + 128, :])
                for half in range(2):
                    for k in range(8):
                        P.mm(py[:, half * 512:(half + 1) * 512], a[:, k, sb_ * 128:(sb_ + 1) * 128],
                             Wo[:, k, half * 512:(half + 1) * 512], start=(k == 0), stop=(k == 7))
                if pend[0] is not None:
                    pend[0]()
                    pend[0] = None
                s1, r1 = ss[(2 * sub_i) % 4], rs[(2 * sub_i) % 4]
                P.act(scr.v, py.v, AF.Square, accum=s1.v)
                P.act(r1.v, s1.v, AF.Sqrt, bias=eps_t.v, scale=1.0 / DM)
                P.recip(r1.v, r1.v)
                P.stt(tmp.v, py.v, r1.v, MV[(s, 2)].v, ALU.mult, ALU.mult)
                P.tt(xo.v, tmp.v, x.v, ALU.add)
                P.store(R[r0:r0 + 128, :], xo.v)
                s2, r2 = ss[(2 * sub_i + 1) % 4], rs[(2 * sub_i + 1) % 4]
                norm_rows(P, xo.v, scr, s2, r2, MV[(s, 3)], MV[(s, 4)], h, tmp, eps_t)
                def fin(pt=pT[sub_i % 2], h=h, hTt=hTt, sb_=sb_, t0=t0, TS=TS, lastsub=(sb_ == TS // 128 - 1)):
                    for k in range(8):
                        P.tr(pt[:, k, :], h[:, k * 128:(k + 1) * 128], ident.v)
                    P.copy(hTt[:, :, sb_ * 128:(sb_ + 1) * 128], pt.v, eng="act")
                    if lastsub:
                        P.store(Ht[:, t0:t0 + TS].rearrange("(k p) t -> p k t", p=128), hTt[:, :, 0:TS])
                pend[0] = fin
                sub_i += 1
        if pend[0] is not None:
            pend[0]()
        P.end()

    def stage_ffn_up(l, with_ctx):
        P.begin("ffnup%d" % l)
        H = P.sb("H", [128, 8, TT], BF16)
        tiles = [t for t in token_tiles() if with_ctx or t[0] >= TC]
        for (t0, TS) in tiles:
            P.load(H[:, :, t0:t0 + TS], Ht[:, t0:t0 + TS].rearrange("(k p) t -> p k t", p=128))
        cw = P.sb("cw", [128, 44, 4], F32)
        P.load(cw.v, convw[l])
        NU = TT + 3
        U = [[P.sb("U%d_%d" % (b, part), [128, NU], F32) for part in range(2)] for b in range(2)]
        for ub in U:
            for u in ub:
                P.memset(u[:, 0:1], 0.0)
                P.memset(u[:, TC + 1:TC + 2], 0.0)
                P.memset(u[:, NU - 1:NU], 0.0)
        pieces = ([(0, TC)] if with_ctx else []) + [(TC + 1024 * i, 1024) for i in range(4)]
        acc = [[P.sb("acc%d_%d" % (part, pi), [128, qn_], F32) for pi, (q0, qn_) in enumerate(pieces)]
               for part in range(2)]
        mo = P.sbn("mo", 2, [128, TT], BF16)
        wa = P.sbn("wa", 4, [128, 8, 128], BF16)
        wstg = P.sbn("wstg", 2, [128, 1024], F32)
        wcnt = [0]
        pU = P.psn("pU", 6, [128, 512])

        def ucol(t):
            return t + 1 if t < TC else t + 2

        st = {"pi": 0, "wi": 0}

        def emit_mm(j):
            for part in range(2):
                c0 = part * DFF + j * 128
                w = wa[st["wi"] % 4]
                st["wi"] += 1
                load_cast(P, w.v, w_up[l][:, c0:c0 + 128].rearrange("(k p) n -> p k n", p=128), wstg, wcnt)
                u = U[j % 2][part]
                for (t0, TS) in tiles:
                    p = pU[st["pi"] % 6]
                    st["pi"] += 1
                    for k in range(8):
                        P.mm(p[:, 0:TS], w[:, k, :], H[:, k, t0:t0 + TS], start=(k == 0), stop=(k == 7))
                    P.copy(u[:, ucol(t0):ucol(t0) + TS], p[:, 0:TS], eng="act")

        def emit_conv(j):
            for part in range(2):
                cidx = part * 22 + j
                u = U[j % 2][part]
                for pi, (q0, qn_) in enumerate(pieces):
                    ac = acc[part][pi]
                    uc = ucol(q0)
                    P.ts(ac.v, u[:, uc:uc + qn_], cw[:, cidx, 1:2], cw[:, cidx, 3:4], ALU.mult, ALU.add, eng="pool")
                    P.stt(ac.v, u[:, uc - 1:uc - 1 + qn_], cw[:, cidx, 0:1], ac.v, ALU.mult, ALU.add)
                    P.stt(ac.v, u[:, uc + 1:uc + 1 + qn_], cw[:, cidx, 2:3], ac.v, ALU.mult, ALU.add)
            m = mo[j % 2]
            for pi, (q0, qn_) in enumerate(pieces):
                P.act(acc[1][pi].v, acc[1][pi].v, AF.Silu)
                P.tt(m[:, q0:q0 + qn_], acc[1][pi].v, acc[0][pi].v, ALU.mult)
            lo = 0 if with_ctx else TC
            P.store(Mt[j * 128:(j + 1) * 128, lo:TT], m[:, lo:TT])

        for j in range(23):
            if j < 22:
                emit_mm(j)
            if j >= 1:
                emit_conv(j - 1)
        P.end()

    def stage_ffn_down(l, with_ctx, last):
        P.begin("ffndn%d" % l)
        Wd = P.sb("Wd", [128, 22, DM], BF16)
        wstg = P.sbn("wstg", 2, [128, DM], F32)
        wcnt = [0]
        for k in range(22):
            load_cast(P, Wd[:, k, :], w_down[l][k * 128:(k + 1) * 128, :], wstg, wcnt)
        eps_t = P.sb("eps", [128, 1], F32)
        P.memset(eps_t.v, EPS)
        G2 = {}
        for s in range(2):
            if s == 1 and not with_ctx:
                continue
            t = P.sb("G2%d" % s, [128, DM], F32)
            load_bc(P, t.v, l, s, 5)
            G2[s] = t
        M = P.sbn("M", 2, [128, 22, 512], BF16)
        xr = P.sbn("xr", 2, [128, DM], F32)
        xn = P.sbn("xn", 2, [128, DM], F32)
        scr = P.sb("scr", [128, DM], BF16)
        tmp = P.sb("tmp", [128, DM], F32)
        ss = P.sbn("ss", 2, [128, 1], F32)
        rs = P.sbn("rs", 2, [128, 1], F32)
        pY = P.psn("pY", 2, [128, DM])
        sub_i = 0
        for ti, (t0, TS) in enumerate(token_tiles()):
            is_ctx = t0 < TC
            if is_ctx and not with_ctx:
                continue
            s = 1 if is_ctx else 0
            m = M[ti % 2]
            for (k, kn) in ((0, 6), (6, 6), (12, 5), (17, 5)):
                P.load(m[:, k:k + kn, 0:TS], Mt[k * 128:(k + kn) * 128, t0:t0 + TS].rearrange("(k p) t -> p k t", p=128))
            for sb_ in range(TS // 128):
                r0 = t0 + sb_ * 128
                x = xr[sub_i % 2]
                xo = xn[sub_i % 2]
                py = pY[sub_i % 2]
                P.load(x.v, R[r0:r0 + 128, :])
                for half in range(2):
                    for k in range(22):
                        P.mm(py[:, half * 512:(half + 1) * 512], m[:, k, sb_ * 128:(sb_ + 1) * 128],
                             Wd[:, k, half * 512:(half + 1) * 512], start=(k == 0), stop=(k == 21))
                s1, r1 = ss[sub_i % 2], rs[sub_i % 2]
                P.act(scr.v, py.v, AF.Square, accum=s1.v)
                P.act(r1.v, s1.v, AF.Sqrt, bias=eps_t.v, scale=1.0 / DM)
                P.recip(r1.v, r1.v)
                P.stt(tmp.v, py.v, r1.v, G2[s].v, ALU.mult, ALU.mult)
                P.tt(xo.v, tmp.v, x.v, ALU.add)
                if last:
                    P.store(y_out[r0 - TC:r0 - TC + 128, :], xo.v)
                else:
                    P.store(R[r0:r0 + 128, :], xo.v)
                sub_i += 1
        P.end()

    def assemble():
        stage_prologue()
        if done():
            return
        for l in layers:
            even = (l % 2 == 0)
            with_ctx = l < DEPTH - 1
            last = (l == DEPTH - 1)
            stage_proj(l, even)
            if done():
                return
            if even:
                stage_gqa(l, with_ctx)
                if done():
                    return
                stage_hgrn(l, with_ctx)
                if done():
                    return
                stage_hgrn_out(l, with_ctx)
                if done():
                    return
            else:
                stage_natten(l, with_ctx)
                if done():
                    return
            stage_outproj(l, even, with_ctx)
            if done():
                return
            stage_ffn_up(l, with_ctx)
            if done():
                return
            stage_ffn_down(l, with_ctx, last)
            if done():
                return

    assemble()
    P.finish()
    return nc


def _consts():
    ident = np.eye(128, dtype=np.float32).astype(ml_dtypes.bfloat16)
    t = np.arange(TL)
    row = (t // GRID_W).astype(np.float32)
    col = (t % GRID_W).astype(np.float32)
    inv = (np.float32(10000.0) ** (-np.arange(16, dtype=np.float32) / np.float32(16))).astype(np.float32)
    ang = np.concatenate([row[:, None] * inv, col[:, None] * inv], axis=-1).astype(np.float32)
    rope = np.concatenate([np.cos(ang), np.sin(ang)], axis=-1).astype(np.float32)
    s = np.arange(32)
    tri_f = (s[:, None] <= s[None, :]).astype(np.float32)
    tri_b = (s[:, None] >= s[None, :]).astype(np.float32)
    hmask = np.concatenate([tri_f, tri_b], axis=1)
    segm = np.ones((128, 512), np.float32)
    segm[:, ::32] = 0.0
    p = np.arange(128)
    same = (p[:, None] // 32) == (p[None, :] // 32)
    same16 = (p[:, None] // 16) == (p[None, :] // 16)
    dF = (same16 & (p[:, None] <= p[None, :])).astype(np.float32)
    dB = (same16 & (p[:, None] >= p[None, :])).astype(np.float32)
    sh0 = (p[:, None] % 32) < 16
    th1 = (p[None, :] % 32) >= 16
    oF = (same & sh0 & th1).astype(np.float32)
    oB = (same & (~sh0) & (~th1)).astype(np.float32)
    hmask4 = np.concatenate([dF, dB, oF, oB], axis=1)
    cmask = ((p[:, None] // 32) == np.arange(4)[None, :]).astype(np.float32)
    return ident, rope, hmask, segm, hmask4, cmask


def _t2_table(rpb):
    qc = np.arange(64)
    c0 = np.clip(qc - 8, 0, 48)
    kc = np.arange(64)
    inwin = (kc[:, None] >= c0[None, :]) & (kc[:, None] < c0[None, :] + 16)
    dx = np.clip(kc[:, None] - qc[None, :] + 15, 0, 30)
    m = np.arange(14)
    ee = np.arange(2)
    dy = m[None, :] + ee[:, None]
    g = rpb[:, :, dy[:, None, :, None], dx[None, :, None, :]]
    out = np.where(inwin[None, None, None, :, None, :], g, np.float32(NEG)).astype(np.float32)
    return np.ascontiguousarray(out.reshape(2, 16, 128, 14, 64))


def make_in_maps(inp, cores):
    ident, rope, hmask, segm, hmask4, cmask = _consts()
    f = lambda a: np.ascontiguousarray(np.asarray(a, dtype=np.float32))
    nrm = np.ascontiguousarray(np.stack([f(inp["norm_pre_mix"]), f(inp["norm_post_mix"]),
                                         f(inp["norm_pre_ffn"]), f(inp["norm_post_ffn"])], axis=1))
    qkn = np.ascontiguousarray(np.stack([f(inp["even_q_norm"]), f(inp["even_k_norm"])], axis=1))
    lg = f(inp["hgrn_lb_logits"]).reshape(2, 2, 4, 128)
    lbl = np.ascontiguousarray(lg.transpose(3, 0, 1, 2).reshape(128, 16))
    wn = np.ascontiguousarray(f(inp["hgrn_out_norm"]).T)
    t2 = _t2_table(f(inp["odd_rpb"]))
    cw = f(inp["ffn_conv_w"])
    cb = f(inp["ffn_conv_b"])
    cwb = np.concatenate([cw, cb[:, None, :]], axis=1)
    convw = np.ascontiguousarray(cwb.reshape(DEPTH, 4, 44, 128).transpose(0, 3, 2, 1))
    shared = {
        "b_mod": f(inp["b_mod"]), "nrm": nrm, "qkn": qkn, "lbl": lbl, "wn": wn, "convw": convw,
        "ident": ident, "rope": rope, "hmask": hmask, "segm": segm, "hmask4": hmask4, "cmask": cmask,
    }
    for l in range(DEPTH):
        shared["w_mod%d" % l] = f(inp["w_mod"][l])
        shared["ffn_w_up%d" % l] = f(inp["ffn_w_up"][l])
        shared["ffn_w_down%d" % l] = f(inp["ffn_w_down"][l])
    for e in range(2):
        shared["even_w_in%d" % e] = f(inp["even_w_in"][e])
        shared["even_w_out%d" % e] = f(inp["even_w_out"][e])
        shared["odd_w_qkv%d" % e] = f(inp["odd_w_qkv"][e])
        shared["odd_w_out%d" % e] = f(inp["odd_w_out"][e])
        shared["t2_%d" % e] = t2[e]
    x = f(inp["x"])
    ctx = f(inp["ctx"])
    c = f(inp["c"])
    cctx = f(inp["c_ctx"])
    maps = []
    for b in cores:
        cc = np.stack([c[b].reshape(8, 128).T, cctx.reshape(8, 128).T], axis=-1)
        m = dict(shared)
        m["x"] = x[b]
        m["ctx"] = ctx[b]
        m["cc"] = np.ascontiguousarray(cc)
        maps.append(m)
    return maps


def kernel(**inputs):
    nc = build()
    maps = make_in_maps(inputs, list(range(8)))
    res = run_bass_kernel_spmd(nc, maps, core_ids=list(range(8)))
    return np.stack([np.asarray(r["y"], dtype=np.float32) for r in res.results], axis=0)
```

```python
import contextlib
import math
import os


def CUT(n):
    return int(os.environ.get('KCUT', '-1')) == n
import numpy as np
import ml_dtypes
import concourse.bass as bass
import concourse.mybir as mybir
from concourse.bass_utils import run_bass_kernel_spmd

F32 = mybir.dt.float32
BF16 = mybir.dt.bfloat16
AF = mybir.ActivationFunctionType
ALU = mybir.AluOpType
AX = mybir.AxisListType

DM = 1024
TC = 256
TL = 4096
TT = TC + TL
DEPTH = 4
DFF = 2816
EPS = 1e-6
NEG = -30000.0
ATTN_SCALE = 0.125
HG_SCALE = 128 ** -0.5
GRID_W = 64
SAME_ENG_SYNC = os.environ.get("KSES", "1") == "1"

ENGS = ["pe", "act", "dve", "pool", "sp"]


class Sem:
    def __init__(self, h):
        self.h = h
        self.cnt = 0


class Tile:
    def __init__(self, h, name):
        self.h = h
        self.name = name
        self.w = None
        self.r = {}
        self.sem = None

    def __getitem__(self, idx):
        return View(self, self.h[idx])

    @property
    def v(self):
        return View(self, self.h[:])


class View:
    def __init__(self, t, ap):
        self.t = t
        self.ap = ap

    def __getitem__(self, idx):
        return View(self.t, self.ap[idx])

    def re(self, s, **kw):
        return View(self.t, self.ap.rearrange(s, **kw))

    def bc(self, shape):
        return View(self.t, self.ap.to_broadcast(list(shape)))

    def un(self, axis):
        return View(self.t, self.ap.unsqueeze(axis))


def _ap(x):
    return x.ap if isinstance(x, View) else x


class Prog:
    def __init__(self, nc):
        self.nc = nc
        self.gstack = contextlib.ExitStack()
        self.esem = {e: self.gstack.enter_context(nc.semaphore("es_" + e)) for e in ENGS}
        self.bsem = self.gstack.enter_context(nc.semaphore("bar"))
        self.dsems = [Sem(self.gstack.enter_context(nc.semaphore("ds%d" % i))) for i in range(56)]
        self.nstage = 0
        self.uid = 0
        self._reset()

    def _reset(self):
        self.ops = {e: [] for e in ENGS}
        self.seen = {e: {} for e in ENGS}
        self.dnext = 0
        self.used_dsems = []
        self.stack = None

    def eng(self, e):
        nc = self.nc
        return {"pe": nc.tensor, "act": nc.scalar, "dve": nc.vector, "pool": nc.gpsimd, "sp": nc.sync}[e]

    def begin(self, name):
        self.stack = contextlib.ExitStack()
        self.sname = name

    def sb(self, name, shape, dt):
        self.uid += 1
        h = self.stack.enter_context(self.nc.sbuf_tensor("%s_%d" % (name, self.uid), list(shape), dt))
        return Tile(h, name)

    def ps(self, name, shape, dt=F32):
        self.uid += 1
        h = self.stack.enter_context(self.nc.psum_tensor("%s_%d" % (name, self.uid), list(shape), dt))
        return Tile(h, name)

    def sbn(self, name, n, shape, dt):
        return [self.sb("%s%d" % (name, i), shape, dt) for i in range(n)]

    def psn(self, name, n, shape, dt=F32):
        return [self.ps("%s%d" % (name, i), shape, dt) for i in range(n)]

    def _waits(self, e, reads, writes):
        need = {}

        def add(ev):
            if ev is None:
                return
            k, v = ev
            if k == e and (e in ("pe", "sp") or not SAME_ENG_SYNC):
                return
            if need.get(k, 0) < v:
                need[k] = v

        for t in reads:
            add(t.w)
        for t in writes:
            add(t.w)
            for ev in t.r.items():
                add(ev)
        out = []
        for k, v in need.items():
            if self.seen[e].get(k, 0) >= v:
                continue
            self.seen[e][k] = v
            out.append((k, v))
        return out

    def op(self, e, fn, reads=(), writes=()):
        reads = [t for t in reads if t is not None]
        writes = [t for t in writes if t is not None]
        waits = self._waits(e, reads, writes)
        idx = len(self.ops[e]) + 1
        for t in reads:
            if t.r.get(e, 0) < idx:
                t.r[e] = idx
        for t in writes:
            t.w = (e, idx)
            t.r = {}
        self.ops[e].append([waits, fn, None])

    def _dsem(self, tile):
        if tile.sem is None or tile.sem[0] != self.nstage:
            assert self.dnext < len(self.dsems), "out of DMA semaphores in stage " + self.sname
            s = self.dsems[self.dnext]
            self.dnext += 1
            tile.sem = (self.nstage, s)
            self.used_dsems.append(s)
        return tile.sem[1]

    def dma(self, q, out, in_, tile, is_load, **kw):
        o, i = _ap(out), _ap(in_)
        s = self._dsem(tile)
        if is_load:
            waits = self._waits(q, [], [tile])
        else:
            waits = self._waits(q, [tile], [])
        s.cnt += 16
        ev = (s, s.cnt)
        if is_load:
            tile.w = ev
            tile.r = {}
        else:
            tile.r[s] = s.cnt
        self.ops[q].append([waits, lambda eng: eng.dma_start(out=o, in_=i, **kw), s])

    def load(self, view, src, q="sp", **kw):
        self.dma(q, view, src, view.t, True, **kw)

    def store(self, dst, view, q="sp", **kw):
        self.dma(q, dst, view, view.t, False, **kw)

    def end(self):
        nc = self.nc
        sig = {e: set() for e in ENGS}
        for e in ENGS:
            for waits, fn, ds in self.ops[e]:
                for k, v in waits:
                    if isinstance(k, str):
                        sig[k].add(v)
        last = {e: len(self.ops[e]) for e in ENGS if e != "sp"}
        for e, n in last.items():
            if n > 0:
                sig[e].add(n)
        rank = {e: {v: i + 1 for i, v in enumerate(sorted(sig[e]))} for e in ENGS}
        for e in ENGS:
            assert len(rank[e]) < 60000, (self.sname, e, len(rank[e]))
        for s in self.used_dsems:
            assert s.cnt < 60000, (self.sname, s.cnt)
        self.nstage += 1
        k = self.nstage
        esem, bsem = self.esem, self.bsem
        ops = self.ops
        used = list(self.used_dsems)

        def emit(e, eng):
            if k > 1:
                eng.wait_ge(bsem, k - 1)
            for idx, (waits, fn, ds) in enumerate(ops[e]):
                for kk, v in waits:
                    if isinstance(kk, str):
                        eng.wait_ge(esem[kk], rank[kk][v])
                    else:
                        eng.wait_ge(kk.h, v)
                ins = fn(eng)
                if ds is not None:
                    ins.then_inc(ds.h, 16)
                elif (idx + 1) in rank[e]:
                    ins.then_inc(esem[e], 1)
            if e == "sp":
                for e2, n in last.items():
                    if n > 0:
                        eng.wait_ge(esem[e2], rank[e2][n])
                for s in used:
                    eng.wait_ge(s.h, s.cnt)
                for e2 in ENGS:
                    eng.sem_clear(esem[e2])
                for s in used:
                    eng.sem_clear(s.h)
                eng.sem_inc(bsem, 1)

        with nc.Block() as blk:
            blk.sync(lambda eng: emit("sp", eng))
            blk.tensor(lambda eng: emit("pe", eng))
            blk.scalar(lambda eng: emit("act", eng))
            blk.vector(lambda eng: emit("dve", eng))
            blk.gpsimd(lambda eng: emit("pool", eng))
        for s in used:
            s.cnt = 0
        self.stack.close()
        self._reset()

    def finish(self):
        nc = self.nc
        k = self.nstage
        bsem = self.bsem
        with nc.Block() as blk:
            blk.sync(lambda eng: eng.wait_ge(bsem, k))
            blk.scalar(lambda eng: eng.wait_ge(bsem, k))
        self.gstack.close()

    def mm(self, out, lhsT, rhs, start=True, stop=True):
        o, l, r = out.ap, lhsT.ap, rhs.ap
        self.op("pe", lambda e: e.matmul(o, l, r, start=start, stop=stop), [lhsT.t, rhs.t], [out.t])

    def tr(self, out, in_, ident):
        o, i, d = out.ap, in_.ap, ident.ap
        self.op("pe", lambda e: e.transpose(o, i, d), [in_.t, ident.t], [out.t])

    def act(self, out, in_, func, bias=None, scale=None, accum=None, eng="act"):
        o, i = out.ap, in_.ap
        kw = {}
        rd = [in_.t]
        wr = [out.t]
        if bias is not None:
            kw["bias"] = _ap(bias)
            if isinstance(bias, View):
                rd.append(bias.t)
        if scale is not None:
            kw["scale"] = _ap(scale)
            if isinstance(scale, View):
                rd.append(scale.t)
        if accum is not None:
            kw["accum_out"] = accum.ap
            wr.append(accum.t)
        self.op("act", lambda e: e.activation(out=o, in_=i, func=func, **kw), rd, wr)

    def tt(self, out, in0, in1, op, eng="dve"):
        o, a, b = out.ap, in0.ap, in1.ap
        self.op(eng, lambda e: e.tensor_tensor(out=o, in0=a, in1=b, op=op), [in0.t, in1.t], [out.t])

    def ts(self, out, in0, s1, s2, op0, op1=None, eng="dve"):
        o, a = out.ap, in0.ap
        rd = [in0.t]
        for s in (s1, s2):
            if isinstance(s, View):
                rd.append(s.t)
        a1, a2 = _ap(s1), _ap(s2)
        if op1 is None:
            self.op(eng, lambda e: e.tensor_scalar(out=o, in0=a, scalar1=a1, scalar2=None, op0=op0), rd, [out.t])
        else:
            self.op(eng, lambda e: e.tensor_scalar(out=o, in0=a, scalar1=a1, scalar2=a2, op0=op0, op1=op1), rd, [out.t])

    def stt(self, out, in0, scalar, in1, op0, op1):
        o, a, b = out.ap, in0.ap, in1.ap
        rd = [in0.t, in1.t]
        if isinstance(scalar, View):
            rd.append(scalar.t)
        sc = _ap(scalar)
        self.op("dve", lambda e: e.scalar_tensor_tensor(out=o, in0=a, scalar=sc, in1=b, op0=op0, op1=op1), rd, [out.t])

    def copy(self, out, in_, eng="dve"):
        o, i = out.ap, in_.ap
        if eng == "act":
            self.op("act", lambda e: e.copy(out=o, in_=i), [in_.t], [out.t])
        else:
            self.op(eng, lambda e: e.tensor_copy(out=o, in_=i), [in_.t], [out.t])

    def recip(self, out, in_):
        o, i = out.ap, in_.ap
        self.op("dve", lambda e: e.reciprocal(out=o, in_=i), [in_.t], [out.t])

    def reduce(self, out, in_, op=ALU.add, axis=AX.X):
        o, i = out.ap, in_.ap
        self.op("dve", lambda e: e.tensor_reduce(out=o, in_=i, axis=axis, op=op), [in_.t], [out.t])

    def scan(self, out, d0, d1, initial, op0, op1):
        o, a, b = out.ap, d0.ap, d1.ap
        self.op("dve", lambda e: e.tensor_tensor_scan(out=o, data0=a, data1=b, initial=initial, op0=op0, op1=op1),
                [d0.t, d1.t], [out.t])

    def memset(self, view, val, eng="dve"):
        o = view.ap
        self.op(eng, lambda e: e.memset(o, val), [], [view.t])


def token_tiles():
    return [(0, TC)] + [(TC + 512 * i, 512) for i in range(TL // 512)]


def build(debug=(), layers=tuple(range(DEPTH)), stop_after=None, only_inputs=None):
    nc = bass.Bass("TRN2", target_bir_lowering=False)

    def din(name, shape, dt=F32):
        kind = "ExternalInput" if (only_inputs is None or name in only_inputs) else "Internal"
        return nc.dram_tensor(name, list(shape), dt, kind=kind).ap()

    def dscr(name, shape, dt=F32):
        kind = "ExternalOutput" if name in debug else "Internal"
        return nc.dram_tensor(name, list(shape), dt, kind=kind).ap()

    x_in = din("x", [TL, DM])
    ctx_in = din("ctx", [TC, DM])
    cc_in = din("cc", [128, 8, 2])
    w_mod = [din("w_mod%d" % l, [DM, 6 * DM]) for l in range(DEPTH)]
    b_mod = din("b_mod", [DEPTH, 6 * DM])
    nrm = din("nrm", [DEPTH, 4, DM])
    ew_in = [din("even_w_in%d" % e, [DM, 3328]) for e in range(2)]
    ew_out = [din("even_w_out%d" % e, [DM, DM]) for e in range(2)]
    qkn = din("qkn", [2, 2, 64])
    lbl = din("lbl", [128, 16])
    wn_in = din("wn", [128, 2])
    ow_qkv = [din("odd_w_qkv%d" % e, [DM, 3072]) for e in range(2)]
    ow_out = [din("odd_w_out%d" % e, [DM, DM]) for e in range(2)]
    t2_in = [din("t2_%d" % e, [16, 128, 14, 64]) for e in range(2)]
    w_up = [din("ffn_w_up%d" % l, [DM, 2 * DFF]) for l in range(DEPTH)]
    convw = din("convw", [DEPTH, 128, 44, 4])
    w_down = [din("ffn_w_down%d" % l, [DFF, DM]) for l in range(DEPTH)]
    ident_in = din("ident", [128, 128], BF16)
    rope_in = din("rope", [TL, 64])
    hmask_in = din("hmask", [32, 64])
    segm_in = din("segm", [128, 512])
    hmask4_in = din("hmask4", [128, 512])
    cmask_in = din("cmask", [128, 4])
    y_out = nc.dram_tensor("y", [TL, DM], F32, kind="ExternalOutput").ap()

    R = dscr("R", [TT, DM])
    modv = dscr("modv", [DEPTH, 2, 6, DM])
    Ht = dscr("Ht", [DM, TT], BF16)
    Mt = dscr("Mt", [DFF, TT], BF16)
    AOt = dscr("AOt", [DM, TT], BF16)
    Qt = dscr("Qt", [16, 64, TT], BF16)
    Kt = dscr("Kt", [16, 64, TT], BF16)
    Va = dscr("Va", [16, 128, 34, 128], BF16)
    Vb = dscr("Vb", [16, 128, 32, 128], BF16)
    HQ = dscr("HQ", [512, TT])
    ZF = dscr("ZF", [2, 512, TT])
    HV = dscr("HV", [TT, 512], BF16)
    SG = dscr("SG", [512, TT], BF16)
    OFB = dscr("OFB", [2, 512, TT])

    P = Prog(nc)
    nstages = [0]

    def done():
        nstages[0] += 1
        return stop_after is not None and nstages[0] >= stop_after

    def stage_prologue():
        P.begin("prologue")
        dummy = P.sb("dummy", [1, 8], F32)
        P.dma("sp", R[0:TC, :], ctx_in[:, :], dummy, True)
        for i in range(4):
            P.dma("sp", R[TC + i * 1024:TC + (i + 1) * 1024, :], x_in[i * 1024:(i + 1) * 1024, :], dummy, True)
        cc = P.sb("cc", [128, 8, 2], F32)
        sc = P.sb("sc", [128, 8, 2], F32)
        P.load(cc.v, cc_in)
        P.act(sc.v, cc.v, AF.Silu)
        wm = P.sbn("wm", 3, [128, 8, 512], F32)
        pm = P.psn("pm", 2, [2, 512])
        mv = P.sbn("mv", 1, [2, 6, DM], F32)
        bm = P.sbn("bm", 1, [2, 6, DM], F32)
        nr = P.sbn("nr", 1, [2, 4, DM], F32)
        mo = P.sbn("mo", 1, [2, 6, DM], F32)
        it = 0
        for l in layers:
            mvl, bml, nrl, mol = mv[0], bm[0], nr[0], mo[0]
            P.load(bml.v, b_mod[l:l + 1, :].rearrange("o (s d) -> o s d", s=6).partition_broadcast(2))
            P.load(nrl.v, nrm[l:l + 1].partition_broadcast(2))
            for n in range(12):
                w = wm[it % 3]
                p = pm[it % 2]
                it += 1
                P.load(w.v, w_mod[l][:, n * 512:(n + 1) * 512].rearrange("(k p) n -> p k n", p=128))
                for k in range(8):
                    P.mm(p.v, sc[:, k, :], w[:, k, :], start=(k == 0), stop=(k == 7))
                s6, off = n // 2, (n % 2) * 512
                P.tt(mvl[:, s6, off:off + 512], p.v, bml[:, s6, off:off + 512], ALU.add)
            P.stt(mol[:, 0, :], mvl[:, 1, :], 1.0, nrl[:, 0, :], ALU.add, ALU.mult)
            P.copy(mol[:, 1, :], mvl[:, 0, :])
            P.tt(mol[:, 2, :], mvl[:, 2, :], nrl[:, 1, :], ALU.mult)
            P.stt(mol[:, 3, :], mvl[:, 4, :], 1.0, nrl[:, 2, :], ALU.add, ALU.mult)
            P.copy(mol[:, 4, :], mvl[:, 3, :])
            P.tt(mol[:, 5, :], mvl[:, 5, :], nrl[:, 3, :], ALU.mult)
            P.store(modv[l], mol.v)
        P.end()

    def load_cast(P, dst, src, stg, cnt):
        st = stg[cnt[0] % len(stg)]
        cnt[0] += 1
        shp = list(dst.ap.shape)
        if len(shp) == 2:
            sv = st[0:shp[0], 0:shp[1]]
        else:
            sv = st[0:shp[0], 0:shp[1] * shp[2]].re("p (a b) -> p a b", b=shp[2])
        P.load(sv, src)
        P.copy(dst, sv, eng="pool")

    def load_bc(P, tile_view, l, s, which):
        P.load(tile_view, modv[l, s, which:which + 1, :].partition_broadcast(128))

    def norm_rows(P, xv, ssq_scr, ss, rs, A, B, hb, tmp, eps_t):
        P.act(ssq_scr.v, xv, AF.Square, accum=ss.v)
        P.act(rs.v, ss.v, AF.Sqrt, bias=eps_t.v, scale=1.0 / DM)
        P.recip(rs.v, rs.v)
        P.stt(tmp.v, xv, rs.v, A.v, ALU.mult, ALU.mult)
        P.tt(hb.v, tmp.v, B.v, ALU.add)

    def stage_proj(l, even):
        e = l // 2
        P.begin("proj%d" % l)
        NW = 3328 if even else 3072
        wsrc = ew_in[e] if even else ow_qkv[e]
        W = P.sb("W", [128, 8, NW], BF16)
        SK = os.environ.get("KSKIP", "")
        wstg = P.sbn("wstg", 2, [128, 1664], F32)
        wcnt = [0]
        for k in range(8):
            for c0 in range(0, NW, 1664):
                cn = min(1664, NW - c0)
                load_cast(P, W[:, k, c0:c0 + cn], wsrc[k * 128:(k + 1) * 128, c0:c0 + cn], wstg, wcnt)
        ident = P.sb("ident", [128, 128], BF16)
        P.load(ident.v, ident_in)
        eps_t = P.sb("eps", [128, 1], F32)
        P.memset(eps_t.v, EPS)
        AB = {}
        for s in range(2):
            for wh in (0, 1):
                t = P.sb("AB%d%d" % (s, wh), [128, DM], F32)
                if "b" not in SK:
                    load_bc(P, t.v, l, s, wh)
                AB[(s, wh)] = t
        xr = P.sbn("xr", 2, [128, DM], F32)
        scr = P.sb("scr", [128, DM], BF16)
        tmp = P.sb("tmp", [128, DM], F32)
        hb = P.sbn("hb", 2, [128, DM], BF16)
        ss = P.sbn("ss", 2, [128, 1], F32)
        rs = P.sbn("rs", 2, [128, 1], F32)
        hT = P.sbn("hT", 2, [128, 8, 512], BF16)
        pT = P.psn("pT", 1 if even else 2, [128, 8, 128], BF16)
        pF = P.psn("pF", 2, [128, 512])
        pK = P.psn("pK", 2, [128, 512])
        pendB = [None]
        if even:
            pHV = P.psn("pHV", 1, [128, 512])
            wqk = P.sb("wqk", [128, 2, 64], F32)
            P.load(wqk.v, qkn[e:e + 1].partition_broadcast(128))
            rope = P.sbn("rope", 2, [128, 64], F32)
            sq = P.sb("sq", [128, 512], F32)
            s8 = P.sb("s8", [128, 8], F32)
            qn = P.sb("qn", [128, 8, 64], F32)
            t1 = P.sb("t1", [128, 8, 32], F32)
            t2 = P.sb("t2", [128, 8, 32], F32)
            qr = P.sbn("qr", 2, [128, 10, 64], BF16)
            pQ = P.psn("pQ", 1, [64, 10, 128], BF16)
            qT = P.sbn("qT", 2, [64, 10, 512], BF16)
            va = P.sbn("va", 2, [128, 2, 128], BF16)
            for t in va:
                P.memset(t.v, 1.0)
            hv = P.sbn("hv", 2, [128, 512], BF16)
            fo = P.sbn("fo", 3, [128, 512], F32)
            fob = P.sbn("fob", 2, [128, 512], BF16)
        else:
            qT = P.sbn("qT", 2, [64, 16, 512], BF16)
            kT = P.sbn("kT", 2, [64, 16, 512], BF16)
            va = P.sbn("va", 2, [128, 16, 128], BF16)
            for t in va:
                P.memset(t.v, 1.0)
        ci = [0]

        def nxt(lst, key):
            return lst[key % len(lst)]
        if CUT(1):
            P.end()
            return

        sub_i = 0
        f_i = 0
        for ti, (t0, TS) in enumerate(token_tiles()):
            is_ctx = t0 < TC
            s = 1 if is_ctx else 0
            hTt = hT[ti % 2]
            qTt = qT[ti % 2]
            if not even:
                kTt = kT[ti % 2]
            nsub = TS // 128
            for sb_ in range(nsub):
                r0 = t0 + sb_ * 128
                x = xr[sub_i % 2]
                h = hb[sub_i % 2]
                P.load(x.v, R[r0:r0 + 128, :])
                norm_rows(P, x.v, scr, ss[sub_i % 2], rs[sub_i % 2], AB[(s, 0)], AB[(s, 1)], h, tmp, eps_t)
                if CUT(2):
                    P.end()
                    return
                pt = pT[sub_i % len(pT)]
                for k in range(8):
                    P.tr(pt[:, k, :], h[:, k * 128:(k + 1) * 128], ident.v)
                P.copy(hTt[:, :, sb_ * 128:(sb_ + 1) * 128], pt.v, eng="act")
                if CUT(3):
                    P.end()
                    return
                lh = lambda k: hTt[:, k, sb_ * 128:(sb_ + 1) * 128]
                if even:
                    pq = pF[sub_i % 2]
                    pk = pK[sub_i % 2]
                    for k in range(8):
                        P.mm(pq.v, lh(k), W[:, k, 0:512], start=(k == 0), stop=(k == 7))
                    for k in range(8):
                        P.mm(pk[:, 0:256], lh(k), W[:, k, 512:768], start=(k == 0), stop=(k == 7))
                    rp = rope[sub_i % 2]
                    if not is_ctx:
                        P.load(rp.v, rope_in[r0 - TC:r0 - TC + 128, :])
                    vt = va[sub_i % 2]
                    P.copy(vt[:, :, 0:64], pk[:, 128:256].re("p (g d) -> p g d", d=64))
                    P.store(Va[0:2, :, r0 // 128, :].rearrange("g p n -> p g n"), vt.v)
                    pq2 = pHV[0]
                    for k in range(8):
                        P.mm(pq2.v, lh(k), W[:, k, 2304:2816], start=(k == 0), stop=(k == 7))
                    hvt = hv[sub_i % 2]
                    P.copy(hvt.v, pq2.v, eng="act")
                    P.store(HV[r0:r0 + 128, :], hvt.v)
                    if pendB[0] is not None:
                        pendB[0]()
                        pendB[0] = None

                    def stageB(pq=pq, pk=pk, q_r=qr[sub_i % 2], rp=rp, is_ctx=is_ctx, qTt=qTt, sb_=sb_):
                        for (src, nh, wi, dst0) in ((pq.v, 8, 0, 0), (pk[:, 0:128], 2, 1, 8)):
                            n = nh * 64
                            P.act(sq[:, 0:n], src, AF.Square)
                            P.reduce(s8[:, 0:nh], sq[:, 0:n].re("p (h d) -> p h d", d=64))
                            P.act(s8[:, 0:nh], s8[:, 0:nh], AF.Sqrt, bias=eps_t.v, scale=1.0 / 64)
                            P.recip(s8[:, 0:nh], s8[:, 0:nh])
                            qv = qn[:, 0:nh, :]
                            P.tt(qv, src.re("p (h d) -> p h d", d=64), s8[:, 0:nh].un(2).bc([128, nh, 64]), ALU.mult)
                            dst = q_r[:, dst0:dst0 + nh, :]
                            wv = wqk[:, wi, :].un(1).bc([128, nh, 64])
                            if is_ctx:
                                P.tt(dst, qv, wv, ALU.mult)
                            else:
                                P.tt(qv, qv, wv, ALU.mult)
                                cs = rp[:, 0:32].un(1).bc([128, nh, 32])
                                sn = rp[:, 32:64].un(1).bc([128, nh, 32])
                                x1, x2 = qv[:, :, 0:32], qv[:, :, 32:64]
                                a1, a2 = t1[:, 0:nh, :], t2[:, 0:nh, :]
                                P.tt(a1, x1, cs, ALU.mult)
                                P.tt(a2, x2, sn, ALU.mult)
                                P.tt(dst[:, :, 0:32], a1, a2, ALU.subtract)
                                P.tt(a1, x1, sn, ALU.mult)
                                P.tt(a2, x2, cs, ALU.mult)
                                P.tt(dst[:, :, 32:64], a1, a2, ALU.add)
                        pqt = pQ[0]
                        for hh in range(10):
                            P.tr(pqt[:, hh, :], q_r[:, hh, :], ident.v)
                        P.copy(qTt[:, :, sb_ * 128:(sb_ + 1) * 128], pqt.v, eng="act")
                    pendB[0] = stageB
                else:
                    vt = va[sub_i % 2]
                    for half in range(2):
                        pq = (pF if half == 0 else pK)[sub_i % 2]
                        c0 = 2048 + half * 512
                        for k in range(8):
                            P.mm(pq.v, lh(k), W[:, k, c0:c0 + 512], start=(k == 0), stop=(k == 7))
                        P.copy(vt[:, half * 8:(half + 1) * 8, 0:64], pq.v.re("p (g d) -> p g d", d=64),
                               eng=("act" if half == 0 else "dve"))
                    for hf in range(2):
                        P.store(Va[hf * 8:(hf + 1) * 8, :, r0 // 128, :].rearrange("g p n -> p g n"), vt[:, hf * 8:(hf + 1) * 8, :])
                    if not is_ctx:
                        cb = (r0 - TC) // 128
                        for hf in range(2):
                            if cb <= 30:
                                P.store(Vb[hf * 8:(hf + 1) * 8, 0:64, cb, :].rearrange("g p n -> p g n"), vt[64:128, hf * 8:(hf + 1) * 8, :])
                            if cb >= 1:
                                P.store(Vb[hf * 8:(hf + 1) * 8, 64:128, cb - 1, :].rearrange("g p n -> p g n"), vt[0:64, hf * 8:(hf + 1) * 8, :])
                sub_i += 1
            if even:
                if pendB[0] is not None:
                    pendB[0]()
                    pendB[0] = None
                P.store(Qt[0:8, :, t0:t0 + TS].rearrange("h d t -> d h t"), qTt[:, 0:8, 0:TS])
                P.store(Kt[0:2, :, t0:t0 + TS].rearrange("h d t -> d h t"), qTt[:, 8:10, 0:TS])
                groups = [(768, "q"), (1280, "f"), (1792, "b"), (2816, "g")]
                for (c0, kind) in groups:
                    for j in range(4):
                        p = pK[f_i % 2]
                        for k in range(8):
                            P.mm(p[:, 0:TS], W[:, k, c0 + j * 128:c0 + (j + 1) * 128], hTt[:, k, 0:TS],
                                 start=(k == 0), stop=(k == 7))
                        if kind == "g":
                            o = fob[f_i % 2]
                            P.act(o[:, 0:TS], p[:, 0:TS], AF.Silu)
                            P.store(SG[j * 128:(j + 1) * 128, t0:t0 + TS], o[:, 0:TS])
                        else:
                            o = fo[f_i % 3]
                            if kind == "q":
                                P.act(o[:, 0:TS], p[:, 0:TS], AF.Silu)
                                P.store(HQ[j * 128:(j + 1) * 128, t0:t0 + TS], o[:, 0:TS])
                            else:
                                P.copy(o[:, 0:TS], p[:, 0:TS], eng="dve")
                                d = 0 if kind == "f" else 1
                                P.store(ZF[d, j * 128:(j + 1) * 128, t0:t0 + TS], o[:, 0:TS])
                        f_i += 1
                if CUT(8):
                    P.end()
                    return
            else:
                for which, dstT in ((0, qTt), (1, kTt)):
                    for hh in range(16):
                        p = pK[f_i % 2] if (f_i % 4) < 2 else pF[f_i % 2]
                        c0 = which * 1024 + hh * 64
                        for k in range(8):
                            P.mm(p[0:64, 0:TS], W[:, k, c0:c0 + 64], hTt[:, k, 0:TS], start=(k == 0), stop=(k == 7))
                        if which == 0:
                            P.act(dstT[:, hh, 0:TS], p[0:64, 0:TS], AF.Identity, scale=ATTN_SCALE)
                        else:
                            P.copy(dstT[:, hh, 0:TS], p[0:64, 0:TS], eng="dve")
                        f_i += 1
                P.store(Qt[:, :, t0:t0 + TS].rearrange("h d t -> d h t"), qTt[:, :, 0:TS])
                P.store(Kt[:, :, t0:t0 + TS].rearrange("h d t -> d h t"), kTt[:, :, 0:TS])
        P.end()

    def stage_gqa(l, with_ctx):
        P.begin("gqa%d" % l)
        KT = P.sb("KT", [128, 2, TT], BF16)
        P.memset(KT[64:128], 0.0, eng="pool")
        P.load(KT[0:64], Kt[0:2].rearrange("g d t -> d g t"))
        VA = P.sb("VA", [128, 2, 34, 128], BF16)
        for g in range(2):
            P.load(VA[:, g], Va[g])
        QT = P.sbn("QT", 2, [128, TT], BF16)
        for q in QT:
            P.memset(q[64:128], 0.0, eng="pool")
        NS = 4
        pS = P.psn("pS", NS, [128, 512])
        pO = P.psn("pO", 2, [128, 512])
        Pt = P.sbn("Pt", NS, [128, 512], BF16)
        dsh = P.sbn("dsh", 2, [64, 512], F32)
        ot = P.sbn("ot", 2, [64, 512], BF16)
        items = []
        oi = 0
        for h in range(8):
            tiles = [(TC + 512 * i, 512, 34) for i in range(8)]
            if with_ctx:
                tiles = [(0, TC, 2)] + tiles
            for (t0, n, nk) in tiles:
                for kc in range(nk):
                    items.append((h, t0, n, nk, kc, oi))
                oi += 1
        LA = 2
        loaded = set()
        for i in range(len(items) + LA):
            if i < len(items):
                h, t0, n, nk, kc, o_i = items[i]
                q = QT[h % 2]
                for hh in (h, h + 1):
                    if hh < 8 and hh not in loaded:
                        loaded.add(hh)
                        P.load(QT[hh % 2][0:64, :], Qt[hh])
                P.mm(pS[i % NS][:, 0:n], KT[:, h // 4, kc * 128:(kc + 1) * 128], q[:, t0:t0 + n])
            j = i - LA
            if j >= 0:
                h, t0, n, nk, kc, o_i = items[j]
                ps, pt, po = pS[j % NS], Pt[j % NS], pO[o_i % 2]
                P.act(pt[:, 0:n], ps[:, 0:n], AF.Exp, scale=ATTN_SCALE)
                P.mm(po[:, 0:n], VA[:, h // 4, kc, :], pt[:, 0:n], start=(kc == 0), stop=(kc == nk - 1))
                if kc == nk - 1:
                    d = dsh[o_i % 2]
                    o = ot[o_i % 2]
                    P.copy(d[:, 0:n], po[64:128, 0:n], eng="dve")
                    P.recip(d[:, 0:n], d[:, 0:n])
                    P.tt(o[:, 0:n], po[0:64, 0:n], d[:, 0:n], ALU.mult)
                    P.store(AOt[h * 64:(h + 1) * 64, t0:t0 + n], o[:, 0:n])
        P.end()

    def stage_hgrn(l, with_ctx):
        e = l // 2
        P.begin("hgrn%d" % l)
        ident = P.sb("ident", [128, 128], BF16)
        P.load(ident.v, ident_in)
        hm4 = P.sb("hm4", [128, 512], F32)
        P.load(hm4.v, hmask4_in)
        cm = P.sb("cm", [128, 4], F32)
        P.load(cm.v, cmask_in)
        segm = P.sb("segm", [128, 512], F32)
        P.load(segm.v, segm_in)
        lns = P.sb("lns", [128, 1], F32)
        P.memset(lns.v, math.log(HG_SCALE))
        lb_raw = P.sb("lbraw", [128, 16], F32)
        P.load(lb_raw.v, lbl)
        lb = P.sb("lb", [128, 8], F32)
        oml = P.sb("oml", [128, 8], F32)
        if e == 0:
            P.memset(lb.v, 0.0)
        else:
            P.tt(lb.v, lb_raw[:, 0:8], lb_raw[:, 8:16], ALU.subtract)
            P.act(lb.v, lb.v, AF.Exp)
            P.ts(lb.v, lb.v, 1.0, None, ALU.add)
            P.recip(lb.v, lb.v)
        P.ts(oml.v, lb.v, -1.0, 1.0, ALU.mult, ALU.add)
        chains = [(d, hh) for hh in range(4) for d in range(2)]
        NCH = len(chains)
        S = [P.sb("S%d" % c, [128, 128], F32) for c in range(NCH)]
        for s_ in S:
            P.memset(s_.v, 0.0)
        Smid = P.sbn("Smid", 6, [128, 128], BF16)

        def bufs(name, shape, dt, n=2):
            return [[P.sb("%s%d_%d" % (name, c, i), shape, dt) for i in range(n)] for c in range(NCH)]

        def shared(name):
            two = [P.sb("%s_%d" % (name, i), [128, 512], F32) for i in range(2)]
            return [[two[c % 2]] for c in range(NCH)]
        zt = shared("z")
        qf = shared("qf")
        ft = shared("f")
        lf = shared("lf")
        bt = shared("b")
        at = shared("a")
        qA_ = bufs("qA", [128, 512], BF16)
        kA_ = bufs("kA", [128, 512], BF16)
        qO_ = bufs("qO", [128, 512], BF16)
        kO_ = bufs("kO", [128, 512], BF16)
        qI_ = bufs("qI", [128, 512], BF16)
        kS_ = bufs("kS", [128, 512], BF16)
        vt_ = bufs("v", [128, 4, 128], BF16)
        el = bufs("el", [128, 16], F32)
        osb = bufs("o", [128, 512], F32)
        pAA = P.psn("pAA", 1, [128, 256])
        pKt = P.psn("pKt", 1, [128, 128], BF16)
        pO = P.psn("pO", 4, [128, 128])
        pD = P.psn("pD", 2, [128, 128])
        AtD = P.sbn("AtD", 3, [128, 128], BF16)
        AtO = P.sbn("AtO", 3, [128, 128], BF16)
        ktok = P.sbn("ktok", 16, [128, 128], BF16)
        sbs = [(0, TC)] + [(TC + 512 * i, 512) for i in range(8)]
        cnt = {"a": 0, "k": 0, "s": 0, "d": 0}
        for step in range(9):
            info = []
            for c, (d, hh) in enumerate(chains):
                if d == 0:
                    sbi = step
                else:
                    sbi = 0 if step == 0 else 9 - step
                t0, n = sbs[sbi]
                nch = n // 32
                n16 = n // 16
                bi = step % 2
                z, q, f, lff, b, a = zt[c][0], qf[c][0], ft[c][0], lf[c][0], bt[c][0], at[c][0]
                v = vt_[c][bi]
                elc = el[c][bi]
                info.append((t0, n, nch, bi))
                N = slice(0, n)
                P.load(z[:, N], ZF[d, hh * 128:(hh + 1) * 128, t0:t0 + n])
                P.load(q[:, N], HQ[hh * 128:(hh + 1) * 128, t0:t0 + n])
                P.load(v[:, 0:n // 128, :], HV[t0:t0 + n, hh * 128:(hh + 1) * 128].rearrange("(g p) f -> p g f", p=128))
                lbc = lb[:, d * 4 + hh:d * 4 + hh + 1]
                omc = oml[:, d * 4 + hh:d * 4 + hh + 1]
                P.act(f[:, N], z[:, N], AF.Exp, scale=-1.0)
                P.act(f[:, N], f[:, N], AF.Ln, bias=1.0)
                P.act(f[:, N], f[:, N], AF.Exp, scale=-1.0)
                P.ts(f[:, N], f[:, N], omc, lbc, ALU.mult, ALU.add)
                P.act(lff[:, N], f[:, N], AF.Ln)
                P.ts(z[:, N], f[:, N], -1.0, 1.0, ALU.mult, ALU.add)
                P.scan(b[:, N], segm[:, N], lff[:, N], 0.0, ALU.mult, ALU.add)
                b3 = b[:, N].re("p (c t) -> p c t", t=32)
                a3 = a[:, N].re("p (c t) -> p c t", t=32)
                b16 = b[:, N].re("p (c t) -> p c t", t=16)
                a16 = a[:, N].re("p (c t) -> p c t", t=16)
                if d == 0:
                    last, r16, mo_ = 31, 7, 15
                else:
                    l3 = lff[:, N].re("p (c t) -> p c t", t=32)
                    P.tt(a3, b3[:, :, 31:32].bc([128, nch, 32]), b3, ALU.subtract)
                    P.tt(b3, a3, l3, ALU.add)
                    last, r16, mo_ = 0, 8, 16
                P.act(elc[:, 0:nch], b3[:, :, last], AF.Exp)
                P.tt(a16, b16, b16[:, :, r16:r16 + 1].bc([128, n16, 16]), ALU.subtract)
                P.act(f[:, N], a[:, N], AF.Exp, bias=lns.v)
                P.act(lff[:, N], a[:, N], AF.Exp, scale=-1.0)
                P.tt(qA_[c][bi][:, N], q[:, N], f[:, N], ALU.mult, eng="pool")
                P.tt(kA_[c][bi][:, N], z[:, N], lff[:, N], ALU.mult, eng="pool")
                P.tt(a3, b3, b3[:, :, mo_:mo_ + 1].bc([128, nch, 32]), ALU.subtract)
                P.ts(f[:, N], a[:, N], 0.0, None, ALU.min)
                P.act(f[:, N], f[:, N], AF.Exp, bias=lns.v)
                P.tt(qO_[c][bi][:, N], q[:, N], f[:, N], ALU.mult, eng="pool")
                P.ts(lff[:, N], a[:, N], -1.0, 0.0, ALU.mult, ALU.min)
                P.act(lff[:, N], lff[:, N], AF.Exp)
                P.tt(kO_[c][bi][:, N], z[:, N], lff[:, N], ALU.mult, eng="pool")
                P.act(f[:, N], b[:, N], AF.Exp, bias=lns.v)
                P.tt(qI_[c][bi][:, N], q[:, N], f[:, N], ALU.mult, eng="pool")
                P.tt(a3, b3[:, :, last:last + 1].bc([128, nch, 32]), b3, ALU.subtract)
                P.act(lff[:, N], a[:, N], AF.Exp)
                P.tt(kS_[c][bi][:, N], z[:, N], lff[:, N], ALU.mult, eng="pool")
            ng = 4 if step > 0 else 2
            for gj in range(ng):
                for wave in range(2):
                    wch = [c for c in range(NCH) if c // 4 == wave]
                    cur = {}
                    for wi, c in enumerate(wch):
                        d, hh = chains[c]
                        t0, n, nch, bi = info[c]
                        gi = gj if d == 0 else ng - 1 - gj
                        c4 = slice(gi * 128, gi * 128 + 128)
                        v = vt_[c][bi]
                        pa, pk, po = pAA[0], pKt[0], pO[wi]
                        atd = AtD[cnt["a"] % 3]
                        ato = AtO[cnt["a"] % 3]
                        cnt["a"] += 1
                        P.mm(pa[:, 0:128], kA_[c][bi][:, c4], qA_[c][bi][:, c4])
                        P.mm(pa[:, 128:256], kO_[c][bi][:, c4], qO_[c][bi][:, c4])
                        P.tt(atd.v, pa[:, 0:128], hm4[:, d * 128:(d + 1) * 128], ALU.mult)
                        P.tt(ato.v, pa[:, 128:256], hm4[:, 256 + d * 128:256 + (d + 1) * 128], ALU.mult)
                        P.tr(pk.v, kS_[c][bi][:, c4], ident.v)
                        kts = []
                        for cc in range(4):
                            kk = ktok[cnt["k"] % 16]
                            cnt["k"] += 1
                            P.act(kk.v, pk.v, AF.Identity, scale=cm[:, cc:cc + 1])
                            kts.append(kk)
                        P.mm(po.v, v[:, gi, :], atd.v, start=True, stop=False)
                        P.mm(po.v, v[:, gi, :], ato.v, start=False, stop=False)
                        cur[c] = (gi, kts, po)
                    for cj in range(4):
                        for wi, c in enumerate(wch):
                            d, hh = chains[c]
                            t0, n, nch, bi = info[c]
                            gi, kts, po = cur[c]
                            cc = cj if d == 0 else 3 - cj
                            ci_ = gi * 4 + cc
                            cols = slice(ci_ * 32, ci_ * 32 + 32)
                            v = vt_[c][bi]
                            elc = el[c][bi]
                            sm = Smid[cnt["s"] % 6]
                            cnt["s"] += 1
                            pd = pD[cnt["d"] % 2]
                            cnt["d"] += 1
                            P.copy(sm.v, S[c].v, eng="act")
                            P.mm(po[:, cc * 32:(cc + 1) * 32], sm.v, qI_[c][bi][:, cols], start=False, stop=(cj == 3))
                            P.mm(pd.v, kts[cc].v, v[:, gi, :])
                            P.stt(S[c].v, S[c].v, elc[:, ci_:ci_ + 1], pd.v, ALU.mult, ALU.add)
                    for wi, c in enumerate(wch):
                        t0, n, nch, bi = info[c]
                        gi, kts, po = cur[c]
                        P.copy(osb[c][bi][:, gi * 128:(gi + 1) * 128], po.v, eng="act")
            for c, (d, hh) in enumerate(chains):
                t0, n, nch, bi = info[c]
                if t0 < TC and not with_ctx:
                    continue
                P.store(OFB[d, hh * 128:(hh + 1) * 128, t0:t0 + n], osb[c][bi][:, 0:n])
        P.end()

    def stage_hgrn_out(l, with_ctx):
        e = l // 2
        P.begin("hgout%d" % l)
        onesm = P.sb("onesm", [128, 128], F32)
        P.memset(onesm.v, 1.0 / 128)
        eps_t = P.sb("eps", [128, 1], F32)
        P.memset(eps_t.v, EPS)
        wn = P.sb("wn", [128, 2], F32)
        P.load(wn.v, wn_in)
        of = P.sbn("of", 2, [128, 512], F32)
        ob = P.sbn("ob", 2, [128, 512], F32)
        sg = P.sbn("sg", 2, [128, 512], BF16)
        sq = P.sbn("sq", 2, [128, 512], F32)
        rr = P.sbn("rr", 2, [128, 512], F32)
        yo = P.sbn("yo", 2, [128, 512], BF16)
        pM = P.psn("pM", 2, [128, 512])
        i = 0
        for hh in range(4):
            for (t0, n) in token_tiles():
                if t0 < TC and not with_ctx:
                    continue
                a, b, g, s, r, y, p = of[i % 2], ob[i % 2], sg[i % 2], sq[i % 2], rr[i % 2], yo[i % 2], pM[i % 2]
                i += 1
                P.load(a[:, 0:n], OFB[0, hh * 128:(hh + 1) * 128, t0:t0 + n])
                P.load(b[:, 0:n], OFB[1, hh * 128:(hh + 1) * 128, t0:t0 + n])
                P.load(g[:, 0:n], SG[hh * 128:(hh + 1) * 128, t0:t0 + n])
                P.tt(a[:, 0:n], a[:, 0:n], b[:, 0:n], ALU.add)
                P.act(s[:, 0:n], a[:, 0:n], AF.Square)
                P.mm(p[:, 0:n], onesm.v, s[:, 0:n])
                P.act(r[:, 0:n], p[:, 0:n], AF.Ln, bias=eps_t.v, scale=1.0)
                P.act(r[:, 0:n], r[:, 0:n], AF.Exp, scale=-0.5)
                P.tt(r[:, 0:n], a[:, 0:n], r[:, 0:n], ALU.mult)
                P.stt(y[:, 0:n], r[:, 0:n], wn[:, e:e + 1], g[:, 0:n], ALU.mult, ALU.mult)
                P.store(AOt[512 + hh * 128:512 + (hh + 1) * 128, t0:t0 + n], y[:, 0:n])
        P.end()

    def stage_natten(l, with_ctx):
        o_ = l // 2
        P.begin("nat%d" % l)
        KT = P.sbn("KT", 2, [128, TT], BF16)
        QT = P.sbn("QT", 2, [128, TT], BF16)
        for t in KT + QT:
            P.memset(t[64:128], 0.0, eng="pool")
        VA = P.sbn("VA", 2, [128, 34, 128], BF16)
        VB = P.sbn("VB", 2, [128, 32, 128], BF16)
        T2 = P.sbn("T2", 2, [128, 14, 64], F32)
        NS = 4
        pS = P.psn("pS", NS, [128, 512])
        pO = P.psn("pO", 2, [128, 512])
        sl = P.sbn("sl", NS, [128, 4, 64], F32)
        pt = P.sbn("pt", NS, [128, 6, 64], BF16)
        dsh = P.sbn("dsh", 2, [64, 512], F32)
        ot = P.sbn("ot", 2, [64, 512], BF16)
        items = []
        oi = 0
        for h in range(int(os.environ.get("KHEADS", "16"))):
            if with_ctx:
                for qh in range(2):
                    items.append((h, "c", qh, oi))
                oi += 1
            for r in range(64):
                items.append((h, "r", r, oi))
                if r % 8 == 7:
                    oi += 1
        LA = 2
        loaded = set()

        def bufs_of(h):
            return KT[h % 2], QT[h % 2], VA[h % 2], VB[h % 2], T2[h % 2]

        def load_head(hh):
            if hh < 16 and hh not in loaded:
                loaded.add(hh)
                kt_, qt_, va_, vb_, t2_ = bufs_of(hh)
                P.load(kt_[0:64, :], Kt[hh])
                P.load(qt_[0:64, :], Qt[hh])
                P.load(va_.v, Va[hh])
                P.load(vb_[:, 0:31, :], Vb[hh][:, 0:31, :])
                P.load(t2_.v, t2_in[o_][hh])

        for i in range(len(items) + LA):
            if i < len(items):
                h, kind, idx, o_i = items[i]
                kt, qt, va, vb, t2 = bufs_of(h)
                load_head(h)
                ps = pS[i % NS]
                if kind == "c":
                    q0 = idx * 128
                    for kc in range(2):
                        P.mm(ps[:, kc * 128:(kc + 1) * 128], kt[:, kc * 128:(kc + 1) * 128], qt[:, q0:q0 + 128])
                else:
                    r = idx
                    r0 = min(max(r - 4, 0), 56)
                    qv = qt[:, TC + r * 64:TC + (r + 1) * 64]
                    for jj in range(4):
                        k0 = TC + (r0 + 2 * jj) * 64
                        P.mm(ps[:, jj * 64:(jj + 1) * 64], kt[:, k0:k0 + 128], qv)
                    for jj in range(2):
                        P.mm(ps[:, (4 + jj) * 64:(5 + jj) * 64], kt[:, jj * 128:(jj + 1) * 128], qv)
            j = i - LA
            if j >= 0:
                h, kind, idx, o_i = items[j]
                load_head(h + 1)
                kt, qt, va, vb, t2 = bufs_of(h)
                ps, p_, s_ = pS[j % NS], pt[j % NS], sl[j % NS]
                po, d, o = pO[o_i % 2], dsh[o_i % 2], ot[o_i % 2]
                ptv = p_.v.re("p a q -> p (a q)")
                if kind == "c":
                    q0 = idx * 128
                    P.act(ptv[:, 0:256], ps[:, 0:256], AF.Exp)
                    for kc in range(2):
                        P.mm(po[:, q0:q0 + 128], va[:, kc, :], ptv[:, kc * 128:(kc + 1) * 128],
                             start=(kc == 0), stop=(kc == 1))
                    if idx == 1:
                        P.copy(d[:, 0:256], po[64:128, 0:256], eng="dve")
                        P.recip(d[:, 0:256], d[:, 0:256])
                        P.tt(o[:, 0:256], po[0:64, 0:256], d[:, 0:256], ALU.mult)
                        P.store(AOt[h * 64:(h + 1) * 64, 0:TC], o[:, 0:256])
                else:
                    r = idx
                    r0 = min(max(r - 4, 0), 56)
                    dy0 = r0 - r + 7
                    rr_ = r % 8
                    t2v = t2.v.re("p (a b) q -> p a b q", b=2)
                    psv = ps[:, 0:384].re("p (a q) -> p a q", q=64)
                    P.tt(s_.v, psv[:, 0:4, :], t2v[:, dy0 // 2:dy0 // 2 + 4, dy0 % 2, :], ALU.add)
                    P.act(p_[:, 0:4, :], s_.v, AF.Exp)
                    P.act(p_[:, 4:6, :], psv[:, 4:6, :], AF.Exp)
                    ocol = po[:, rr_ * 64:(rr_ + 1) * 64]
                    for jj in range(6):
                        if jj < 4:
                            row = r0 + 2 * jj
                            vv = va[:, 2 + row // 2, :] if row % 2 == 0 else vb[:, (row - 1) // 2, :]
                        else:
                            vv = va[:, jj - 4, :]
                        P.mm(ocol, vv, p_[:, jj, :], start=(jj == 0), stop=(jj == 5))
                    if rr_ == 7:
                        r8 = r // 8
                        P.copy(d.v, po[64:128, :], eng="dve")
                        P.recip(d.v, d.v)
                        P.tt(o.v, po[0:64, :], d.v, ALU.mult)
                        P.store(AOt[h * 64:(h + 1) * 64, TC + r8 * 512:TC + (r8 + 1) * 512], o.v)
        P.end()

    def stage_outproj(l, even, with_ctx):
        e = l // 2
        P.begin("oproj%d" % l)
        Wo = P.sb("Wo", [128, 8, DM], BF16)
        wsrc = ew_out[e] if even else ow_out[e]
        wstg = P.sbn("wstg", 2, [128, DM], F32)
        wcnt = [0]
        for k in range(8):
            load_cast(P, Wo[:, k, :], wsrc[k * 128:(k + 1) * 128, :], wstg, wcnt)
        ident = P.sb("ident", [128, 128], BF16)
        P.load(ident.v, ident_in)
        eps_t = P.sb("eps", [128, 1], F32)
        P.memset(eps_t.v, EPS)
        MV = {}
        for s in range(2):
            if s == 1 and not with_ctx:
                continue
            for wh in (2, 3, 4):
                t = P.sb("MV%d%d" % (s, wh), [128, DM], F32)
                load_bc(P, t.v, l, s, wh)
                MV[(s, wh)] = t
        ao = P.sbn("ao", 2, [128, 8, 512], BF16)
        xr = P.sbn("xr", 2, [128, DM], F32)
        xn = P.sbn("xn", 2, [128, DM], F32)
        scr = P.sb("scr", [128, DM], BF16)
        tmp = P.sb("tmp", [128, DM], F32)
        hb = P.sbn("hb", 2, [128, DM], BF16)
        ss = P.sbn("ss", 4, [128, 1], F32)
        rs = P.sbn("rs", 4, [128, 1], F32)
        hT = P.sbn("hT", 2, [128, 8, 512], BF16)
        pY = P.psn("pY", 2, [128, DM])
        pT = P.psn("pT", 2, [128, 8, 128], BF16)
        sub_i = 0
        pend = [None]
        for ti, (t0, TS) in enumerate(token_tiles()):
            is_ctx = t0 < TC
            if is_ctx and not with_ctx:
                continue
            s = 1 if is_ctx else 0
            a = ao[ti % 2]
            hTt = hT[ti % 2]
            P.load(a[:, :, 0:TS], AOt[:, t0:t0 + TS].rearrange("(k p) t -> p k t", p=128))
            for sb_ in range(TS // 128):
                r0 = t0 + sb_ * 128
                x = xr[sub_i % 2]
                xo = xn[sub_i % 2]
                h = hb[sub_i % 2]
                py = pY[sub_i % 2]
                P.load(x.v, R[r0:r0 + 128, :])
                for half in range(2):
                    for k in range(8):
                        P.mm(py[:, half * 512:(half + 1) * 512], a[:, k, sb_ * 128:(sb_ + 1) * 128],
                             Wo[:, k, half * 512:(half + 1) * 512], start=(k == 0), stop=(k == 7))
                if pend[0] is not None:
                    pend[0]()
                    pend[0] = None
                s1, r1 = ss[(2 * sub_i) % 4], rs[(2 * sub_i) % 4]
                P.act(scr.v, py.v, AF.Square, accum=s1.v)
                P.act(r1.v, s1.v, AF.Sqrt, bias=eps_t.v, scale=1.0 / DM)
                P.recip(r1.v, r1.v)
                P.stt(tmp.v, py.v, r1.v, MV[(s, 2)].v, ALU.mult, ALU.mult)
                P.tt(xo.v, tmp.v, x.v, ALU.add)
                P.store(R[r0:r0 + 128, :], xo.v)
                s2, r2 = ss[(2 * sub_i + 1) % 4], rs[(2 * sub_i + 1) % 4]
                norm_rows(P, xo.v, scr, s2, r2, MV[(s, 3)], MV[(s, 4)], h, tmp, eps_t)
                def fin(pt=pT[sub_i % 2], h=h, hTt=hTt, sb_=sb_, t0=t0, TS=TS, lastsub=(sb_ == TS // 128 - 1)):
                    for k in range(8):
                        P.tr(pt[:, k, :], h[:, k * 128:(k + 1) * 128], ident.v)
                    P.copy(hTt[:, :, sb_ * 128:(sb_ + 1) * 128], pt.v, eng="act")
                    if lastsub:
                        P.store(Ht[:, t0:t0 + TS].rearrange("(k p) t -> p k t", p=128), hTt[:, :, 0:TS])
                pend[0] = fin
                sub_i += 1
        if pend[0] is not None:
            pend[0]()
        P.end()

    def stage_ffn_up(l, with_ctx):
        P.begin("ffnup%d" % l)
        H = P.sb("H", [128, 8, TT], BF16)
        tiles = [t for t in token_tiles() if with_ctx or t[0] >= TC]
        for (t0, TS) in tiles:
            P.load(H[:, :, t0:t0 + TS], Ht[:, t0:t0 + TS].rearrange("(k p) t -> p k t", p=128))
        cw = P.sb("cw", [128, 44, 4], F32)
        P.load(cw.v, convw[l])
        NU = TT + 3
        U = [[P.sb("U%d_%d" % (b, part), [128, NU], F32) for part in range(2)] for b in range(2)]
        for ub in U:
            for u in ub:
                P.memset(u[:, 0:1], 0.0)
                P.memset(u[:, TC + 1:TC + 2], 0.0)
                P.memset(u[:, NU - 1:NU], 0.0)
        pieces = ([(0, TC)] if with_ctx else []) + [(TC + 1024 * i, 1024) for i in range(4)]
        acc = [[P.sb("acc%d_%d" % (part, pi), [128, qn_], F32) for pi, (q0, qn_) in enumerate(pieces)]
               for part in range(2)]
        mo = P.sbn("mo", 2, [128, TT], BF16)
        wa = P.sbn("wa", 4, [128, 8, 128], BF16)
        wstg = P.sbn("wstg", 2, [128, 1024], F32)
        wcnt = [0]
        pU = P.psn("pU", 6, [128, 512])

        def ucol(t):
            return t + 1 if t < TC else t + 2

        st = {"pi": 0, "wi": 0}

        def emit_mm(j):
            for part in range(2):
                c0 = part * DFF + j * 128
                w = wa[st["wi"] % 4]
                st["wi"] += 1
                load_cast(P, w.v, w_up[l][:, c0:c0 + 128].rearrange("(k p) n -> p k n", p=128), wstg, wcnt)
                u = U[j % 2][part]
                for (t0, TS) in tiles:
                    p = pU[st["pi"] % 6]
                    st["pi"] += 1
                    for k in range(8):
                        P.mm(p[:, 0:TS], w[:, k, :], H[:, k, t0:t0 + TS], start=(k == 0), stop=(k == 7))
                    P.copy(u[:, ucol(t0):ucol(t0) + TS], p[:, 0:TS], eng="act")

        def emit_conv(j):
            for part in range(2):
                cidx = part * 22 + j
                u = U[j % 2][part]
                for pi, (q0, qn_) in enumerate(pieces):
                    ac = acc[part][pi]
                    uc = ucol(q0)
                    P.ts(ac.v, u[:, uc:uc + qn_], cw[:, cidx, 1:2], cw[:, cidx, 3:4], ALU.mult, ALU.add, eng="pool")
                    P.stt(ac.v, u[:, uc - 1:uc - 1 + qn_], cw[:, cidx, 0:1], ac.v, ALU.mult, ALU.add)
                    P.stt(ac.v, u[:, uc + 1:uc + 1 + qn_], cw[:, cidx, 2:3], ac.v, ALU.mult, ALU.add)
            m = mo[j % 2]
            for pi, (q0, qn_) in enumerate(pieces):
                P.act(acc[1][pi].v, acc[1][pi].v, AF.Silu)
                P.tt(m[:, q0:q0 + qn_], acc[1][pi].v, acc[0][pi].v, ALU.mult)
            lo = 0 if with_ctx else TC
            P.store(Mt[j * 128:(j + 1) * 128, lo:TT], m[:, lo:TT])

        for j in range(23):
            if j < 22:
                emit_mm(j)
            if j >= 1:
                emit_conv(j - 1)
        P.end()

    def stage_ffn_down(l, with_ctx, last):
        P.begin("ffndn%d" % l)
        Wd = P.sb("Wd", [128, 22, DM], BF16)
        wstg = P.sbn("wstg", 2, [128, DM], F32)
        wcnt = [0]
        for k in range(22):
            load_cast(P, Wd[:, k, :], w_down[l][k * 128:(k + 1) * 128, :], wstg, wcnt)
        eps_t = P.sb("eps", [128, 1], F32)
        P.memset(eps_t.v, EPS)
        G2 = {}
        for s in range(2):
            if s == 1 and not with_ctx:
                continue
            t = P.sb("G2%d" % s, [128, DM], F32)
            load_bc(P, t.v, l, s, 5)
            G2[s] = t
        M = P.sbn("M", 2, [128, 22, 512], BF16)
        xr = P.sbn("xr", 2, [128, DM], F32)
        xn = P.sbn("xn", 2, [128, DM], F32)
        scr = P.sb("scr", [128, DM], BF16)
        tmp = P.sb("tmp", [128, DM], F32)
        ss = P.sbn("ss", 2, [128, 1], F32)
        rs = P.sbn("rs", 2, [128, 1], F32)
        pY = P.psn("pY", 2, [128, DM])
        sub_i = 0
        for ti, (t0, TS) in enumerate(token_tiles()):
            is_ctx = t0 < TC
            if is_ctx and not with_ctx:
                continue
            s = 1 if is_ctx else 0
            m = M[ti % 2]
            for (k, kn) in ((0, 6), (6, 6), (12, 5), (17, 5)):
                P.load(m[:, k:k + kn, 0:TS], Mt[k * 128:(k + kn) * 128, t0:t0 + TS].rearrange("(k p) t -> p k t", p=128))
            for sb_ in range(TS // 128):
                r0 = t0 + sb_ * 128
                x = xr[sub_i % 2]
                xo = xn[sub_i % 2]
                py = pY[sub_i % 2]
                P.load(x.v, R[r0:r0 + 128, :])
                for half in range(2):
                    for k in range(22):
                        P.mm(py[:, half * 512:(half + 1) * 512], m[:, k, sb_ * 128:(sb_ + 1) * 128],
                             Wd[:, k, half * 512:(half + 1) * 512], start=(k == 0), stop=(k == 21))
                s1, r1 = ss[sub_i % 2], rs[sub_i % 2]
                P.act(scr.v, py.v, AF.Square, accum=s1.v)
                P.act(r1.v, s1.v, AF.Sqrt, bias=eps_t.v, scale=1.0 / DM)
                P.recip(r1.v, r1.v)
                P.stt(tmp.v, py.v, r1.v, G2[s].v, ALU.mult, ALU.mult)
                P.tt(xo.v, tmp.v, x.v, ALU.add)
                if last:
                    P.store(y_out[r0 - TC:r0 - TC + 128, :], xo.v)
                else:
                    P.store(R[r0:r0 + 128, :], xo.v)
                sub_i += 1
        P.end()

    def assemble():
        stage_prologue()
        if done():
            return
        for l in layers:
            even = (l % 2 == 0)
            with_ctx = l < DEPTH - 1
            last = (l == DEPTH - 1)
            stage_proj(l, even)
            if done():
                return
            if even:
                stage_gqa(l, with_ctx)
                if done():
                    return
                stage_hgrn(l, with_ctx)
                if done():
                    return
                stage_hgrn_out(l, with_ctx)
                if done():
                    return
            else:
                stage_natten(l, with_ctx)
                if done():
                    return
            stage_outproj(l, even, with_ctx)
            if done():
                return
            stage_ffn_up(l, with_ctx)
            if done():
                return
            stage_ffn_down(l, with_ctx, last)
            if done():
                return

    assemble()
    P.finish()
    return nc


def _consts():
    ident = np.eye(128, dtype=np.float32).astype(ml_dtypes.bfloat16)
    t = np.arange(TL)
    row = (t // GRID_W).astype(np.float32)
    col = (t % GRID_W).astype(np.float32)
    inv = (np.float32(10000.0) ** (-np.arange(16, dtype=np.float32) / np.float32(16))).astype(np.float32)
    ang = np.concatenate([row[:, None] * inv, col[:, None] * inv], axis=-1).astype(np.float32)
    rope = np.concatenate([np.cos(ang), np.sin(ang)], axis=-1).astype(np.float32)
    s = np.arange(32)
    tri_f = (s[:, None] <= s[None, :]).astype(np.float32)
    tri_b = (s[:, None] >= s[None, :]).astype(np.float32)
    hmask = np.concatenate([tri_f, tri_b], axis=1)
    segm = np.ones((128, 512), np.float32)
    segm[:, ::32] = 0.0
    p = np.arange(128)
    same = (p[:, None] // 32) == (p[None, :] // 32)
    same16 = (p[:, None] // 16) == (p[None, :] // 16)
    dF = (same16 & (p[:, None] <= p[None, :])).astype(np.float32)
    dB = (same16 & (p[:, None] >= p[None, :])).astype(np.float32)
    sh0 = (p[:, None] % 32) < 16
    th1 = (p[None, :] % 32) >= 16
    oF = (same & sh0 & th1).astype(np.float32)
    oB = (same & (~sh0) & (~th1)).astype(np.float32)
    hmask4 = np.concatenate([dF, dB, oF, oB], axis=1)
    cmask = ((p[:, None] // 32) == np.arange(4)[None, :]).astype(np.float32)
    return ident, rope, hmask, segm, hmask4, cmask


def _t2_table(rpb):
    qc = np.arange(64)
    c0 = np.clip(qc - 8, 0, 48)
    kc = np.arange(64)
    inwin = (kc[:, None] >= c0[None, :]) & (kc[:, None] < c0[None, :] + 16)
    dx = np.clip(kc[:, None] - qc[None, :] + 15, 0, 30)
    m = np.arange(14)
    ee = np.arange(2)
    dy = m[None, :] + ee[:, None]
    g = rpb[:, :, dy[:, None, :, None], dx[None, :, None, :]]
    out = np.where(inwin[None, None, None, :, None, :], g, np.float32(NEG)).astype(np.float32)
    return np.ascontiguousarray(out.reshape(2, 16, 128, 14, 64))


def make_in_maps(inp, cores):
    ident, rope, hmask, segm, hmask4, cmask = _consts()
    f = lambda a: np.ascontiguousarray(np.asarray(a, dtype=np.float32))
    nrm = np.ascontiguousarray(np.stack([f(inp["norm_pre_mix"]), f(inp["norm_post_mix"]),
                                         f(inp["norm_pre_ffn"]), f(inp["norm_post_ffn"])], axis=1))
    qkn = np.ascontiguousarray(np.stack([f(inp["even_q_norm"]), f(inp["even_k_norm"])], axis=1))
    lg = f(inp["hgrn_lb_logits"]).reshape(2, 2, 4, 128)
    lbl = np.ascontiguousarray(lg.transpose(3, 0, 1, 2).reshape(128, 16))
    wn = np.ascontiguousarray(f(inp["hgrn_out_norm"]).T)
    t2 = _t2_table(f(inp["odd_rpb"]))
    cw = f(inp["ffn_conv_w"])
    cb = f(inp["ffn_conv_b"])
    cwb = np.concatenate([cw, cb[:, None, :]], axis=1)
    convw = np.ascontiguousarray(cwb.reshape(DEPTH, 4, 44, 128).transpose(0, 3, 2, 1))
    shared = {
        "b_mod": f(inp["b_mod"]), "nrm": nrm, "qkn": qkn, "lbl": lbl, "wn": wn, "convw": convw,
        "ident": ident, "rope": rope, "hmask": hmask, "segm": segm, "hmask4": hmask4, "cmask": cmask,
    }
    for l in range(DEPTH):
        shared["w_mod%d" % l] = f(inp["w_mod"][l])
        shared["ffn_w_up%d" % l] = f(inp["ffn_w_up"][l])
        shared["ffn_w_down%d" % l] = f(inp["ffn_w_down"][l])
    for e in range(2):
        shared["even_w_in%d" % e] = f(inp["even_w_in"][e])
        shared["even_w_out%d" % e] = f(inp["even_w_out"][e])
        shared["odd_w_qkv%d" % e] = f(inp["odd_w_qkv"][e])
        shared["odd_w_out%d" % e] = f(inp["odd_w_out"][e])
        shared["t2_%d" % e] = t2[e]
    x = f(inp["x"])
    ctx = f(inp["ctx"])
    c = f(inp["c"])
    cctx = f(inp["c_ctx"])
    maps = []
    for b in cores:
        cc = np.stack([c[b].reshape(8, 128).T, cctx.reshape(8, 128).T], axis=-1)
        m = dict(shared)
        m["x"] = x[b]
        m["ctx"] = ctx[b]
        m["cc"] = np.ascontiguousarray(cc)
        maps.append(m)
    return maps


def kernel(**inputs):
    nc = build()
    maps = make_in_maps(inputs, list(range(8)))
    res = run_bass_kernel_spmd(nc, maps, core_ids=list(range(8)))
    return np.stack([np.asarray(r["y"], dtype=np.float32) for r in res.results], axis=0)
```
